# Optimizing a Trainium2 kernel written in Bass

```python
import math
import jax, jax.numpy as jnp
from jax import lax
import numpy as np

D_MODEL = 1024
BATCH = 16
SEQ = 2048
DEPTH = 1

MEM_LEN = 256
MAX_POS_OFFSET = 1024
ML_HEADS = 4
ML_DQK = 128
ML_DV = 256
ML_CHUNK = 64
ML_CONV = 4
DA_HEADS = 8
DA_DHEAD = 64
DA_DV = 2 * DA_DHEAD
Q_BLOCK = 128
ROPE_THETA = 10000.0
XA_HEADS = 4
XA_DHEAD = D_MODEL // XA_HEADS
FFN_HIDDEN = -(-8 * D_MODEL // (3 * 256)) * 256
ALPHA = (2.0 * DEPTH) ** 0.25
BETA = (8.0 * DEPTH) ** -0.25
LN_EPS = 1e-5

ML_QK_W = 2 * ML_HEADS * ML_DQK
ML_V_W = ML_HEADS * ML_DV
ML_O_W = ML_HEADS * ML_DV
ML_IF_W = 2 * ML_HEADS
DA_Q_W = DA_HEADS * 2 * DA_DHEAD
DA_K_W = DA_HEADS * 2 * DA_DHEAD
DA_V_W = DA_HEADS * DA_DV
GATE_W = 2 * D_MODEL
IN_SIZES = (ML_QK_W, ML_V_W, ML_O_W, ML_IF_W, DA_Q_W, DA_K_W, DA_V_W, GATE_W)
IN_DIM = ML_QK_W + ML_V_W + ML_O_W + ML_IF_W + DA_Q_W + DA_K_W + DA_V_W + GATE_W
F_GATE_OFFSET = ML_QK_W + ML_V_W + ML_O_W + ML_HEADS

kernel_name = 'hybrid_mlstm_diffattn_gated_deepnorm'


def layer_norm(x, g, b):
    xf = x.astype(jnp.float32)
    mu = xf.mean(-1, keepdims=True)
    var = jnp.square(xf - mu).mean(-1, keepdims=True)
    return ((xf - mu) * lax.rsqrt(var + LN_EPS) * g + b).astype(x.dtype)


def head_layer_norm(h, g):
    mu = h.mean(-1, keepdims=True)
    var = jnp.square(h - mu).mean(-1, keepdims=True)
    return (h - mu) * lax.rsqrt(var + LN_EPS) * g


def head_rms_norm(h, g):
    hf = h.astype(jnp.float32)
    return hf * lax.rsqrt(jnp.square(hf).mean(-1, keepdims=True) + LN_EPS) * g


def causal_dwconv(x, w, b):
    K, C = w.shape
    y = lax.conv_general_dilated(x, w[:, None, :], window_strides=(1,), padding=((K - 1, 0),),
                                 dimension_numbers=('NWC', 'WIO', 'NWC'), feature_group_count=C)
    return y + b


def rope(x, pos):
    half = x.shape[-1] // 2
    inv = ROPE_THETA ** (-jnp.arange(half, dtype=jnp.float32) / half)
    ang = pos.astype(jnp.float32)[..., None] * inv
    cos = jnp.cos(ang)[:, :, None, :]
    sin = jnp.sin(ang)[:, :, None, :]
    x1, x2 = x[..., :half], x[..., half:]
    return jnp.concatenate([x1 * cos - x2 * sin, x2 * cos + x1 * sin], axis=-1).astype(x.dtype)


def mlstm_chunkwise(q, k, v, i_pre, f_pre):
    B, S, H, dk = q.shape
    dv = v.shape[-1]
    L = ML_CHUNK
    nc = S // L
    f32 = jnp.float32

    def chunks(t):
        return t.astype(f32).reshape(B, nc, L, H, t.shape[-1]).transpose(1, 0, 3, 2, 4)

    def gchunks(t):
        return t.astype(f32).reshape(B, nc, L, H).transpose(1, 0, 3, 2)

    qc = chunks(q) * (dk ** -0.5)
    kc, vc = chunks(k), chunks(v)
    ic = gchunks(i_pre)
    lfc = jax.nn.log_sigmoid(gchunks(f_pre))
    causal = jnp.tril(jnp.ones((L, L), dtype=bool))

    def step(carry, inp):
        C, n, m = carry
        qb, kb, vb, ib, lfb = inp
        bcum = jnp.cumsum(lfb, axis=-1)
        dmat = bcum[..., :, None] - bcum[..., None, :] + ib[..., None, :]
        dmat = jnp.where(causal, dmat, -jnp.inf)
        inter = bcum + m[..., None]
        m_row = jnp.maximum(inter, dmat.max(-1))
        w_inter = jnp.exp(inter - m_row)
        w_intra = jnp.exp(dmat - m_row[..., None])
        qk = jnp.einsum('bhtd,bhsd->bhts', qb, kb) * w_intra
        num = (w_inter[..., None] * jnp.einsum('bhtd,bhdv->bhtv', qb, C)
               + jnp.einsum('bhts,bhsv->bhtv', qk, vb))
        den = w_inter * jnp.einsum('bhtd,bhd->bht', qb, n) + qk.sum(-1)
        h = num / jnp.maximum(jnp.abs(den), jnp.exp(-m_row))[..., None]
        b_last = bcum[..., -1]
        g = b_last[..., None] - bcum + ib
        m_new = jnp.maximum(b_last + m, g.max(-1))
        decay = jnp.exp(b_last + m - m_new)
        ws = jnp.exp(g - m_new[..., None])
        C_new = decay[..., None, None] * C + jnp.einsum('bhs,bhsd,bhsv->bhdv', ws, kb, vb)
        n_new = decay[..., None] * n + jnp.einsum('bhs,bhsd->bhd', ws, kb)
        return (C_new, n_new, m_new), h

    init = (jnp.zeros((B, H, dk, dv), f32), jnp.zeros((B, H, dk), f32), jnp.zeros((B, H), f32))
    _, hs = lax.scan(step, init, (qc, kc, vc, ic, lfc))
    return hs.transpose(1, 0, 3, 2, 4).reshape(B, S, H, dv)


def diff_attention(q, k, v, lam):
    B, S, H, _, d = q.shape
    dv = v.shape[-1]
    nb = S // Q_BLOCK
    qb = q.reshape(B, nb, Q_BLOCK, H, 2, d).transpose(1, 0, 3, 4, 2, 5)
    kt = k.transpose(0, 2, 3, 1, 4)
    vt = v.transpose(0, 2, 1, 3)
    kpos = jnp.arange(S)
    scale = d ** -0.5

    def block(args):
        qi, i = args
        s = jnp.einsum('bhcqd,bhckd->bhcqk', qi, kt).astype(jnp.float32) * scale
        qpos = i * Q_BLOCK + jnp.arange(Q_BLOCK)
        s = jnp.where(kpos[None, :] <= qpos[:, None], s, -jnp.inf)
        p = jax.nn.softmax(s, axis=-1)
        a = p[:, :, 0] - lam * p[:, :, 1]
        return jnp.einsum('bhqk,bhkv->bhqv', a.astype(vt.dtype), vt)

    o = lax.map(block, (qb, jnp.arange(nb)))
    return o.transpose(1, 0, 3, 2, 4).reshape(B, S, H, dv)


def setup_inputs(seed: int = 0) -> dict:
    key = jax.random.key(seed)
    ks = jax.random.split(key, 32)

    def nrm(i, shape, scale):
        return jax.random.normal(ks[i], shape, jnp.float32) * scale

    D, F = D_MODEL, FFN_HIDDEN
    x = nrm(0, (BATCH, SEQ, D), 1.0)
    mem = nrm(1, (BATCH, MEM_LEN, D), 1.0)
    offs = jax.random.randint(ks[2], (BATCH, 1), 0, MAX_POS_OFFSET, dtype=jnp.int32)
    positions = offs + jnp.arange(SEQ, dtype=jnp.int32)[None, :]
    w_in = nrm(3, (DEPTH, D, IN_DIM), D ** -0.5)
    b_in = nrm(4, (DEPTH, IN_DIM), 0.02)
    b_in = b_in.at[:, F_GATE_OFFSET:F_GATE_OFFSET + ML_HEADS].add(jnp.linspace(3.0, 6.0, ML_HEADS))
    conv_w = nrm(5, (DEPTH, ML_CONV, ML_QK_W), ML_CONV ** -0.5)
    conv_b = nrm(6, (DEPTH, ML_QK_W), 0.02)
    ml_norm_w = 1.0 + nrm(7, (DEPTH, ML_HEADS * ML_DV), 0.02)
    lam_q1 = nrm(8, (DEPTH, DA_DHEAD), 0.1)
    lam_k1 = nrm(9, (DEPTH, DA_DHEAD), 0.1)
    lam_q2 = nrm(10, (DEPTH, DA_DHEAD), 0.1)
    lam_k2 = nrm(11, (DEPTH, DA_DHEAD), 0.1)
    da_norm_w = 1.0 + nrm(12, (DEPTH, DA_HEADS * DA_DV), 0.02)
    w_proj_a = nrm(13, (DEPTH, ML_V_W, D), ML_V_W ** -0.5)
    w_proj_b = nrm(14, (DEPTH, DA_V_W, D), DA_V_W ** -0.5)
    w_mix_out = nrm(15, (DEPTH, D, D), BETA * D ** -0.5)
    ln1_g = 1.0 + nrm(16, (DEPTH, D), 0.02)
    ln1_b = nrm(17, (DEPTH, D), 0.02)
    w_xq = nrm(18, (DEPTH, D, XA_HEADS * XA_DHEAD), D ** -0.5)
    w_xk = nrm(19, (DEPTH, D, XA_HEADS * XA_DHEAD), D ** -0.5)
    w_xv = nrm(20, (DEPTH, D, XA_HEADS * XA_DHEAD), D ** -0.5)
    w_xo = nrm(21, (DEPTH, XA_HEADS * XA_DHEAD, D), BETA * (XA_HEADS * XA_DHEAD) ** -0.5)
    ln2_g = 1.0 + nrm(22, (DEPTH, D), 0.02)
    ln2_b = nrm(23, (DEPTH, D), 0.02)
    w_ffn_gate = nrm(24, (DEPTH, D, F), D ** -0.5)
    w_ffn_up = nrm(25, (DEPTH, D, F), D ** -0.5)
    w_ffn_down = nrm(26, (DEPTH, F, D), BETA * F ** -0.5)
    ln3_g = 1.0 + nrm(27, (DEPTH, D), 0.02)
    ln3_b = nrm(28, (DEPTH, D), 0.02)
    return {'x': x, 'mem': mem, 'positions': positions, 'w_in': w_in, 'b_in': b_in,
            'conv_w': conv_w, 'conv_b': conv_b, 'ml_norm_w': ml_norm_w,
            'lam_q1': lam_q1, 'lam_k1': lam_k1, 'lam_q2': lam_q2, 'lam_k2': lam_k2,
            'da_norm_w': da_norm_w, 'w_proj_a': w_proj_a, 'w_proj_b': w_proj_b,
            'w_mix_out': w_mix_out, 'ln1_g': ln1_g, 'ln1_b': ln1_b,
            'w_xq': w_xq, 'w_xk': w_xk, 'w_xv': w_xv, 'w_xo': w_xo, 'ln2_g': ln2_g, 'ln2_b': ln2_b,
            'w_ffn_gate': w_ffn_gate, 'w_ffn_up': w_ffn_up, 'w_ffn_down': w_ffn_down,
            'ln3_g': ln3_g, 'ln3_b': ln3_b}


def reference(x, mem, positions, w_in, b_in, conv_w, conv_b, ml_norm_w, lam_q1, lam_k1, lam_q2,
              lam_k2, da_norm_w, w_proj_a, w_proj_b, w_mix_out, ln1_g, ln1_b, w_xq, w_xk, w_xv,
              w_xo, ln2_g, ln2_b, w_ffn_gate, w_ffn_up, w_ffn_down, ln3_g, ln3_b):
    B, S, D = x.shape
    M = mem.shape[1]
    offsets = [int(o) for o in np.cumsum(IN_SIZES)[:-1]]
    for l in range(DEPTH):
        proj = jnp.einsum('bsd,de->bse', x, w_in[l]) + b_in[l]
        ml_qk, ml_v, ml_o, ml_if, da_q, da_k, da_v, gates = jnp.split(proj, offsets, axis=-1)

        qk = jax.nn.silu(causal_dwconv(ml_qk, conv_w[l], conv_b[l]))
        mq, mk = jnp.split(qk, 2, axis=-1)
        h = mlstm_chunkwise(mq.reshape(B, S, ML_HEADS, ML_DQK), mk.reshape(B, S, ML_HEADS, ML_DQK),
                            ml_v.reshape(B, S, ML_HEADS, ML_DV),
                            ml_if[..., :ML_HEADS], ml_if[..., ML_HEADS:])
        h = head_layer_norm(h, ml_norm_w[l].reshape(ML_HEADS, ML_DV)).reshape(B, S, ML_V_W)
        y_a = (jax.nn.sigmoid(ml_o) * h).astype(x.dtype) @ w_proj_a[l]

        dq = rope(da_q.reshape(B, S, DA_HEADS * 2, DA_DHEAD), positions).reshape(B, S, DA_HEADS, 2, DA_DHEAD)
        dk = rope(da_k.reshape(B, S, DA_HEADS * 2, DA_DHEAD), positions).reshape(B, S, DA_HEADS, 2, DA_DHEAD)
        dvv = da_v.reshape(B, S, DA_HEADS, DA_DV)
        lam_init = 0.8 - 0.6 * math.exp(-0.3 * l)
        lam = (jnp.exp(jnp.sum(lam_q1[l] * lam_k1[l]).astype(jnp.float32))
               - jnp.exp(jnp.sum(lam_q2[l] * lam_k2[l]).astype(jnp.float32)) + lam_init)
        o = diff_attention(dq, dk, dvv, lam)
        o = head_rms_norm(o, da_norm_w[l].reshape(DA_HEADS, DA_DV)) * (1.0 - lam_init)
        y_b = o.reshape(B, S, DA_V_W).astype(x.dtype) @ w_proj_b[l]

        g_a, g_b = jnp.split(jax.nn.sigmoid(gates), 2, axis=-1)
        mix = (g_a * y_a + g_b * y_b) @ w_mix_out[l]
        x = layer_norm(ALPHA * x + mix, ln1_g[l], ln1_b[l])

        xq = (x @ w_xq[l]).reshape(B, S, XA_HEADS, XA_DHEAD)
        xk = (mem @ w_xk[l]).reshape(B, M, XA_HEADS, XA_DHEAD)
        xv = (mem @ w_xv[l]).reshape(B, M, XA_HEADS, XA_DHEAD)
        s = jnp.einsum('bqhd,bkhd->bhqk', xq, xk).astype(jnp.float32) * (XA_DHEAD ** -0.5)
        p = jax.nn.softmax(s, axis=-1)
        xo = jnp.einsum('bhqk,bkhd->bqhd', p.astype(xv.dtype), xv).reshape(B, S, XA_HEADS * XA_DHEAD)
        x = layer_norm(ALPHA * x + xo @ w_xo[l], ln2_g[l], ln2_b[l])

        hid = jax.nn.silu(x @ w_ffn_gate[l]) * (x @ w_ffn_up[l])
        x = layer_norm(ALPHA * x + hid @ w_ffn_down[l], ln3_g[l], ln3_b[l])
    return x
```

```python
import math
from contextlib import ExitStack
import numpy as np
import concourse.bass as bass
import concourse.mybir as mybir
from concourse.bass_utils import run_bass_kernel_spmd

F32 = mybir.dt.float32
BF16 = mybir.dt.bfloat16
I32 = mybir.dt.int32
AF = mybir.ActivationFunctionType
ALU = mybir.AluOpType
PE, ACT, DVE, POOL, SP = "tensor", "scalar", "vector", "gpsimd", "sync"
ENGS = [PE, ACT, DVE, POOL, SP]

D = 1024
KC = 8
MEM = 256
FH = 2816
NF = FH // 128
ALPHA = 2.0 ** 0.25
EPS = 1e-5
LAM_INIT = 0.8 - 0.6 * math.exp(0.0)
MAGIC = 12582912.0
LNC = math.log(128.0 ** -0.5)
TWO_PI = 2.0 * math.pi
CHW = 4096


class T:
    def __init__(self, ap, res):
        self.ap = ap
        self.res = list(res) if isinstance(res, (list, tuple)) else [res]

    def __getitem__(self, k):
        return T(self.ap[k], self.res)


class Prog:
    def __init__(self, nc, stack):
        self.nc = nc
        self.stack = stack
        self.ops = {e: [] for e in ENGS}
        self.cnt = {e: 0 for e in ENGS}
        self.esem = {e: stack.enter_context(nc.semaphore("s_" + e)) for e in ENGS}
        self.res = {}
        self.known = {e: {} for e in ENGS}
        self.dsem = {}
        self.dcnt = {}
        self.prefix_init = []

    def _r(self, name):
        r = self.res.get(name)
        if r is None:
            r = {"w": None, "r": {}}
            for pre, summ in self.prefix_init:
                if name.startswith(pre):
                    for src, ev in summ.items():
                        o = r["r"].get(src)
                        if o is None or o[2] < ev[2]:
                            r["r"][src] = ev
            self.res[name] = r
        return r

    def summary(self, prefix):
        out = {}
        for n, r in self.res.items():
            if n.startswith(prefix):
                evs = list(r["r"].values())
                if r["w"] is not None:
                    evs.append(r["w"])
                for ev in evs:
                    o = out.get(ev[0])
                    if o is None or o[2] < ev[2]:
                        out[ev[0]] = ev
        return out

    def add_readers(self, name, summ):
        r = self._r(name)
        for src, ev in summ.items():
            o = r["r"].get(src)
            if o is None or o[2] < ev[2]:
                r["r"][src] = ev

    def emit(self, eng, fn, reads=(), writes=(), dma=None):
        deps = []
        for n in reads:
            r = self._r(n)
            w = r["w"]
            if w is not None:
                deps.append((w, "raw"))
            if n.startswith("ps") and dma is None:
                for src_, ev_ in r["r"].items():
                    if src_ != eng:
                        deps.append((ev_, "psum"))
        for n in writes:
            r = self._r(n)
            if r["w"] is not None:
                deps.append((r["w"], "waw"))
            for ev in r["r"].values():
                deps.append((ev, "war"))
        if dma is None:
            self.cnt[eng] += 1
            ev = (eng, self.esem[eng], self.cnt[eng])
        else:
            if dma not in self.dsem:
                self.dsem[dma] = self.stack.enter_context(self.nc.semaphore("d_" + dma))
                self.dcnt[dma] = 0
            if self.dcnt[dma] > 0:
                deps.append((("dma:" + dma, self.dsem[dma], self.dcnt[dma]), "waw"))
            self.dcnt[dma] += 16
            ev = ("dma:" + dma, self.dsem[dma], self.dcnt[dma])
        waits = {}
        for (src, sem, val), kind in deps:
            if src == eng and dma is None:
                if eng == PE:
                    continue
            if self.known[eng].get(src, 0) >= val:
                continue
            if src not in waits or waits[src][1] < val:
                waits[src] = (sem, val)
        for src, (sem, val) in waits.items():
            self.known[eng][src] = val
        self.ops[eng].append((list(waits.values()), fn, ev))
        for n in reads:
            r = self._r(n)
            o = r["r"].get(ev[0])
            if o is None or o[2] < ev[2]:
                r["r"][ev[0]] = ev
        for n in writes:
            r = self._r(n)
            r["w"] = ev
            r["r"] = {}
        return ev

    def final_wait(self, eng, prefixes):
        waits = {}
        for pre in prefixes:
            for src, ev in self.summary(pre).items():
                if src not in waits or waits[src][1] < ev[2]:
                    waits[src] = (ev[1], ev[2])
        self.ops[eng].append((list(waits.values()), None, None))

    def replay(self):
        nc = self.nc
        with nc.Block() as block:
            for e in ENGS:
                ops = self.ops[e]

                def body(engine, ops=ops):
                    for waits, fn, ev in ops:
                        for sem, val in waits:
                            engine.wait_ge(sem, val)
                        if fn is None:
                            continue
                        ins = fn(engine)
                        ins.then_inc(ev[1], 16 if ev[0].startswith("dma:") else 1)

                getattr(block, e)(body)


class Arena:
    def __init__(self, P, ap, nbytes):
        self.P = P
        self.ap = ap
        self.nbytes = nbytes
        self.live = {}
        self.ghosts = []
        self.gen = 0
        self.peak = 0

    def take(self, base, shape, dtype, parts=128, top=False):
        esz = 2 if dtype == BF16 else 4
        n = 1
        for s in shape[1:]:
            n *= s
        nb = (n * esz + 63) // 64 * 64
        segs = sorted(self.live.values())
        if not top:
            lo = 0
            for a, b in segs:
                if lo + nb <= a:
                    break
                lo = max(lo, b)
        else:
            hi = self.nbytes
            lo = None
            for a, b in reversed(segs):
                if b + nb <= hi:
                    break
                hi = min(hi, a)
            lo = hi - nb
        assert lo >= 0 and lo + nb <= self.nbytes, \
            f"arena overflow taking {base} {nb} (live={sum(b - a for a, b in segs)}) {sorted((v, k) for k, v in self.live.items())}"
        self.gen += 1
        name = f"{base}@{self.gen}"
        self.live[name] = (lo, lo + nb)
        self.peak = max(self.peak, lo + nb)
        summ = {}
        for a, b, s in self.ghosts:
            if a < lo + nb and lo < b:
                for src, ev in s.items():
                    o = summ.get(src)
                    if o is None or o[2] < ev[2]:
                        summ[src] = ev
        if summ:
            self.P.prefix_init.append((name + ":", summ))
        v = self.ap[0:shape[0], lo // 4:(lo + nb) // 4]
        if dtype != F32:
            v = v.bitcast(dtype)
        v = v[:, 0:n]
        if len(shape) == 3:
            v = v.rearrange("p (a b) -> p a b", b=shape[2])
        elif len(shape) == 4:
            v = v.rearrange("p (a b c) -> p a b c", b=shape[2], c=shape[3])
        return Buf(self, name, v)

    def release(self, buf):
        lo, hi = self.live.pop(buf.name)
        self.ghosts.append((lo, hi, self.P.summary(buf.name + ":")))


class Buf:
    def __init__(self, arena, name, ap):
        self.arena = arena
        self.name = name
        self.ap = ap

    def t(self, sub="", key=None):
        ap = self.ap if key is None else self.ap[key]
        return T(ap, f"{self.name}:{sub}")

    def free(self):
        self.arena.release(self)


def chunk_plan():
    plan = []
    for hd in range(8):
        plan.append((f"da{hd}", KC * 384))
    for h in range(4):
        plan.append((f"mlqk{h}", KC * 256))
        plan.append((f"mlv{h}", KC * 256))
        plan.append((f"mlo{h}", KC * 256))
    for j in range(8):
        plan.append((f"p4_{j}", KC * 512))
    for c in range(2):
        plan.append((f"mix{c}", KC * 512))
    for nm in ("xk", "xv", "xq", "xo"):
        for c in range(2):
            plan.append((f"{nm}{c}", KC * 512))
    for c in range(NF // 2):
        plan.append((f"gu{c}", KC * 512))
    for j in range(8):
        plan.append((f"dn{j}", NF * 128))
    return plan


CHUNKS = chunk_plan()
CHIDX = {k: i for i, (k, n) in enumerate(CHUNKS)}
CHLEN = {k: n for k, n in CHUNKS}


def col_plan():
    names = []
    for h in range(4):
        names += [f"bq{h}", f"bk{h}", f"bo{h}_0", f"bo{h}_1"]
        for qk in "qk":
            for tap in range(4):
                names.append(f"cw{qk}{h}_{tap}")
            names.append(f"cb{qk}{h}")
    for hd in range(8):
        names += [f"dbq{hd}", f"dbk{hd}"]
    for j in range(8):
        names += [f"bga{j}", f"bgb{j}", f"mlnw{j}", f"danw{j}"]
        for l in (1, 2, 3):
            names += [f"ln{l}g{j}", f"ln{l}b{j}"]
    names += ["bi", "bf", "invf", "sgn"]
    return {n: i for i, n in enumerate(names)}


COLS = col_plan()
NCOL = len(COLS)

CB_ID, CB_ONE, CB_OD, CB_O256, CB_O128, CB_MASK, CB_PERM = [i * 128 for i in range(7)]
NCB = 7 * 128
CF_ID, CF_SEL = 0, 128
NCF = 128 + 512


def _pk(wc):
    K, n = wc.shape
    kc = K // 128
    a = np.ascontiguousarray(wc.reshape(kc, 128, n).transpose(1, 0, 2)).reshape(128, kc * n)
    out = np.zeros((128, CHW), np.float32)
    out[:, :kc * n] = a
    return out


def pack_host(inp):
    w_in = inp["w_in"][0]
    b_in = inp["b_in"][0]
    O_MLV, O_MLO, O_IF = 1024, 2048, 3072
    O_DQ, O_DK, O_DV, O_G = 3080, 4104, 5128, 6152
    chunks = {}
    for hd in range(8):
        cols = np.r_[O_DQ + hd * 128:O_DQ + hd * 128 + 128, O_DK + hd * 128:O_DK + hd * 128 + 128,
                     O_DV + hd * 128:O_DV + hd * 128 + 128]
        chunks[f"da{hd}"] = _pk(w_in[:, cols])
    for h in range(4):
        cols = np.r_[h * 128:h * 128 + 128, 512 + h * 128:512 + h * 128 + 128]
        chunks[f"mlqk{h}"] = _pk(w_in[:, cols])
        chunks[f"mlv{h}"] = _pk(w_in[:, O_MLV + h * 256:O_MLV + h * 256 + 256])
        chunks[f"mlo{h}"] = _pk(w_in[:, O_MLO + h * 256:O_MLO + h * 256 + 256])
    pa, pb = inp["w_proj_a"][0], inp["w_proj_b"][0]
    for j in range(8):
        sl = slice(j * 128, j * 128 + 128)
        chunks[f"p4_{j}"] = _pk(np.concatenate(
            [pa[:, sl], pb[:, sl], w_in[:, O_G + j * 128:O_G + j * 128 + 128],
             w_in[:, O_G + 1024 + j * 128:O_G + 1024 + j * 128 + 128]], axis=1))
    for nm, key in (("mix", "w_mix_out"), ("xk", "w_xk"), ("xv", "w_xv"), ("xq", "w_xq"), ("xo", "w_xo")):
        w = inp[key][0]
        for c in range(2):
            chunks[f"{nm}{c}"] = _pk(w[:, c * 512:c * 512 + 512])
    wg, wu, wd = inp["w_ffn_gate"][0], inp["w_ffn_up"][0], inp["w_ffn_down"][0]
    for c in range(NF // 2):
        f0, f1 = 2 * c, 2 * c + 1
        chunks[f"gu{c}"] = _pk(np.concatenate(
            [wg[:, f0 * 128:f0 * 128 + 128], wu[:, f0 * 128:f0 * 128 + 128],
             wg[:, f1 * 128:f1 * 128 + 128], wu[:, f1 * 128:f1 * 128 + 128]], axis=1))
    for j in range(8):
        chunks[f"dn{j}"] = _pk(wd[:, j * 128:j * 128 + 128])
    wpk = np.stack([chunks[k] for k, _ in CHUNKS], axis=0)

    wif = np.ascontiguousarray(w_in[:, O_IF:O_IF + 8].reshape(KC, 128, 8).transpose(1, 0, 2)).reshape(128, KC * 8)

    cols = np.zeros((128, NCOL), np.float32)
    conv_w, conv_b = inp["conv_w"][0], inp["conv_b"][0]
    for h in range(4):
        cols[:, COLS[f"bq{h}"]] = b_in[h * 128:h * 128 + 128]
        cols[:, COLS[f"bk{h}"]] = b_in[512 + h * 128:512 + h * 128 + 128]
        cols[:, COLS[f"bo{h}_0"]] = b_in[O_MLO + h * 256:O_MLO + h * 256 + 128]
        cols[:, COLS[f"bo{h}_1"]] = b_in[O_MLO + h * 256 + 128:O_MLO + h * 256 + 256]
        for qk, off in (("q", 0), ("k", 512)):
            ch = slice(off + h * 128, off + h * 128 + 128)
            for tap in range(4):
                cols[:, COLS[f"cw{qk}{h}_{tap}"]] = conv_w[tap, ch]
            cols[:, COLS[f"cb{qk}{h}"]] = conv_b[ch]
    for hd in range(8):
        cols[:, COLS[f"dbq{hd}"]] = b_in[O_DQ + hd * 128:O_DQ + hd * 128 + 128]
        cols[:, COLS[f"dbk{hd}"]] = b_in[O_DK + hd * 128:O_DK + hd * 128 + 128]
    for j in range(8):
        sl = slice(j * 128, j * 128 + 128)
        cols[:, COLS[f"bga{j}"]] = b_in[O_G + j * 128:O_G + j * 128 + 128]
        cols[:, COLS[f"bgb{j}"]] = b_in[O_G + 1024 + j * 128:O_G + 1024 + j * 128 + 128]
        cols[:, COLS[f"mlnw{j}"]] = inp["ml_norm_w"][0][sl]
        cols[:, COLS[f"danw{j}"]] = inp["da_norm_w"][0][sl]
        for l in (1, 2, 3):
            cols[:, COLS[f"ln{l}g{j}"]] = inp[f"ln{l}_g"][0][sl]
            cols[:, COLS[f"ln{l}b{j}"]] = inp[f"ln{l}_b"][0][sl]
    cols[0:4, COLS["bi"]] = b_in[O_IF:O_IF + 4]
    cols[0:4, COLS["bf"]] = b_in[O_IF + 4:O_IF + 8]
    half = 32
    inv = (10000.0 ** (-np.arange(half, dtype=np.float32) / half)).astype(np.float32)
    p = np.arange(128)
    cols[:, COLS["invf"]] = inv[p % 32]
    cols[:, COLS["sgn"]] = np.where((p % 64) < 32, -1.0, 1.0)

    brow = np.concatenate([b_in[O_MLV:O_MLV + 1024], b_in[O_DV:O_DV + 1024]])[None, :].astype(np.float32)
    lam = np.concatenate([inp["lam_q1"][0], inp["lam_k1"][0], inp["lam_q2"][0], inp["lam_k2"][0]])[None, :]
    lam = np.ascontiguousarray(lam, dtype=np.float32)

    cbf = np.zeros((128, NCB), np.float32)
    cbf[:, CB_ID:CB_ID + 128] = np.eye(128)
    cbf[:, CB_ONE:CB_ONE + 128] = 1.0
    cbf[:, CB_OD:CB_OD + 128] = 1.0 / 1024
    cbf[:, CB_O256:CB_O256 + 128] = 1.0 / 256
    cbf[:, CB_O128:CB_O128 + 128] = 1.0 / 128
    s_, t_ = np.meshgrid(np.arange(128), np.arange(128), indexing="ij")
    cbf[:, CB_MASK:CB_MASK + 128] = np.where(s_ <= t_, 0.0, -30000.0)
    partner = np.where((p % 64) < 32, p + 32, p - 32)
    perm = np.zeros((128, 128), np.float32)
    perm[partner, p] = 1.0
    cbf[:, CB_PERM:CB_PERM + 128] = perm
    cf = np.zeros((128, NCF), np.float32)
    cf[:, CF_ID:CF_ID + 128] = np.eye(128)
    for h in range(4):
        cf[h, CF_SEL + h * 128:CF_SEL + h * 128 + 128] = 1.0
    return dict(wpk=wpk, wif=wif, cols=cols, brow=brow, lam=lam, cbf=cbf, cf=cf)


def build(S=2048, NSEQ=2, dbg=None):
    NT, TB = S // 128, S // 512
    nc = bass.Bass("TRN2", target_bir_lowering=False)
    xT_d = nc.dram_tensor("xT", [NSEQ, D, S], F32, kind="ExternalInput").ap()
    memT_d = nc.dram_tensor("memT", [NSEQ, D, MEM], F32, kind="ExternalInput").ap()
    pos_d = nc.dram_tensor("pos", [NSEQ, S], I32, kind="ExternalInput").ap()
    wpk_d = nc.dram_tensor("wpk", [len(CHUNKS), 128, CHW], F32, kind="ExternalInput").ap()
    wif_d = nc.dram_tensor("wif", [128, KC * 8], F32, kind="ExternalInput").ap()
    cols_d = nc.dram_tensor("cols", [128, NCOL], F32, kind="ExternalInput").ap()
    brow_d = nc.dram_tensor("brow", [1, 2048], F32, kind="ExternalInput").ap()
    lam_d = nc.dram_tensor("lam", [1, 256], F32, kind="ExternalInput").ap()
    cbf_d = nc.dram_tensor("cbf", [128, NCB], F32, kind="ExternalInput").ap()
    cf_d = nc.dram_tensor("cf", [128, NCF], F32, kind="ExternalInput").ap()
    outT_d = nc.dram_tensor("outT", [NSEQ, D, S], F32, kind="ExternalOutput").ap()
    dbg_d = {}
    if dbg:
        for k, shp in dbg.items():
            dbg_d[k] = nc.dram_tensor("dbg_" + k, list(shp), F32, kind="ExternalOutput").ap()

    st = ExitStack()
    with st:
        P = Prog(nc, st)
        ARB = 206 * 1024
        arena_t = st.enter_context(nc.sbuf_tensor("arena", [128, ARB // 4], F32))
        AR = Arena(P, arena_t[:], ARB)
        pbank = [st.enter_context(nc.psum_tensor(f"ps{i}", [128, 512], F32)) for i in range(8)]
        PS = [T(pbank[i][:], f"ps{i}") for i in range(8)]

        def rd(*ts):
            out = []
            for t in ts:
                if isinstance(t, T):
                    out += t.res
            return out

        def mm(out, lhsT, rhs, start=True, stop=True):
            P.emit(PE, lambda e: e.matmul(out.ap, lhsT=lhsT.ap, rhs=rhs.ap, start=start, stop=stop),
                   reads=rd(lhsT, rhs), writes=out.res)

        def tr(out, in_, ident):
            P.emit(PE, lambda e: e.transpose(out.ap, in_.ap, ident.ap), reads=rd(in_, ident), writes=out.res)

        def apof(x):
            return x.ap if isinstance(x, T) else x

        def act(out, in_, func, bias=None, scale=None, eng=ACT):
            kw = {}
            if bias is not None:
                kw["bias"] = apof(bias)
            if scale is not None:
                kw["scale"] = apof(scale)
            P.emit(ACT, lambda e: e.activation(out=out.ap, in_=in_.ap, func=func, **kw),
                   reads=rd(in_, bias, scale), writes=out.res)

        def tt(eng, out, in0, in1, op):
            P.emit(eng, lambda e: e.tensor_tensor(out=out.ap, in0=in0.ap, in1=in1.ap, op=op),
                   reads=rd(in0, in1), writes=out.res)

        def ts(eng, out, in0, s1, op0, s2=None, op1=None):
            if op1 is None:
                P.emit(eng, lambda e: e.tensor_scalar(out=out.ap, in0=in0.ap, scalar1=apof(s1), scalar2=None, op0=op0),
                       reads=rd(in0, s1), writes=out.res)
            else:
                P.emit(eng, lambda e: e.tensor_scalar(out=out.ap, in0=in0.ap, scalar1=apof(s1), scalar2=apof(s2),
                                                      op0=op0, op1=op1),
                       reads=rd(in0, s1, s2), writes=out.res)

        def stt(out, in0, sc, in1, op0, op1):
            P.emit(DVE, lambda e: e.scalar_tensor_tensor(out=out.ap, in0=in0.ap, scalar=apof(sc), in1=in1.ap,
                                                         op0=op0, op1=op1),
                   reads=rd(in0, sc, in1), writes=out.res)

        def cp(eng, out, in_):
            if eng == ACT:
                act(out, in_, AF.Copy)
            else:
                P.emit(eng, lambda e: e.tensor_copy(out=out.ap, in_=in_.ap), reads=rd(in_), writes=out.res)

        def mset(eng, out, val):
            P.emit(eng, lambda e: e.memset(out.ap, val), writes=out.res)

        def scan(out, d0, d1, init, op0, op1):
            P.emit(DVE, lambda e: e.tensor_tensor_scan(out=out.ap, data0=d0.ap, data1=d1.ap, initial=init,
                                                       op0=op0, op1=op1),
                   reads=rd(d0, d1), writes=out.res)

        def rcp(out, in_):
            P.emit(DVE, lambda e: e.reciprocal(out=out.ap, in_=in_.ap), reads=rd(in_), writes=out.res)

        def dma(eng, out_ap, in_ap, reads=(), writes=(), sem=None):
            P.emit(eng, lambda e: e.dma_start(out=out_ap, in_=in_ap), reads=list(reads), writes=list(writes), dma=sem)

        dbg_n = [0]

        def dump(key, t, dst_key=None):
            if key in dbg_d:
                dst = dbg_d[key] if dst_key is None else dbg_d[key][dst_key]
                dbg_n[0] += 1
                res = []
                for r_ in t.res:
                    res += [n for n in P.res if n.startswith(r_)]
                pieces = [(t.ap, dst)] if len(t.ap.shape) == 2 else [(t.ap[:, j_, :], dst[:, j_, :]) for j_ in range(t.ap.shape[1])]
                for sap, dap in pieces:
                    stg = AR.take("dbgstg", list(sap.shape), F32)
                    P.emit(DVE, lambda e, sap=sap, stg=stg: e.tensor_copy(out=stg.ap, in_=sap), reads=res, writes=[stg.name + ":"])
                    dma(SP, dap, stg.ap, reads=[stg.name + ":"], sem=f"dbg{dbg_n[0] % 4}")
                    stg.free()

        cbf = AR.take("cbf", [128, NCB], BF16)
        cfc = AR.take("cf", [128, NCF], F32)
        colb = AR.take("cols", [128, NCOL], F32)
        browb = AR.take("brow", [128, 2048], BF16)
        lamb = AR.take("lam", [1, 256], F32, parts=1)
        misc = AR.take("misc", [128, 16], F32)
        dma(POOL, cbf.ap, cbf_d, writes=[cbf.name + ":"], sem="c_cbf")
        dma(SP, cfc.ap, cf_d, writes=[cfc.name + ":"], sem="c_cf")
        dma(SP, colb.ap, cols_d, writes=[colb.name + ":"], sem="c_cols")
        dma(POOL, browb.ap, brow_d.partition_broadcast(128), writes=[browb.name + ":"], sem="c_brow")
        dma(SP, lamb.ap, lam_d, writes=[lamb.name + ":"], sem="c_lam")
        CBT = cbf.t()
        ident_b = CBT[:, CB_ID:CB_ID + 128]
        ones_b = CBT[:, CB_ONE:CB_ONE + 128]
        onesD_b = CBT[:, CB_OD:CB_OD + 128]
        ones256_b = CBT[:, CB_O256:CB_O256 + 128]
        ones128_b = CBT[:, CB_O128:CB_O128 + 128]
        mask_b = CBT[:, CB_MASK:CB_MASK + 128]
        perm_b = CBT[:, CB_PERM:CB_PERM + 128]
        CFT = cfc.t()
        ident_f = CFT[:, CF_ID:CF_ID + 128]

        def sel_f(h):
            return CFT[0:4, CF_SEL + h * 128:CF_SEL + h * 128 + 128]

        ones_row_f = CFT[0:1, CF_SEL:CF_SEL + 128]
        ones_row_b = T(cbf.ap[0:1, CB_ONE:CB_ONE + 128], cbf.name + ":")
        COLT = colb.t()

        def col(name, parts=128):
            i = COLS[name]
            return COLT[0:parts, i:i + 1]

        MISC = misc.t()
        neglam = MISC[:, 0:1]
        nbf = MISC[0:4, 1:2]
        dagn = AR.take("dagn", [128, 8], F32)
        for j in range(8):
            ts(DVE, dagn.t()[:, j:j + 1], col(f"danw{j}"), 1.0 - LAM_INIT, ALU.mult)
        DAGN = dagn.t()
        lt = AR.take("lamtmp", [1, 192], F32, parts=1)
        LT = lt.t()
        LAMT = lamb.t()
        tt(DVE, LT[:, 0:64], LAMT[:, 0:64], LAMT[:, 64:128], ALU.mult)
        tt(DVE, LT[:, 64:128], LAMT[:, 128:192], LAMT[:, 192:256], ALU.mult)
        P.emit(DVE, lambda e: e.tensor_reduce(out=lt.ap[:, 128:129], in_=lt.ap[:, 0:64], axis=mybir.AxisListType.X,
                                              op=ALU.add), reads=LT.res, writes=LT.res)
        P.emit(DVE, lambda e: e.tensor_reduce(out=lt.ap[:, 129:130], in_=lt.ap[:, 64:128], axis=mybir.AxisListType.X,
                                              op=ALU.add), reads=LT.res, writes=LT.res)
        act(LT[:, 130:132], LT[:, 128:130], AF.Exp)
        tt(DVE, LT[:, 132:133], LT[:, 131:132], LT[:, 130:131], ALU.subtract)
        ts(DVE, LT[:, 133:134], LT[:, 132:133], -LAM_INIT, ALU.add)
        mm(PS[0][:, 0:2], ones_row_f, LT[:, 132:134])
        cp(DVE, neglam, PS[0][:, 1:2])
        ts(DVE, nbf, col("bf", 4), -1.0, ALU.mult)
        lt.free()

        diagb = AR.take("diag", [128, 32, 128], BF16)
        for h_ in range(4):
            for qi_, qk_ in enumerate("qk"):
                for tap_ in range(4):
                    ts(DVE, T(diagb.ap[:, (h_ * 2 + qi_) * 4 + tap_, :], diagb.name + ":"), ident_f,
                       col(f"cw{qk_}{h_}_{tap_}"), ALU.mult)
        DIAG = diagb.t()

        NSLOT = 3
        wslots = [AR.take(f"wslot{i}", [128, CHW], BF16) for i in range(NSLOT)]
        wstate = {"n": 0}

        def wload(key):
            i = wstate["n"] % NSLOT
            wstate["n"] += 1
            n = CHLEN[key]
            slot = wslots[i]
            res = f"{slot.name}:w"
            half = n // 2
            src = wpk_d[CHIDX[key]]
            P.emit(POOL, lambda e: e.dma_start(out=slot.ap[:, 0:n], in_=src[:, 0:n]), writes=[res], dma=f"ws{i}")
            return T(slot.ap[:, 0:n], res)

        def wview(wt, ncols, kc=KC):
            return [wt[:, k * ncols:(k + 1) * ncols] for k in range(kc)]

        for sq in range(NSEQ):
            xTb = AR.take("xTb", [128, KC, S], BF16, top=True)
            xsrc = xT_d[sq].rearrange("(kc p) s -> p kc s", p=128)
            for k in range(KC):
                P.emit(POOL, lambda e, k=k, xTb=xTb, xsrc=xsrc: e.dma_start(out=xTb.ap[:, k, :], in_=xsrc[:, k, :]),
                       writes=[f"{xTb.name}:{k}"], dma=f"x{k}")

            def xT(k, lo, hi):
                return T(xTb.ap[:, k, lo:hi], f"{xTb.name}:{k}")

            rows = {n: AR.take("row_" + n, [4, S], F32, parts=4) for n in ("i", "l", "a", "A", "w1", "w2")}
            negA = AR.take("negA", [4, S], F32, parts=4)
            wqr = AR.take("wqr", [4, S], F32, parts=4)
            flr = AR.take("flr", [4, S], F32, parts=4)
            aT = AR.take("aT", [128, NT, 4], F32)
            wsT = AR.take("wsT", [128, NT, 4], F32)
            decb = AR.take("decb", [128, 4, NT], F32)
            wifb = AR.take("wifb", [128, KC * 8], BF16)
            dma(POOL, wifb.ap, wif_d, writes=[wifb.name + ":"], sem="wif")
            R = {n: b.t() for n, b in rows.items()}
            for tb in range(TB):
                sl = slice(tb * 512, tb * 512 + 512)
                for g, ps in ((0, PS[6]), (1, PS[7])):
                    for k in range(KC):
                        mm(ps[0:4, :], T(wifb.ap[:, k * 8 + g * 4:k * 8 + g * 4 + 4], wifb.name + ":"),
                           xT(k, tb * 512, tb * 512 + 512), start=(k == 0), stop=(k == KC - 1))
                act(R["i"][:, sl], PS[6][0:4, :], AF.Identity, bias=col("bi", 4))
                act(R["l"][:, sl], PS[7][0:4, :], AF.Exp, bias=nbf, scale=-1.0)
            act(R["l"], R["l"], AF.Ln, bias=1.0)
            mset(DVE, R["w1"], 1.0)
            scan(R["w2"], R["w1"], R["l"], 0.0, ALU.mult, ALU.add)
            tt(DVE, R["a"], R["i"], R["w2"], ALU.add)
            scan(R["A"], R["a"], R["a"], 0.0, ALU.max, ALU.max)
            ts(DVE, negA.t(), R["A"], -1.0, ALU.mult)
            tt(DVE, flr.t(), R["w2"], R["A"], ALU.subtract)
            act(flr.t(), flr.t(), AF.Exp)
            A3 = rows["A"].ap.rearrange("p (c t) -> p c t", t=128)
            w13 = rows["w1"].ap.rearrange("p (c t) -> p c t", t=128)
            il3 = rows["i"].ap.rearrange("p (c t) -> p c t", t=128)
            P.emit(DVE, lambda e, w13=w13, A3=A3: e.tensor_copy(out=w13, in_=A3[:, :, 127:128].to_broadcast([4, NT, 128])),
                   reads=R["A"].res, writes=R["w1"].res)
            mset(DVE, T(il3[:, 0, :], R["i"].res), 0.0)
            if NT > 1:
                P.emit(DVE, lambda e, il3=il3, w13=w13: e.tensor_copy(out=il3[:, 1:NT, :], in_=w13[:, 0:NT - 1, :]),
                       reads=R["w1"].res, writes=R["i"].res)
            tt(DVE, wqr.t(), R["i"], R["A"], ALU.subtract)
            act(wqr.t(), wqr.t(), AF.Exp, bias=LNC)
            tt(DVE, R["l"], R["a"], R["w1"], ALU.subtract)
            act(R["l"], R["l"], AF.Exp)
            P.emit(DVE, lambda e, rows=rows, il3=il3, w13=w13: e.tensor_tensor(out=rows["w2"].ap[:, 0:NT], in0=il3[:, :, 0], in1=w13[:, :, 0],
                                                  op=ALU.subtract), reads=R["i"].res + R["w1"].res, writes=R["w2"].res)
            act(R["w2"][:, 0:NT], R["w2"][:, 0:NT], AF.Exp)
            for c in range(NT):
                tr(PS[6][:, c * 4:c * 4 + 4], R["a"][:, c * 128:c * 128 + 128], ident_f[0:4, 0:4])
                tr(PS[7][:, c * 4:c * 4 + 4], R["l"][:, c * 128:c * 128 + 128], ident_f[0:4, 0:4])
            ts(DVE, T(aT.ap.rearrange("p c h -> p (c h)"), aT.name + ":"), PS[6][:, 0:NT * 4], LNC, ALU.add)
            cp(DVE, T(wsT.ap.rearrange("p c h -> p (c h)"), wsT.name + ":"), PS[7][:, 0:NT * 4])
            for h in range(4):
                mm(PS[6][:, h * NT:(h + 1) * NT], sel_f(h), R["w2"][:, 0:NT])
            cp(DVE, T(decb.ap.rearrange("p h c -> p (h c)"), decb.name + ":"), PS[6][:, 0:4 * NT])
            dump("negA", negA.t())
            dump("flr", flr.t())
            dump("wqr", wqr.t())
            for b_ in rows.values():
                b_.free()
            wifb.free()

            hnA = AR.take("hnA", [128, 8, S], BF16)
            qT_b = AR.take("qT", [128, S], BF16)
            kT_b = AR.take("kT", [128, S], BF16)
            vml = AR.take("vml", [128, NT, 256], BF16)
            ogb = AR.take("og", [128, 2, S], BF16)
            xpad = {qk: [AR.take(f"xp{qk}{i}", [128, 516], BF16) for i in range(2)] for qk in "qk"}
            sgb = [AR.take(f"sg{i}", [128, 512], F32) for i in range(2)]
            Cst = AR.take("Cst", [128, 384], F32)
            Cbf = [AR.take(f"Cbf{i}", [128, 384], BF16) for i in range(3)]
            hbuf = [AR.take(f"hbuf{i}", [128, 2, 512], BF16) for i in range(2)]
            hsq = AR.take("hsq", [128, 2, 512], BF16)
            DTb = [AR.take(f"DT{i}", [128, 128], F32) for i in range(2)]
            Stb = [AR.take(f"St{i}", [128, 128], BF16) for i in range(2)]
            qsb = [AR.take(f"qs{i}", [128, 128], BF16) for i in range(2)]
            ksb = [AR.take(f"ks{i}", [128, 128], BF16) for i in range(2)]
            flb = [AR.take(f"fl{i}", [128, 128], F32) for i in range(2)]
            dnb = [AR.take(f"dn{i}", [128, 128], F32) for i in range(2)]
            lnm = AR.take("lnm", [128, 512], F32)
            lnv = AR.take("lnv", [128, 512], F32)
            lnr = AR.take("lnr", [128, 512], F32)
            lnt = [AR.take(f"lnt{i}", [128, 512], F32) for i in range(2)]
            for h in range(4):
                wqk = wview(wload(f"mlqk{h}"), 256)
                wv = wview(wload(f"mlv{h}"), 256)
                wo = wview(wload(f"mlo{h}"), 256)
                late_cv = []
                for qi, (qk, dstb) in enumerate((("q", qT_b), ("k", kT_b))):
                    for tb in range(TB):
                        ps = PS[6 + ((qi * TB + tb) % 2)]
                        for k in range(KC):
                            mm(ps, wqk[k][:, qi * 128:qi * 128 + 128], xT(k, tb * 512, tb * 512 + 512),
                               start=(k == 0), stop=(k == KC - 1))
                        xp = xpad[qk][tb % 2]
                        xpp = xpad[qk][(tb + 1) % 2]
                        XP = xp.t()
                        if tb == 0:
                            mset(DVE, XP[:, 0:4], 0.0)
                        else:
                            cp(DVE, XP[:, 0:4], xpp.t()[:, 512:516])
                        act(XP[:, 4:516], ps, AF.Identity, bias=col(f"b{qk}{h}"))

                        def conv_part(qi=qi, qk=qk, tb=tb, XP=XP, dstb=dstb):
                            pcv = PS[4 + ((qi * TB + tb) % 2)]
                            for tap in range(4):
                                mm(pcv, DIAG[:, ((h * 2 + qi) * 4 + tap), :], XP[:, 1 + tap:513 + tap],
                                   start=(tap == 0), stop=(tap == 3))
                            sg = sgb[tb % 2].t()
                            act(sg, pcv, AF.Sigmoid, bias=col(f"cb{qk}{h}"))
                            stt(T(dstb.ap[:, tb * 512:tb * 512 + 512], f"{dstb.name}:{tb}"), pcv, col(f"cb{qk}{h}"), sg,
                                ALU.add, ALU.mult)

                        while late_cv:
                            late_cv.pop(0)()
                        late_cv.append(conv_part)
                for tt_ in range(NT):
                    ps = PS[6 + (tt_ % 2)]
                    for k in range(KC):
                        mm(ps[:, 0:256], xT(k, tt_ * 128, tt_ * 128 + 128), wv[k], start=(k == 0), stop=False)
                    mm(ps[:, 0:256], ident_b, T(browb.ap[:, h * 256:h * 256 + 256], browb.name + ":"),
                       start=False, stop=True)
                    cp(ACT, T(vml.ap[:, tt_, :], f"{vml.name}:{tt_}"), ps[:, 0:256])
                    while late_cv:
                        late_cv.pop(0)()
                for b2 in range(2):
                    for tb in range(TB):
                        ps = PS[6 + ((b2 * TB + tb) % 2)]
                        for k in range(KC):
                            mm(ps, wo[k][:, b2 * 128:b2 * 128 + 128], xT(k, tb * 512, tb * 512 + 512),
                               start=(k == 0), stop=(k == KC - 1))
                        act(T(ogb.ap[:, b2, tb * 512:tb * 512 + 512], f"{ogb.name}:{b2}_{tb}"), ps, AF.Sigmoid,
                            bias=col(f"bo{h}_{b2}"))
                if h == 0:
                    dump("mq0", qT_b.t())
                    dump("mk0", kT_b.t())
                CS = Cst.t()
                CBS = [Cbf[0].t(), Cbf[1].t(), Cbf[2].t()]

                def stageA(c):
                    i2 = c % 2
                    cs = slice(c * 128, c * 128 + 128)
                    tb = c // 4
                    pa = PS[0 + i2]
                    po = PS[2 + i2]
                    qc = T(qT_b.ap[:, cs], f"{qT_b.name}:{tb}")
                    kc_ = T(kT_b.ap[:, cs], f"{kT_b.name}:{tb}")
                    mm(pa[:, 0:128], ident_b, mask_b, start=True, stop=False)
                    mm(pa[:, 0:128], sel_f(h), negA.t()[:, cs], start=False, stop=True)
                    mm(pa[:, 256:384], sel_f(h), wqr.t()[:, cs])
                    mm(po[:, 384:512], sel_f(h), flr.t()[:, cs])
                    mm(pa[:, 128:256], kc_, qc)
                    ptb = T(pbank[5][:].bitcast(BF16)[:, 0:128], "ps5")
                    tr(ptb, kc_, ident_b)
                    DT = DTb[i2].t()
                    act(DT, pa[:, 0:128], AF.Exp, bias=T(aT.ap[:, c, h:h + 1], aT.name + ":"))
                    act(ksb[i2].t(), ptb, AF.Copy, scale=T(wsT.ap[:, c, h:h + 1], wsT.name + ":"))
                    tt(DVE, Stb[i2].t(), pa[:, 128:256], DT, ALU.mult)
                    tt(DVE, qsb[i2].t(), pa[:, 256:384], qc, ALU.mult)

                def stageU(c):
                    if c >= NT - 1:
                        return
                    pc = PS[4]
                    ks = ksb[c % 2].t()
                    vc = T(vml.ap[:, c, :], f"{vml.name}:{c}")
                    mm(pc[:, 0:256], ks, vc)
                    mm(pc[:, 256:384], ks, ones_b)
                    if c == 0:
                        cp(DVE, CS, pc[:, 0:384])
                    else:
                        stt(CS, CS, T(decb.ap[:, h, c:c + 1], decb.name + ":"), pc[:, 0:384], ALU.mult, ALU.add)
                    cp(ACT, CBS[(c + 1) % 3], CS)

                def stageB(c):
                    i2 = c % 2
                    tb = c // 4
                    po = PS[2 + i2]
                    St, qs = Stb[i2].t(), qsb[i2].t()
                    CB_ = CBS[c % 3]
                    vc = T(vml.ap[:, c, :], f"{vml.name}:{c}")
                    for b3 in range(3):
                        lh = vc[:, b3 * 128:b3 * 128 + 128] if b3 < 2 else ones_b
                        mm(po[:, b3 * 128:b3 * 128 + 128], lh, St, start=True, stop=(c == 0))
                        if c > 0:
                            mm(po[:, b3 * 128:b3 * 128 + 128], CB_[:, b3 * 128:b3 * 128 + 128], qs, start=False, stop=True)
                    dn_ = dnb[i2].t()
                    act(dn_, po[:, 256:384], AF.Abs)
                    tt(DVE, dn_, dn_, po[:, 384:512], ALU.max)
                    act(dn_, dn_, AF.Ln)
                    act(dn_, dn_, AF.Exp, scale=-1.0)
                    HB = hbuf[tb % 2].t("c%d" % (c % 4))
                    for b3 in range(2):
                        tt(DVE, HB[:, b3, (c % 4) * 128:(c % 4) * 128 + 128], po[:, b3 * 128:b3 * 128 + 128], dn_, ALU.mult)

                def stageLN(tb):
                    hb_ = hbuf[tb % 2]
                    HALL = T(hb_.ap, [f"{hb_.name}:c{x}" for x in range(4)])
                    act(hsq.t(), HALL, AF.Square)
                    pm, pq = PS[6], PS[7]
                    for b3 in range(2):
                        mm(pm, ones256_b, HALL[:, b3, :], start=(b3 == 0), stop=(b3 == 1))
                    for b3 in range(2):
                        mm(pq, ones256_b, hsq.t()[:, b3, :], start=(b3 == 0), stop=(b3 == 1))
                    act(lnm.t(), pm, AF.Copy)
                    act(lnv.t(), pm, AF.Square)
                    tt(DVE, lnv.t(), pq, lnv.t(), ALU.subtract)
                    act(lnr.t(), lnv.t(), AF.Ln, bias=EPS)
                    act(lnr.t(), lnr.t(), AF.Exp, scale=-0.5)
                    for b3 in range(2):
                        lt_ = lnt[b3].t()
                        tt(DVE, lt_, HALL[:, b3, :], lnm.t(), ALU.subtract)
                        tt(DVE, lt_, lt_, lnr.t(), ALU.mult)
                        stt(T(hnA.ap[:, 2 * h + b3, tb * 512:tb * 512 + 512], f"{hnA.name}:{2 * h + b3}_{tb}"), lt_,
                            col(f"mlnw{2 * h + b3}"),
                            T(ogb.ap[:, b3, tb * 512:tb * 512 + 512], f"{ogb.name}:{b3}_{tb}"), ALU.mult, ALU.mult)

                stageA(0)
                ln_pending = []
                for c in range(NT):
                    if c + 1 < NT:
                        stageA(c + 1)
                    stageB(c)
                    stageU(c)
                    if ln_pending and c >= ln_pending[0][1]:
                        stageLN(ln_pending.pop(0)[0])
                    if c % 4 == 3:
                        ln_pending.append((c // 4, c + 2))
                for tb_, _ in ln_pending:
                    stageLN(tb_)
            dump("hnA", T(hnA.ap, hnA.name + ":"))
            for b_ in ([qT_b, kT_b, vml, ogb, Cst, hsq, lnm, lnv, lnr, negA, wqr, flr, aT, wsT, decb]
                       + Cbf + hbuf + xpad["q"] + xpad["k"] + sgb + DTb + Stb + qsb + ksb + flb + dnb + lnt):
                b_.free()

            cosb = AR.take("cosT", [128, S], F32)
            sinb = AR.take("sinT", [128, S], F32)
            posi = AR.take("posi", [128, S], I32)
            ang = AR.take("ang", [128, S], F32)
            tmpa = AR.take("tmpa", [128, S], F32)
            tmpb = AR.take("tmpb", [128, S], F32)
            dma(SP, posi.ap, pos_d[sq:sq + 1, :].partition_broadcast(128), writes=[posi.name + ":"], sem="pos")
            cp(DVE, ang.t(), posi.t())
            ts(DVE, ang.t(), ang.t(), col("invf"), ALU.mult)
            for tab, shift, scale_ap in ((sinb, 0.0, col("sgn")), (cosb, math.pi / 2, None)):
                src_ = ang.t()
                if shift != 0.0:
                    ts(DVE, tmpb.t(), ang.t(), shift, ALU.add)
                    src_ = tmpb.t()
                ts(DVE, tmpa.t(), src_, 1.0 / TWO_PI, ALU.mult, MAGIC, ALU.add)
                ts(DVE, tmpa.t(), tmpa.t(), MAGIC, ALU.subtract)
                stt(tmpa.t(), tmpa.t(), -TWO_PI, src_, ALU.mult, ALU.add)
                ts(DVE, tmpa.t(), tmpa.t(), 3.14159, ALU.min, -3.14159, ALU.max)
                act(tab.t(), tmpa.t(), AF.Sin, scale=scale_ap)
            for b_ in (posi, ang, tmpa, tmpb):
                b_.free()
            dump("cos", cosb.t())
            dump("sin", sinb.t())

            oB = AR.take("oB", [128, 8, S], BF16, top=True)
            qz = [AR.take(f"qz{i}", [128, S], BF16) for i in range(2)]
            mset(DVE, T(qz[0].ap[64:128, :], f"{qz[0].name}:z"), 0.0)
            mset(DVE, T(qz[1].ap[0:64, :], f"{qz[1].name}:z"), 0.0)
            kr = AR.take("kr", [128, S], BF16)
            vda = AR.take("vda", [128, NT, 128], BF16)
            raw = [AR.take(f"raw{i}", [128, 512], BF16) for i in range(2)]
            t1 = [AR.take(f"t1_{i}", [128, 512], F32) for i in range(2)]
            u1 = [AR.take(f"u1_{i}", [128, 512], F32) for i in range(2)]
            ET = [AR.take(f"ET{i}", [128, 512], BF16) for i in range(6)]
            o1b = AR.take("o1", [128, 512], F32)
            odbs = [AR.take(f"od{i}", [128, 512], F32) for i in range(2)]
            epi = []
            nblk = [0]
            rcb = [AR.take(f"rc{i}", [128, 512], F32) for i in range(2)]
            osqb = AR.take("osq", [128, 512], BF16)
            rsb = AR.take("rs", [128, 512], F32)
            cnt = {"raw": 0, "et": 0, "sc": 0, "acc": 0}
            for hd in range(8):
                wt = wload(f"da{hd}")
                wk = wview(wt, 384)
                late = []
                for which, dstb, bname in ((0, None, f"dbq{hd}"), (1, kr, f"dbk{hd}")):
                    for tb in range(TB):
                        ps = PS[6 + (cnt["raw"] % 2)]
                        for k in range(KC):
                            mm(ps, wk[k][:, which * 128:which * 128 + 128], xT(k, tb * 512, tb * 512 + 512),
                               start=(k == 0), stop=(k == KC - 1))
                        i2 = cnt["raw"] % 2
                        cnt["raw"] += 1
                        rw = raw[i2].t()
                        act(rw, ps, AF.Identity, bias=col(bname))

                        def rope_part(i2=i2, rw=rw, tb=tb, dstb=dstb):
                            ps2 = PS[4 + i2]
                            mm(ps2, perm_b, rw)
                            tt(DVE, t1[i2].t(), rw, cosb.t()[:, tb * 512:tb * 512 + 512], ALU.mult)
                            tt(DVE, u1[i2].t(), ps2, sinb.t()[:, tb * 512:tb * 512 + 512], ALU.mult)
                            tsl = slice(tb * 512, tb * 512 + 512)
                            if dstb is None:
                                tt(DVE, T(qz[0].ap[0:64, tsl], f"{qz[0].name}:{tb}"), t1[i2].t()[0:64], u1[i2].t()[0:64],
                                   ALU.add)
                                tt(DVE, T(qz[1].ap[64:128, tsl], f"{qz[1].name}:{tb}"), t1[i2].t()[64:128],
                                   u1[i2].t()[64:128], ALU.add)
                            else:
                                tt(DVE, T(dstb.ap[:, tsl], f"{dstb.name}:{tb}"), t1[i2].t(), u1[i2].t(), ALU.add)

                        while late:
                            late.pop(0)()
                        late.append(rope_part)
                for tt_ in range(NT):
                    ps = PS[6 + (tt_ % 2)]
                    for k in range(KC):
                        mm(ps[:, 0:128], xT(k, tt_ * 128, tt_ * 128 + 128), wk[k][:, 256:384], start=(k == 0), stop=False)
                    mm(ps[:, 0:128], ident_b, T(browb.ap[:, 1024 + hd * 128:1024 + hd * 128 + 128], browb.name + ":"),
                       start=False, stop=True)
                    cp(ACT, T(vda.ap[:, tt_, :], f"{vda.name}:{tt_}"), ps[:, 0:128])
                    while late:
                        late.pop(0)()
                if hd == 0:
                    dump("kr0", kr.t())
                for qb in range(TB):
                    q0 = qb * 512
                    for c in range(2):
                        pr = slice(c * 64, c * 64 + 64)
                        Ob = PS[0 + 2 * (cnt["acc"] % 2)]
                        Db = PS[1 + 2 * (cnt["acc"] % 2)]
                        cnt["acc"] += 1
                        tiles = []
                        for j in range(4 * qb + 4):
                            r = j - 4 * qb
                            off = 128 * r if r > 0 else 0
                            tiles.append((j, off, 512 - off, r >= 0))

                        def qk(tile):
                            j, off, w, diag = tile
                            sc = PS[4 + (cnt["sc"] % 3)]
                            et = ET[cnt["et"] % 6]
                            cnt["sc"] += 1
                            cnt["et"] += 1
                            qT_ = T(qz[c].ap[:, q0 + off:q0 + 512], [f"{qz[c].name}:{qb}", f"{qz[c].name}:z"])
                            kT_ = T(kr.ap[:, j * 128:j * 128 + 128], f"{kr.name}:{j // 4}")
                            mm(sc[:, 0:w], kT_, qT_, start=True, stop=not diag)
                            if diag:
                                mm(sc[:, 0:128], ident_b, mask_b, start=False, stop=True)
                            e_ = et.t()[:, 0:w]
                            act(e_, sc[:, 0:w], AF.Exp, scale=0.125)
                            return e_

                        def pv(tile, e_, first, last):
                            j, off, w, diag = tile
                            v_ = T(vda.ap[:, j, :], f"{vda.name}:{j}")
                            mm(Ob[:, off:512], v_, e_, start=first, stop=last)
                            mm(Db[:, off:512], ones_b, e_, start=first, stop=last)

                        pend = []
                        issued = 0
                        ntl = len(tiles)
                        while issued < min(2, ntl):
                            pend.append(qk(tiles[issued]))
                            issued += 1
                        for ti in range(ntl):
                            if issued < ntl:
                                pend.append(qk(tiles[issued]))
                                issued += 1
                            pv(tiles[ti], pend.pop(0), ti == 0, ti == ntl - 1)
                            if c == 0 and ti == 2:
                                while epi:
                                    epi.pop(0)()
                        rc = rcb[c].t()
                        rcp(rc, Db)
                        if c == 0:
                            tt(DVE, o1b.t(), Ob, rc, ALU.mult)
                            while epi:
                                epi.pop(0)()
                        else:
                            odb = odbs[nblk[0] % 2]
                            nblk[0] += 1
                            tt(DVE, odb.t(), Ob, rc, ALU.mult)
                            stt(odb.t(), odb.t(), neglam, o1b.t(), ALU.mult, ALU.add)

                            tt(DVE, osqb.t(), odb.t(), odb.t(), ALU.mult)

                            def epilogue(odb=odb, hd=hd, qb=qb, q0=q0):
                                psn = PS[7]
                                mm(psn, ones128_b, osqb.t())
                                act(rsb.t(), psn, AF.Ln, bias=EPS)
                                act(rsb.t(), rsb.t(), AF.Exp, scale=-0.5)
                                stt(T(oB.ap[:, hd, q0:q0 + 512], f"{oB.name}:{hd}_{qb}"), odb.t(), DAGN[:, hd:hd + 1],
                                    rsb.t(), ALU.mult, ALU.mult)

                            epi.append(epilogue)
            while epi:
                epi.pop(0)()
            for b_ in qz + [kr, vda, o1b, osqb, rsb, cosb, sinb] + odbs + raw + t1 + u1 + ET + rcb:
                b_.free()

            def layer_norm(zj_fn, l, out_bf, out_f32, tb, produce=None, cast_eng=ACT):
                zb16 = AR.take("zb16", [128, 8, 512], BF16)
                zsq = AR.take("zsq", [128, 8, 512], BF16)
                mean = AR.take("mean", [128, 512], F32)
                var = AR.take("var", [128, 512], F32)
                rstd = AR.take("rstd", [128, 512], F32)
                pm, pq = PS[6], PS[7]
                pend_st = []
                for j in range(8):
                    if produce is not None:
                        produce(j)
                    zj = zj_fn(j)
                    zbj = T(zb16.ap[:, j, :], f"{zb16.name}:{j}")
                    zsj = T(zsq.ap[:, j, :], f"{zsq.name}:{j}")
                    cp(cast_eng, zbj, zj)
                    act(zsj, zj, AF.Square)
                    pend_st.append((j, zbj, zsj))
                    if len(pend_st) > 3:
                        j_, zb_, zs_ = pend_st.pop(0)
                        mm(pm, onesD_b, zb_, start=(j_ == 0), stop=(j_ == 7))
                        mm(pq, onesD_b, zs_, start=(j_ == 0), stop=(j_ == 7))
                for j_, zb_, zs_ in pend_st:
                    mm(pm, onesD_b, zb_, start=(j_ == 0), stop=(j_ == 7))
                    mm(pq, onesD_b, zs_, start=(j_ == 0), stop=(j_ == 7))
                act(mean.t(), pm, AF.Copy)
                act(var.t(), pm, AF.Square)
                tt(DVE, var.t(), pq, var.t(), ALU.subtract)
                act(rstd.t(), var.t(), AF.Ln, bias=EPS)
                act(rstd.t(), rstd.t(), AF.Exp, scale=-0.5)
                tmps = [AR.take(f"lntmp{i}", [128, 512], F32) for i in range(3)]
                for j in range(8):
                    tmp = tmps[j % 3]
                    tt(DVE, tmp.t(), zj_fn(j), mean.t(), ALU.subtract)
                    tt(DVE, tmp.t(), tmp.t(), rstd.t(), ALU.mult)
                    if out_bf is not None:
                        act(T(out_bf.ap[:, j, tb * 512:tb * 512 + 512], f"{out_bf.name}:{j}_{tb}"), tmp.t(), AF.Identity,
                            bias=col(f"ln{l}b{j}"), scale=col(f"ln{l}g{j}"))
                    if out_f32 is not None:
                        out_f32(j, tmp)
                for b_ in [zb16, zsq, mean, var, rstd] + tmps:
                    b_.free()

            mixin = AR.take("mixin", [128, 8, S], BF16)
            sab = [AR.take(f"sa{i}", [128, 512], F32) for i in range(2)]
            sbb = [AR.take(f"sb{i}", [128, 512], F32) for i in range(2)]
            tab = [AR.take(f"ta{i}", [128, 512], F32) for i in range(2)]
            tbb = [AR.take(f"tb{i}", [128, 512], F32) for i in range(2)]
            for j in range(8):
                wj = wview(wload(f"p4_{j}"), 512)
                for tb in range(TB):
                    i2 = tb % 2
                    pss = [PS[4 * i2 + q] for q in range(4)]
                    srcs = [lambda k: T(hnA.ap[:, k, tb * 512:tb * 512 + 512], f"{hnA.name}:{k}_{tb}"),
                            lambda k: T(oB.ap[:, k, tb * 512:tb * 512 + 512], f"{oB.name}:{k}_{tb}"),
                            lambda k: xT(k, tb * 512, tb * 512 + 512), lambda k: xT(k, tb * 512, tb * 512 + 512)]
                    for q in range(4):
                        for k in range(KC):
                            mm(pss[q], wj[k][:, q * 128:q * 128 + 128], srcs[q](k), start=(k == 0), stop=(k == KC - 1))
                    act(sab[i2].t(), pss[2], AF.Sigmoid, bias=col(f"bga{j}"))
                    act(sbb[i2].t(), pss[3], AF.Sigmoid, bias=col(f"bgb{j}"))
                    tt(DVE, tab[i2].t(), pss[0], sab[i2].t(), ALU.mult)
                    tt(DVE, tbb[i2].t(), pss[1], sbb[i2].t(), ALU.mult)
                    tt(DVE, T(mixin.ap[:, j, tb * 512:tb * 512 + 512], f"{mixin.name}:{j}_{tb}"), tab[i2].t(),
                       tbb[i2].t(), ALU.add)
            for b_ in [hnA, oB, xTb] + sab + sbb + tab + tbb:
                b_.free()
            dump("mixin", T(mixin.ap, mixin.name + ":"))

            x1b = AR.take("x1b", [128, 8, S], BF16, top=True)
            wm = [wview(wload(f"mix{c}"), 512) for c in range(2)]
            zbufs5 = [AR.take(f"z{i}", [128, 8, 512], F32) for i in range(2)]
            xress5 = [AR.take(f"xres{i}", [128, 8, 512], F32) for i in range(2)]
            for tb in range(TB):
                zbuf = zbufs5[tb % 2]
                xres = xress5[tb % 2]
                dma(SP, xres.ap, xT_d[sq].rearrange("(kc p) s -> p kc s", p=128)[:, :, tb * 512:tb * 512 + 512],
                    writes=[xres.name + ":"], sem=f"xres{tb % 2}")
                def prod5(j, tb=tb, zbuf=zbuf, xres=xres):
                    ps = PS[j % 4]
                    for k in range(KC):
                        mm(ps, wm[j // 4][k][:, (j % 4) * 128:(j % 4) * 128 + 128],
                           T(mixin.ap[:, k, tb * 512:tb * 512 + 512], f"{mixin.name}:{k}_{tb}"),
                           start=(k == 0), stop=(k == KC - 1))
                    stt(T(zbuf.ap[:, j, :], f"{zbuf.name}:{j}"), xres.t()[:, j, :], ALPHA, ps, ALU.mult, ALU.add)

                layer_norm(lambda j, zbuf=zbuf: T(zbuf.ap[:, j, :], f"{zbuf.name}:{j}"), 1, x1b, None, tb, produce=prod5)
            for b_ in zbufs5 + xress5:
                b_.free()
            mixin.free()
            dump("x1", T(x1b.ap, x1b.name + ":"))

            memb = AR.take("memb", [128, KC, MEM], BF16)
            msrc = memT_d[sq].rearrange("(kc p) s -> p kc s", p=128)
            P.emit(POOL, lambda e, memb=memb, msrc=msrc: e.dma_start(out=memb.ap, in_=msrc), writes=[memb.name + ":"], dma="mem")
            xkT = AR.take("xkT", [128, 8, MEM], BF16)
            xvb = AR.take("xvb", [128, 2, D], BF16)
            MB = memb.t()
            for c in range(2):
                wc = wview(wload(f"xk{c}"), 512)
                for jj in range(4):
                    ps = PS[jj % 4]
                    for k in range(KC):
                        mm(ps[:, 0:MEM], wc[k][:, jj * 128:jj * 128 + 128], MB[:, k, :], start=(k == 0), stop=(k == KC - 1))
                    cp(ACT, T(xkT.ap[:, c * 4 + jj, :], f"{xkT.name}:{c * 4 + jj}"), ps[:, 0:MEM])
            for c in range(2):
                wc = wview(wload(f"xv{c}"), 512)
                for m in range(2):
                    ps = PS[4 + m]
                    for k in range(KC):
                        mm(ps, MB[:, k, m * 128:m * 128 + 128], wc[k], start=(k == 0), stop=(k == KC - 1))
                    cp(ACT, T(xvb.ap[:, m, c * 512:c * 512 + 512], f"{xvb.name}:{m}_{c}"), ps)
            xqT = AR.take("xqT", [128, 8, S], BF16)
            for c in range(2):
                wc = wview(wload(f"xq{c}"), 512)
                for jj in range(4):
                    for tb in range(TB):
                        ps = PS[(jj * TB + tb) % 4]
                        for k in range(KC):
                            mm(ps, wc[k][:, jj * 128:jj * 128 + 128],
                               T(x1b.ap[:, k, tb * 512:tb * 512 + 512], f"{x1b.name}:{k}_{tb}"),
                               start=(k == 0), stop=(k == KC - 1))
                        cp(ACT if (jj + tb) % 2 else DVE,
                           T(xqT.ap[:, c * 4 + jj, tb * 512:tb * 512 + 512], f"{xqT.name}:{c * 4 + jj}_{tb}"), ps)
            xoT = AR.take("xoT", [128, 8, S], BF16, top=True)
            EX = [AR.take(f"EX{i}", [128, 2, 512], BF16) for i in range(2)]
            rcx = [AR.take(f"rcx{i}", [128, 512], F32) for i in range(2)]
            n_ = 0
            iters = [(hx, tb) for hx in range(4) for tb in range(TB)]

            def xa_scores(n):
                hx, tb = iters[n]
                i2 = n % 2
                for m in range(2):
                    ps = PS[4 + 2 * i2 + m]
                    for kk in range(2):
                        mm(ps, T(xkT.ap[:, hx * 2 + kk, m * 128:m * 128 + 128], f"{xkT.name}:{hx * 2 + kk}"),
                           T(xqT.ap[:, hx * 2 + kk, tb * 512:tb * 512 + 512], f"{xqT.name}:{hx * 2 + kk}_{tb}"),
                           start=(kk == 0), stop=(kk == 1))
                    act(T(EX[i2].ap[:, m, :], f"{EX[i2].name}:{m}"), ps, AF.Exp, scale=1.0 / 16.0)

            def xa_rest(n):
                hx, tb = iters[n]
                i2 = n % 2
                pd = PS[0 + 3 * i2]
                for m in range(2):
                    mm(pd, ones_b, T(EX[i2].ap[:, m, :], f"{EX[i2].name}:{m}"), start=(m == 0), stop=(m == 1))
                act(rcx[i2].t(), pd, AF.Ln)
                act(rcx[i2].t(), rcx[i2].t(), AF.Exp, scale=-1.0)
                for b2 in range(2):
                    po_ = PS[1 + b2]
                    for m in range(2):
                        mm(po_, T(xvb.ap[:, m, hx * 256 + b2 * 128:hx * 256 + b2 * 128 + 128], f"{xvb.name}:{m}_{hx // 2}"),
                           T(EX[i2].ap[:, m, :], f"{EX[i2].name}:{m}"), start=(m == 0), stop=(m == 1))
                    tt(DVE, T(xoT.ap[:, hx * 2 + b2, tb * 512:tb * 512 + 512], f"{xoT.name}:{hx * 2 + b2}_{tb}"),
                       po_, rcx[i2].t(), ALU.mult)

            xa_scores(0)
            for n in range(len(iters)):
                if n + 1 < len(iters):
                    xa_scores(n + 1)
                xa_rest(n)
            for b_ in [memb, xkT, xvb, xqT] + EX + rcx:
                b_.free()
            dump("xo", T(xoT.ap, xoT.name + ":"))

            x2b = AR.take("x2b", [128, 8, S], BF16)
            wx = [wview(wload(f"xo{c}"), 512) for c in range(2)]
            zbufs7 = [AR.take(f"z{i}", [128, 8, 512], F32) for i in range(2)]
            for tb in range(TB):
                zbuf = zbufs7[tb % 2]

                def prod7(j, tb=tb, zbuf=zbuf):
                    ps = PS[j % 4]
                    for k in range(KC):
                        mm(ps, wx[j // 4][k][:, (j % 4) * 128:(j % 4) * 128 + 128],
                           T(xoT.ap[:, k, tb * 512:tb * 512 + 512], f"{xoT.name}:{k}_{tb}"),
                           start=(k == 0), stop=(k == KC - 1))
                    stt(T(zbuf.ap[:, j, :], f"{zbuf.name}:{j}"),
                        T(x1b.ap[:, j, tb * 512:tb * 512 + 512], f"{x1b.name}:{j}_{tb}"), ALPHA, ps, ALU.mult, ALU.add)

                layer_norm(lambda j, zbuf=zbuf: T(zbuf.ap[:, j, :], f"{zbuf.name}:{j}"), 2, x2b, None, tb, produce=prod7)
            for b_ in zbufs7:
                b_.free()
            xoT.free()
            x1b.free()
            dump("x2", T(x2b.ap, x2b.name + ":"))

            zx = AR.take("zx", [128, (34 * S) // 4], F32, top=True)
            x2c_ap = zx.ap[:, (18 * S) // 4:(34 * S) // 4].bitcast(BF16).rearrange("p (a b) -> p a b", b=S)
            zall_ap = zx.ap[:, 0:8 * S].rearrange("p (a b) -> p a b", b=S)

            def x2c(k, tb):
                return T(x2c_ap[:, k, tb * 512:tb * 512 + 512], f"{zx.name}:x{k}_{tb}")

            for k in range(KC):
                for tb in range(TB):
                    cp(DVE if (k + tb) % 2 else ACT, x2c(k, tb),
                       T(x2b.ap[:, k, tb * 512:tb * 512 + 512], f"{x2b.name}:{k}_{tb}"))
            x2b.free()
            hid = AR.take("hid", [128, NF, S], BF16)
            sgf = [AR.take(f"sgf{i}", [128, 512], F32) for i in range(2)]
            tf = [AR.take(f"tf{i}", [128, 512], F32) for i in range(2)]
            n_ = 0
            for c in range(NF // 2):
                wc = wview(wload(f"gu{c}"), 512)
                for ff in range(2):
                    f = 2 * c + ff
                    for tb in range(TB):
                        i2 = n_ % 2
                        n_ += 1
                        pg, pu = PS[2 * i2], PS[2 * i2 + 1]
                        for q, ps in ((0, pg), (1, pu)):
                            for k in range(KC):
                                mm(ps, wc[k][:, (2 * ff + q) * 128:(2 * ff + q) * 128 + 128], x2c(k, tb),
                                   start=(k == 0), stop=(k == KC - 1))
                        act(sgf[i2].t(), pg, AF.Sigmoid)
                        tt(DVE, tf[i2].t(), pg, sgf[i2].t(), ALU.mult)
                        tt(DVE, T(hid.ap[:, f, tb * 512:tb * 512 + 512], f"{hid.name}:{f}_{tb}"), pu, tf[i2].t(), ALU.mult)
            for b_ in sgf + tf:
                b_.free()
            for j in range(8):
                wd = wview(wload(f"dn{j}"), 128, kc=NF)
                for jp in (2 * j - 9, 2 * j - 8):
                    if 0 <= jp < 8:
                        summ = P.summary(f"{zx.name}:x{jp}_")
                        for tb in range(TB):
                            P.add_readers(f"{zx.name}:z{j}_{tb}", summ)
                for tb in range(TB):
                    ps = PS[(j * TB + tb) % 4]
                    for f in range(NF):
                        mm(ps, wd[f], T(hid.ap[:, f, tb * 512:tb * 512 + 512], f"{hid.name}:{f}_{tb}"),
                           start=(f == 0), stop=(f == NF - 1))
                    stt(T(zall_ap[:, j, tb * 512:tb * 512 + 512], f"{zx.name}:z{j}_{tb}"), x2c(j, tb), ALPHA, ps,
                        ALU.mult, ALU.add)
            hid.free()
            osrc = outT_d[sq].rearrange("(kc p) s -> p kc s", p=128)
            yos = [AR.take(f"yo{i}", [128, 512], F32) for i in range(4)]
            for tb in range(TB):
                def outcb(j, tmp, tb=tb):
                    yo = yos[j % 4]
                    act(yo.t(), tmp.t(), AF.Identity, bias=col(f"ln3b{j}"), scale=col(f"ln3g{j}"))
                    dma(SP, osrc[:, j, tb * 512:tb * 512 + 512], yo.ap, reads=yo.t().res, sem=f"out{j}")

                layer_norm(lambda j, tb=tb: T(zall_ap[:, j, tb * 512:tb * 512 + 512], f"{zx.name}:z{j}_{tb}"),
                           3, None, outcb, tb, cast_eng=DVE)
            zx.free()
            for b_ in yos:
                b_.free()

        P.final_wait(SP, ["yo"])
        P.replay()
        print(f"[build] arena peak {AR.peak} / {ARB}; instr counts {P.cnt}")
    return nc


_CACHE = {}


def kernel(**inputs):
    x = np.asarray(inputs["x"], np.float32)
    mem = np.asarray(inputs["mem"], np.float32)
    pos = np.asarray(inputs["positions"], np.int32)
    B, S, _ = x.shape
    n_cores = 8
    nseq = B // n_cores
    pk = pack_host({k: np.asarray(v) for k, v in inputs.items()})
    xT = np.ascontiguousarray(x.transpose(0, 2, 1))
    memT = np.ascontiguousarray(mem.transpose(0, 2, 1))
    key = (S, nseq)
    if key not in _CACHE:
        _CACHE[key] = build(S, nseq)
    nc = _CACHE[key]
    in_maps = []
    for c in range(n_cores):
        sl = slice(c * nseq, (c + 1) * nseq)
        m = dict(xT=xT[sl], memT=memT[sl], pos=pos[sl])
        m.update(pk)
        in_maps.append(m)
    res = run_bass_kernel_spmd(nc, in_maps, core_ids=list(range(n_cores)))
    outT = np.concatenate([r["outT"] for r in res.results], axis=0)
    return np.ascontiguousarray(outT.transpose(0, 2, 1))
```

```python
import math
from contextlib import ExitStack
import numpy as np
import concourse.bass as bass
import concourse.mybir as mybir
from concourse.bass_utils import run_bass_kernel_spmd

F32 = mybir.dt.float32
BF16 = mybir.dt.bfloat16
I32 = mybir.dt.int32
AF = mybir.ActivationFunctionType
ALU = mybir.AluOpType
PE, ACT, DVE, POOL, SP = "tensor", "scalar", "vector", "gpsimd", "sync"
ENGS = [PE, ACT, DVE, POOL, SP]

D = 1024
KC = 8
MEM = 256
FH = 2816
NF = FH // 128
ALPHA = 2.0 ** 0.25
EPS = 1e-5
LAM_INIT = 0.8 - 0.6 * math.exp(0.0)
MAGIC = 12582912.0
LNC = math.log(128.0 ** -0.5)
TWO_PI = 2.0 * math.pi
CHW = 4096


class T:
    def __init__(self, ap, res):
        self.ap = ap
        self.res = list(res) if isinstance(res, (list, tuple)) else [res]

    def __getitem__(self, k):
        return T(self.ap[k], self.res)


class Prog:
    def __init__(self, nc, stack):
        self.nc = nc
        self.stack = stack
        self.ops = {e: [] for e in ENGS}
        self.cnt = {e: 0 for e in ENGS}
        self.esem = {e: stack.enter_context(nc.semaphore("s_" + e)) for e in ENGS}
        self.res = {}
        self.known = {e: {} for e in ENGS}
        self.dsem = {}
        self.dcnt = {}
        self.prefix_init = []

    def _r(self, name):
        r = self.res.get(name)
        if r is None:
            r = {"w": None, "r": {}}
            for pre, summ in self.prefix_init:
                if name.startswith(pre):
                    for src, ev in summ.items():
                        o = r["r"].get(src)
                        if o is None or o[2] < ev[2]:
                            r["r"][src] = ev
            self.res[name] = r
        return r

    def summary(self, prefix):
        out = {}
        for n, r in self.res.items():
            if n.startswith(prefix):
                evs = list(r["r"].values())
                if r["w"] is not None:
                    evs.append(r["w"])
                for ev in evs:
                    o = out.get(ev[0])
                    if o is None or o[2] < ev[2]:
                        out[ev[0]] = ev
        return out

    def add_readers(self, name, summ):
        r = self._r(name)
        for src, ev in summ.items():
            o = r["r"].get(src)
            if o is None or o[2] < ev[2]:
                r["r"][src] = ev

    def emit(self, eng, fn, reads=(), writes=(), dma=None):
        deps = []
        for n in reads:
            r = self._r(n)
            w = r["w"]
            if w is not None:
                deps.append((w, "raw"))
            if n.startswith("ps") and dma is None:
                for src_, ev_ in r["r"].items():
                    if src_ != eng:
                        deps.append((ev_, "psum"))
        for n in writes:
            r = self._r(n)
            if r["w"] is not None:
                deps.append((r["w"], "waw"))
            for ev in r["r"].values():
                deps.append((ev, "war"))
        if dma is None:
            self.cnt[eng] += 1
            ev = (eng, self.esem[eng], self.cnt[eng])
        else:
            if dma not in self.dsem:
                self.dsem[dma] = self.stack.enter_context(self.nc.semaphore("d_" + dma))
                self.dcnt[dma] = 0
            if self.dcnt[dma] > 0:
                deps.append((("dma:" + dma, self.dsem[dma], self.dcnt[dma]), "waw"))
            self.dcnt[dma] += 16
            ev = ("dma:" + dma, self.dsem[dma], self.dcnt[dma])
        waits = {}
        for (src, sem, val), kind in deps:
            if src == eng and dma is None:
                if eng == PE:
                    continue
            if self.known[eng].get(src, 0) >= val:
                continue
            if src not in waits or waits[src][1] < val:
                waits[src] = (sem, val)
        for src, (sem, val) in waits.items():
            self.known[eng][src] = val
        self.ops[eng].append((list(waits.values()), fn, ev))
        for n in reads:
            r = self._r(n)
            o = r["r"].get(ev[0])
            if o is None or o[2] < ev[2]:
                r["r"][ev[0]] = ev
        for n in writes:
            r = self._r(n)
            r["w"] = ev
            r["r"] = {}
        return ev

    def final_wait(self, eng, prefixes):
        waits = {}
        for pre in prefixes:
            for src, ev in self.summary(pre).items():
                if src not in waits or waits[src][1] < ev[2]:
                    waits[src] = (ev[1], ev[2])
        self.ops[eng].append((list(waits.values()), None, None))

    def replay(self):
        nc = self.nc
        with nc.Block() as block:
            for e in ENGS:
                ops = self.ops[e]

                def body(engine, ops=ops):
                    for waits, fn, ev in ops:
                        for sem, val in waits:
                            engine.wait_ge(sem, val)
                        if fn is None:
                            continue
                        ins = fn(engine)
                        ins.then_inc(ev[1], 16 if ev[0].startswith("dma:") else 1)

                getattr(block, e)(body)


class Arena:
    def __init__(self, P, ap, nbytes):
        self.P = P
        self.ap = ap
        self.nbytes = nbytes
        self.live = {}
        self.ghosts = []
        self.gen = 0
        self.peak = 0

    def take(self, base, shape, dtype, parts=128, top=False):
        esz = 2 if dtype == BF16 else 4
        n = 1
        for s in shape[1:]:
            n *= s
        nb = (n * esz + 63) // 64 * 64
        segs = sorted(self.live.values())
        if not top:
            lo = 0
            for a, b in segs:
                if lo + nb <= a:
                    break
                lo = max(lo, b)
        else:
            hi = self.nbytes
            lo = None
            for a, b in reversed(segs):
                if b + nb <= hi:
                    break
                hi = min(hi, a)
            lo = hi - nb
        assert lo >= 0 and lo + nb <= self.nbytes, \
            f"arena overflow taking {base} {nb} (live={sum(b - a for a, b in segs)}) {sorted((v, k) for k, v in self.live.items())}"
        self.gen += 1
        name = f"{base}@{self.gen}"
        self.live[name] = (lo, lo + nb)
        self.peak = max(self.peak, lo + nb)
        summ = {}
        for a, b, s in self.ghosts:
            if a < lo + nb and lo < b:
                for src, ev in s.items():
                    o = summ.get(src)
                    if o is None or o[2] < ev[2]:
                        summ[src] = ev
        if summ:
            self.P.prefix_init.append((name + ":", summ))
        v = self.ap[0:shape[0], lo // 4:(lo + nb) // 4]
        if dtype != F32:
            v = v.bitcast(dtype)
        v = v[:, 0:n]
        if len(shape) == 3:
            v = v.rearrange("p (a b) -> p a b", b=shape[2])
        elif len(shape) == 4:
            v = v.rearrange("p (a b c) -> p a b c", b=shape[2], c=shape[3])
        return Buf(self, name, v)

    def release(self, buf):
        lo, hi = self.live.pop(buf.name)
        self.ghosts.append((lo, hi, self.P.summary(buf.name + ":")))


class Buf:
    def __init__(self, arena, name, ap):
        self.arena = arena
        self.name = name
        self.ap = ap

    def t(self, sub="", key=None):
        ap = self.ap if key is None else self.ap[key]
        return T(ap, f"{self.name}:{sub}")

    def free(self):
        self.arena.release(self)


def chunk_plan():
    plan = []
    for hd in range(8):
        plan.append((f"da{hd}", KC * 384))
    for h in range(4):
        plan.append((f"mlqk{h}", KC * 256))
        plan.append((f"mlv{h}", KC * 256))
        plan.append((f"mlo{h}", KC * 256))
    for j in range(8):
        plan.append((f"p4_{j}", KC * 512))
    for c in range(2):
        plan.append((f"mix{c}", KC * 512))
    for nm in ("xk", "xv", "xq", "xo"):
        for c in range(2):
            plan.append((f"{nm}{c}", KC * 512))
    for c in range(NF // 2):
        plan.append((f"gu{c}", KC * 512))
    for j in range(8):
        plan.append((f"dn{j}", NF * 128))
    return plan


CHUNKS = chunk_plan()
CHIDX = {k: i for i, (k, n) in enumerate(CHUNKS)}
CHLEN = {k: n for k, n in CHUNKS}


def col_plan():
    names = []
    for h in range(4):
        names += [f"bq{h}", f"bk{h}", f"bo{h}_0", f"bo{h}_1"]
        for qk in "qk":
            for tap in range(4):
                names.append(f"cw{qk}{h}_{tap}")
            names.append(f"cb{qk}{h}")
    for hd in range(8):
        names += [f"dbq{hd}", f"dbk{hd}"]
    for j in range(8):
        names += [f"bga{j}", f"bgb{j}", f"mlnw{j}", f"danw{j}"]
        for l in (1, 2, 3):
            names += [f"ln{l}g{j}", f"ln{l}b{j}"]
    names += ["bi", "bf", "invf", "sgn"]
    return {n: i for i, n in enumerate(names)}


COLS = col_plan()
NCOL = len(COLS)

CB_ID, CB_ONE, CB_OD, CB_O256, CB_O128, CB_MASK, CB_PERM = [i * 128 for i in range(7)]
NCB = 7 * 128
CF_ID, CF_SEL = 0, 128
NCF = 128 + 512


def _pk(wc):
    K, n = wc.shape
    kc = K // 128
    a = np.ascontiguousarray(wc.reshape(kc, 128, n).transpose(1, 0, 2)).reshape(128, kc * n)
    out = np.zeros((128, CHW), np.float32)
    out[:, :kc * n] = a
    return out


def pack_host(inp):
    w_in = inp["w_in"][0]
    b_in = inp["b_in"][0]
    O_MLV, O_MLO, O_IF = 1024, 2048, 3072
    O_DQ, O_DK, O_DV, O_G = 3080, 4104, 5128, 6152
    chunks = {}
    for hd in range(8):
        cols = np.r_[O_DQ + hd * 128:O_DQ + hd * 128 + 128, O_DK + hd * 128:O_DK + hd * 128 + 128,
                     O_DV + hd * 128:O_DV + hd * 128 + 128]
        chunks[f"da{hd}"] = _pk(w_in[:, cols])
    for h in range(4):
        cols = np.r_[h * 128:h * 128 + 128, 512 + h * 128:512 + h * 128 + 128]
        chunks[f"mlqk{h}"] = _pk(w_in[:, cols])
        chunks[f"mlv{h}"] = _pk(w_in[:, O_MLV + h * 256:O_MLV + h * 256 + 256])
        chunks[f"mlo{h}"] = _pk(w_in[:, O_MLO + h * 256:O_MLO + h * 256 + 256])
    pa, pb = inp["w_proj_a"][0], inp["w_proj_b"][0]
    for j in range(8):
        sl = slice(j * 128, j * 128 + 128)
        chunks[f"p4_{j}"] = _pk(np.concatenate(
            [pa[:, sl], pb[:, sl], w_in[:, O_G + j * 128:O_G + j * 128 + 128],
             w_in[:, O_G + 1024 + j * 128:O_G + 1024 + j * 128 + 128]], axis=1))
    for nm, key in (("mix", "w_mix_out"), ("xk", "w_xk"), ("xv", "w_xv"), ("xq", "w_xq"), ("xo", "w_xo")):
        w = inp[key][0]
        for c in range(2):
            chunks[f"{nm}{c}"] = _pk(w[:, c * 512:c * 512 + 512])
    wg, wu, wd = inp["w_ffn_gate"][0], inp["w_ffn_up"][0], inp["w_ffn_down"][0]
    for c in range(NF // 2):
        f0, f1 = 2 * c, 2 * c + 1
        chunks[f"gu{c}"] = _pk(np.concatenate(
            [wg[:, f0 * 128:f0 * 128 + 128], wu[:, f0 * 128:f0 * 128 + 128],
             wg[:, f1 * 128:f1 * 128 + 128], wu[:, f1 * 128:f1 * 128 + 128]], axis=1))
    for j in range(8):
        chunks[f"dn{j}"] = _pk(wd[:, j * 128:j * 128 + 128])
    wpk = np.stack([chunks[k] for k, _ in CHUNKS], axis=0)

    wif = np.ascontiguousarray(w_in[:, O_IF:O_IF + 8].reshape(KC, 128, 8).transpose(1, 0, 2)).reshape(128, KC * 8)

    cols = np.zeros((128, NCOL), np.float32)
    conv_w, conv_b = inp["conv_w"][0], inp["conv_b"][0]
    for h in range(4):
        cols[:, COLS[f"bq{h}"]] = b_in[h * 128:h * 128 + 128]
        cols[:, COLS[f"bk{h}"]] = b_in[512 + h * 128:512 + h * 128 + 128]
        cols[:, COLS[f"bo{h}_0"]] = b_in[O_MLO + h * 256:O_MLO + h * 256 + 128]
        cols[:, COLS[f"bo{h}_1"]] = b_in[O_MLO + h * 256 + 128:O_MLO + h * 256 + 256]
        for qk, off in (("q", 0), ("k", 512)):
            ch = slice(off + h * 128, off + h * 128 + 128)
            for tap in range(4):
                cols[:, COLS[f"cw{qk}{h}_{tap}"]] = conv_w[tap, ch]
            cols[:, COLS[f"cb{qk}{h}"]] = conv_b[ch]
    for hd in range(8):
        cols[:, COLS[f"dbq{hd}"]] = b_in[O_DQ + hd * 128:O_DQ + hd * 128 + 128]
        cols[:, COLS[f"dbk{hd}"]] = b_in[O_DK + hd * 128:O_DK + hd * 128 + 128]
    for j in range(8):
        sl = slice(j * 128, j * 128 + 128)
        cols[:, COLS[f"bga{j}"]] = b_in[O_G + j * 128:O_G + j * 128 + 128]
        cols[:, COLS[f"bgb{j}"]] = b_in[O_G + 1024 + j * 128:O_G + 1024 + j * 128 + 128]
        cols[:, COLS[f"mlnw{j}"]] = inp["ml_norm_w"][0][sl]
        cols[:, COLS[f"danw{j}"]] = inp["da_norm_w"][0][sl]
        for l in (1, 2, 3):
            cols[:, COLS[f"ln{l}g{j}"]] = inp[f"ln{l}_g"][0][sl]
            cols[:, COLS[f"ln{l}b{j}"]] = inp[f"ln{l}_b"][0][sl]
    cols[0:4, COLS["bi"]] = b_in[O_IF:O_IF + 4]
    cols[0:4, COLS["bf"]] = b_in[O_IF + 4:O_IF + 8]
    half = 32
    inv = (10000.0 ** (-np.arange(half, dtype=np.float32) / half)).astype(np.float32)
    p = np.arange(128)
    cols[:, COLS["invf"]] = inv[p % 32]
    cols[:, COLS["sgn"]] = np.where((p % 64) < 32, -1.0, 1.0)

    brow = np.concatenate([b_in[O_MLV:O_MLV + 1024], b_in[O_DV:O_DV + 1024]])[None, :].astype(np.float32)
    lam = np.concatenate([inp["lam_q1"][0], inp["lam_k1"][0], inp["lam_q2"][0], inp["lam_k2"][0]])[None, :]
    lam = np.ascontiguousarray(lam, dtype=np.float32)

    cbf = np.zeros((128, NCB), np.float32)
    cbf[:, CB_ID:CB_ID + 128] = np.eye(128)
    cbf[:, CB_ONE:CB_ONE + 128] = 1.0
    cbf[:, CB_OD:CB_OD + 128] = 1.0 / 1024
    cbf[:, CB_O256:CB_O256 + 128] = 1.0 / 256
    cbf[:, CB_O128:CB_O128 + 128] = 1.0 / 128
    s_, t_ = np.meshgrid(np.arange(128), np.arange(128), indexing="ij")
    cbf[:, CB_MASK:CB_MASK + 128] = np.where(s_ <= t_, 0.0, -30000.0)
    partner = np.where((p % 64) < 32, p + 32, p - 32)
    perm = np.zeros((128, 128), np.float32)
    perm[partner, p] = 1.0
    cbf[:, CB_PERM:CB_PERM + 128] = perm
    cf = np.zeros((128, NCF), np.float32)
    cf[:, CF_ID:CF_ID + 128] = np.eye(128)
    for h in range(4):
        cf[h, CF_SEL + h * 128:CF_SEL + h * 128 + 128] = 1.0
    return dict(wpk=wpk, wif=wif, cols=cols, brow=brow, lam=lam, cbf=cbf, cf=cf)


def build(S=2048, NSEQ=2, dbg=None):
    NT, TB = S // 128, S // 512
    nc = bass.Bass("TRN2", target_bir_lowering=False)
    xT_d = nc.dram_tensor("xT", [NSEQ, D, S], F32, kind="ExternalInput").ap()
    memT_d = nc.dram_tensor("memT", [NSEQ, D, MEM], F32, kind="ExternalInput").ap()
    pos_d = nc.dram_tensor("pos", [NSEQ, S], I32, kind="ExternalInput").ap()
    wpk_d = nc.dram_tensor("wpk", [len(CHUNKS), 128, CHW], F32, kind="ExternalInput").ap()
    wif_d = nc.dram_tensor("wif", [128, KC * 8], F32, kind="ExternalInput").ap()
    cols_d = nc.dram_tensor("cols", [128, NCOL], F32, kind="ExternalInput").ap()
    brow_d = nc.dram_tensor("brow", [1, 2048], F32, kind="ExternalInput").ap()
    lam_d = nc.dram_tensor("lam", [1, 256], F32, kind="ExternalInput").ap()
    cbf_d = nc.dram_tensor("cbf", [128, NCB], F32, kind="ExternalInput").ap()
    cf_d = nc.dram_tensor("cf", [128, NCF], F32, kind="ExternalInput").ap()
    outT_d = nc.dram_tensor("outT", [NSEQ, D, S], F32, kind="ExternalOutput").ap()
    dbg_d = {}
    if dbg:
        for k, shp in dbg.items():
            dbg_d[k] = nc.dram_tensor("dbg_" + k, list(shp), F32, kind="ExternalOutput").ap()

    st = ExitStack()
    with st:
        P = Prog(nc, st)
        ARB = 206 * 1024
        arena_t = st.enter_context(nc.sbuf_tensor("arena", [128, ARB // 4], F32))
        AR = Arena(P, arena_t[:], ARB)
        pbank = [st.enter_context(nc.psum_tensor(f"ps{i}", [128, 512], F32)) for i in range(8)]
        PS = [T(pbank[i][:], f"ps{i}") for i in range(8)]

        def rd(*ts):
            out = []
            for t in ts:
                if isinstance(t, T):
                    out += t.res
            return out

        def mm(out, lhsT, rhs, start=True, stop=True):
            P.emit(PE, lambda e: e.matmul(out.ap, lhsT=lhsT.ap, rhs=rhs.ap, start=start, stop=stop),
                   reads=rd(lhsT, rhs), writes=out.res)

        def tr(out, in_, ident):
            P.emit(PE, lambda e: e.transpose(out.ap, in_.ap, ident.ap), reads=rd(in_, ident), writes=out.res)

        def apof(x):
            return x.ap if isinstance(x, T) else x

        def act(out, in_, func, bias=None, scale=None, eng=ACT):
            kw = {}
            if bias is not None:
                kw["bias"] = apof(bias)
            if scale is not None:
                kw["scale"] = apof(scale)
            P.emit(ACT, lambda e: e.activation(out=out.ap, in_=in_.ap, func=func, **kw),
                   reads=rd(in_, bias, scale), writes=out.res)

        def tt(eng, out, in0, in1, op):
            P.emit(eng, lambda e: e.tensor_tensor(out=out.ap, in0=in0.ap, in1=in1.ap, op=op),
                   reads=rd(in0, in1), writes=out.res)

        def ts(eng, out, in0, s1, op0, s2=None, op1=None):
            if op1 is None:
                P.emit(eng, lambda e: e.tensor_scalar(out=out.ap, in0=in0.ap, scalar1=apof(s1), scalar2=None, op0=op0),
                       reads=rd(in0, s1), writes=out.res)
            else:
                P.emit(eng, lambda e: e.tensor_scalar(out=out.ap, in0=in0.ap, scalar1=apof(s1), scalar2=apof(s2),
                                                      op0=op0, op1=op1),
                       reads=rd(in0, s1, s2), writes=out.res)

        def stt(out, in0, sc, in1, op0, op1):
            P.emit(DVE, lambda e: e.scalar_tensor_tensor(out=out.ap, in0=in0.ap, scalar=apof(sc), in1=in1.ap,
                                                         op0=op0, op1=op1),
                   reads=rd(in0, sc, in1), writes=out.res)

        def cp(eng, out, in_):
            if eng == ACT:
                act(out, in_, AF.Copy)
            else:
                P.emit(eng, lambda e: e.tensor_copy(out=out.ap, in_=in_.ap), reads=rd(in_), writes=out.res)

        def mset(eng, out, val):
            P.emit(eng, lambda e: e.memset(out.ap, val), writes=out.res)

        def scan(out, d0, d1, init, op0, op1):
            P.emit(DVE, lambda e: e.tensor_tensor_scan(out=out.ap, data0=d0.ap, data1=d1.ap, initial=init,
                                                       op0=op0, op1=op1),
                   reads=rd(d0, d1), writes=out.res)

        def rcp(out, in_):
            P.emit(DVE, lambda e: e.reciprocal(out=out.ap, in_=in_.ap), reads=rd(in_), writes=out.res)

        def dma(eng, out_ap, in_ap, reads=(), writes=(), sem=None):
            P.emit(eng, lambda e: e.dma_start(out=out_ap, in_=in_ap), reads=list(reads), writes=list(writes), dma=sem)

        dbg_n = [0]

        def dump(key, t, dst_key=None):
            if key in dbg_d:
                dst = dbg_d[key] if dst_key is None else dbg_d[key][dst_key]
                dbg_n[0] += 1
                res = []
                for r_ in t.res:
                    res += [n for n in P.res if n.startswith(r_)]
                pieces = [(t.ap, dst)] if len(t.ap.shape) == 2 else [(t.ap[:, j_, :], dst[:, j_, :]) for j_ in range(t.ap.shape[1])]
                for sap, dap in pieces:
                    stg = AR.take("dbgstg", list(sap.shape), F32)
                    P.emit(DVE, lambda e, sap=sap, stg=stg: e.tensor_copy(out=stg.ap, in_=sap), reads=res, writes=[stg.name + ":"])
                    dma(SP, dap, stg.ap, reads=[stg.name + ":"], sem=f"dbg{dbg_n[0] % 4}")
                    stg.free()

        cbf = AR.take("cbf", [128, NCB], BF16)
        cfc = AR.take("cf", [128, NCF], F32)
        colb = AR.take("cols", [128, NCOL], F32)
        browb = AR.take("brow", [128, 2048], BF16)
        lamb = AR.take("lam", [1, 256], F32, parts=1)
        misc = AR.take("misc", [128, 16], F32)
        dma(POOL, cbf.ap, cbf_d, writes=[cbf.name + ":"], sem="c_cbf")
        dma(SP, cfc.ap, cf_d, writes=[cfc.name + ":"], sem="c_cf")
        dma(SP, colb.ap, cols_d, writes=[colb.name + ":"], sem="c_cols")
        dma(POOL, browb.ap, brow_d.partition_broadcast(128), writes=[browb.name + ":"], sem="c_brow")
        dma(SP, lamb.ap, lam_d, writes=[lamb.name + ":"], sem="c_lam")
        CBT = cbf.t()
        ident_b = CBT[:, CB_ID:CB_ID + 128]
        ones_b = CBT[:, CB_ONE:CB_ONE + 128]
        onesD_b = CBT[:, CB_OD:CB_OD + 128]
        ones256_b = CBT[:, CB_O256:CB_O256 + 128]
        ones128_b = CBT[:, CB_O128:CB_O128 + 128]
        mask_b = CBT[:, CB_MASK:CB_MASK + 128]
        perm_b = CBT[:, CB_PERM:CB_PERM + 128]
        CFT = cfc.t()
        ident_f = CFT[:, CF_ID:CF_ID + 128]

        def sel_f(h):
            return CFT[0:4, CF_SEL + h * 128:CF_SEL + h * 128 + 128]

        ones_row_f = CFT[0:1, CF_SEL:CF_SEL + 128]
        ones_row_b = T(cbf.ap[0:1, CB_ONE:CB_ONE + 128], cbf.name + ":")
        COLT = colb.t()

        def col(name, parts=128):
            i = COLS[name]
            return COLT[0:parts, i:i + 1]

        MISC = misc.t()
        neglam = MISC[:, 0:1]
        nbf = MISC[0:4, 1:2]
        dagn = AR.take("dagn", [128, 8], F32)
        for j in range(8):
            ts(DVE, dagn.t()[:, j:j + 1], col(f"danw{j}"), 1.0 - LAM_INIT, ALU.mult)
        DAGN = dagn.t()
        lt = AR.take("lamtmp", [1, 192], F32, parts=1)
        LT = lt.t()
        LAMT = lamb.t()
        tt(DVE, LT[:, 0:64], LAMT[:, 0:64], LAMT[:, 64:128], ALU.mult)
        tt(DVE, LT[:, 64:128], LAMT[:, 128:192], LAMT[:, 192:256], ALU.mult)
        P.emit(DVE, lambda e: e.tensor_reduce(out=lt.ap[:, 128:129], in_=lt.ap[:, 0:64], axis=mybir.AxisListType.X,
                                              op=ALU.add), reads=LT.res, writes=LT.res)
        P.emit(DVE, lambda e: e.tensor_reduce(out=lt.ap[:, 129:130], in_=lt.ap[:, 64:128], axis=mybir.AxisListType.X,
                                              op=ALU.add), reads=LT.res, writes=LT.res)
        act(LT[:, 130:132], LT[:, 128:130], AF.Exp)
        tt(DVE, LT[:, 132:133], LT[:, 131:132], LT[:, 130:131], ALU.subtract)
        ts(DVE, LT[:, 133:134], LT[:, 132:133], -LAM_INIT, ALU.add)
        mm(PS[0][:, 0:2], ones_row_f, LT[:, 132:134])
        cp(DVE, neglam, PS[0][:, 1:2])
        ts(DVE, nbf, col("bf", 4), -1.0, ALU.mult)
        lt.free()

        diagb = AR.take("diag", [128, 32, 128], BF16)
        for h_ in range(4):
            for qi_, qk_ in enumerate("qk"):
                for tap_ in range(4):
                    ts(DVE, T(diagb.ap[:, (h_ * 2 + qi_) * 4 + tap_, :], diagb.name + ":"), ident_f,
                       col(f"cw{qk_}{h_}_{tap_}"), ALU.mult)
        DIAG = diagb.t()

        NSLOT = 3
        wslots = [AR.take(f"wslot{i}", [128, CHW], BF16) for i in range(NSLOT)]
        wstate = {"n": 0}

        def wload(key):
            i = wstate["n"] % NSLOT
            wstate["n"] += 1
            n = CHLEN[key]
            slot = wslots[i]
            res = f"{slot.name}:w"
            half = n // 2
            src = wpk_d[CHIDX[key]]
            P.emit(POOL, lambda e: e.dma_start(out=slot.ap[:, 0:n], in_=src[:, 0:n]), writes=[res], dma=f"ws{i}")
            return T(slot.ap[:, 0:n], res)

        def wview(wt, ncols, kc=KC):
            return [wt[:, k * ncols:(k + 1) * ncols] for k in range(kc)]

        for sq in range(NSEQ):
            xTb = AR.take("xTb", [128, KC, S], BF16, top=True)
            xsrc = xT_d[sq].rearrange("(kc p) s -> p kc s", p=128)
            for k in range(KC):
                P.emit(POOL, lambda e, k=k, xTb=xTb, xsrc=xsrc: e.dma_start(out=xTb.ap[:, k, :], in_=xsrc[:, k, :]),
                       writes=[f"{xTb.name}:{k}"], dma=f"x{k}")

            def xT(k, lo, hi):
                return T(xTb.ap[:, k, lo:hi], f"{xTb.name}:{k}")

            rows = {n: AR.take("row_" + n, [4, S], F32, parts=4) for n in ("i", "l", "a", "A", "w1", "w2")}
            negA = AR.take("negA", [4, S], F32, parts=4)
            wqr = AR.take("wqr", [4, S], F32, parts=4)
            flr = AR.take("flr", [4, S], F32, parts=4)
            aT = AR.take("aT", [128, NT, 4], F32)
            wsT = AR.take("wsT", [128, NT, 4], F32)
            decb = AR.take("decb", [128, 4, NT], F32)
            wifb = AR.take("wifb", [128, KC * 8], BF16)
            dma(POOL, wifb.ap, wif_d, writes=[wifb.name + ":"], sem="wif")
            R = {n: b.t() for n, b in rows.items()}
            for tb in range(TB):
                sl = slice(tb * 512, tb * 512 + 512)
                for g, ps in ((0, PS[6]), (1, PS[7])):
                    for k in range(KC):
                        mm(ps[0:4, :], T(wifb.ap[:, k * 8 + g * 4:k * 8 + g * 4 + 4], wifb.name + ":"),
                           xT(k, tb * 512, tb * 512 + 512), start=(k == 0), stop=(k == KC - 1))
                act(R["i"][:, sl], PS[6][0:4, :], AF.Identity, bias=col("bi", 4))
                act(R["l"][:, sl], PS[7][0:4, :], AF.Exp, bias=nbf, scale=-1.0)
            act(R["l"], R["l"], AF.Ln, bias=1.0)
            mset(DVE, R["w1"], 1.0)
            scan(R["w2"], R["w1"], R["l"], 0.0, ALU.mult, ALU.add)
            tt(DVE, R["a"], R["i"], R["w2"], ALU.add)
            scan(R["A"], R["a"], R["a"], 0.0, ALU.max, ALU.max)
            ts(DVE, negA.t(), R["A"], -1.0, ALU.mult)
            tt(DVE, flr.t(), R["w2"], R["A"], ALU.subtract)
            act(flr.t(), flr.t(), AF.Exp)
            A3 = rows["A"].ap.rearrange("p (c t) -> p c t", t=128)
            w13 = rows["w1"].ap.rearrange("p (c t) -> p c t", t=128)
            il3 = rows["i"].ap.rearrange("p (c t) -> p c t", t=128)
            P.emit(DVE, lambda e, w13=w13, A3=A3: e.tensor_copy(out=w13, in_=A3[:, :, 127:128].to_broadcast([4, NT, 128])),
                   reads=R["A"].res, writes=R["w1"].res)
            mset(DVE, T(il3[:, 0, :], R["i"].res), 0.0)
            if NT > 1:
                P.emit(DVE, lambda e, il3=il3, w13=w13: e.tensor_copy(out=il3[:, 1:NT, :], in_=w13[:, 0:NT - 1, :]),
                       reads=R["w1"].res, writes=R["i"].res)
            tt(DVE, wqr.t(), R["i"], R["A"], ALU.subtract)
            act(wqr.t(), wqr.t(), AF.Exp, bias=LNC)
            tt(DVE, R["l"], R["a"], R["w1"], ALU.subtract)
            act(R["l"], R["l"], AF.Exp)
            P.emit(DVE, lambda e, rows=rows, il3=il3, w13=w13: e.tensor_tensor(out=rows["w2"].ap[:, 0:NT], in0=il3[:, :, 0], in1=w13[:, :, 0],
                                                  op=ALU.subtract), reads=R["i"].res + R["w1"].res, writes=R["w2"].res)
            act(R["w2"][:, 0:NT], R["w2"][:, 0:NT], AF.Exp)
            for c in range(NT):
                tr(PS[6][:, c * 4:c * 4 + 4], R["a"][:, c * 128:c * 128 + 128], ident_f[0:4, 0:4])
                tr(PS[7][:, c * 4:c * 4 + 4], R["l"][:, c * 128:c * 128 + 128], ident_f[0:4, 0:4])
            ts(DVE, T(aT.ap.rearrange("p c h -> p (c h)"), aT.name + ":"), PS[6][:, 0:NT * 4], LNC, ALU.add)
            cp(DVE, T(wsT.ap.rearrange("p c h -> p (c h)"), wsT.name + ":"), PS[7][:, 0:NT * 4])
            for h in range(4):
                mm(PS[6][:, h * NT:(h + 1) * NT], sel_f(h), R["w2"][:, 0:NT])
            cp(DVE, T(decb.ap.rearrange("p h c -> p (h c)"), decb.name + ":"), PS[6][:, 0:4 * NT])
            dump("negA", negA.t())
            dump("flr", flr.t())
            dump("wqr", wqr.t())
            for b_ in rows.values():
                b_.free()
            wifb.free()

            hnA = AR.take("hnA", [128, 8, S], BF16)
            qT_b = AR.take("qT", [128, S], BF16)
            kT_b = AR.take("kT", [128, S], BF16)
            vml = AR.take("vml", [128, NT, 256], BF16)
            ogb = AR.take("og", [128, 2, S], BF16)
            xpad = {qk: [AR.take(f"xp{qk}{i}", [128, 516], BF16) for i in range(2)] for qk in "qk"}
            sgb = [AR.take(f"sg{i}", [128, 512], F32) for i in range(2)]
            Cst = AR.take("Cst", [128, 384], F32)
            Cbf = [AR.take(f"Cbf{i}", [128, 384], BF16) for i in range(3)]
            hbuf = [AR.take(f"hbuf{i}", [128, 2, 512], BF16) for i in range(2)]
            hsq = AR.take("hsq", [128, 2, 512], BF16)
            DTb = [AR.take(f"DT{i}", [128, 128], F32) for i in range(2)]
            Stb = [AR.take(f"St{i}", [128, 128], BF16) for i in range(2)]
            qsb = [AR.take(f"qs{i}", [128, 128], BF16) for i in range(2)]
            ksb = [AR.take(f"ks{i}", [128, 128], BF16) for i in range(2)]
            flb = [AR.take(f"fl{i}", [128, 128], F32) for i in range(2)]
            dnb = [AR.take(f"dn{i}", [128, 128], F32) for i in range(2)]
            lnm = AR.take("lnm", [128, 512], F32)
            lnv = AR.take("lnv", [128, 512], F32)
            lnr = AR.take("lnr", [128, 512], F32)
            lnt = [AR.take(f"lnt{i}", [128, 512], F32) for i in range(2)]
            for h in range(4):
                wqk = wview(wload(f"mlqk{h}"), 256)
                wv = wview(wload(f"mlv{h}"), 256)
                wo = wview(wload(f"mlo{h}"), 256)
                late_cv = []
                for qi, (qk, dstb) in enumerate((("q", qT_b), ("k", kT_b))):
                    for tb in range(TB):
                        ps = PS[6 + ((qi * TB + tb) % 2)]
                        for k in range(KC):
                            mm(ps, wqk[k][:, qi * 128:qi * 128 + 128], xT(k, tb * 512, tb * 512 + 512),
                               start=(k == 0), stop=(k == KC - 1))
                        xp = xpad[qk][tb % 2]
                        xpp = xpad[qk][(tb + 1) % 2]
                        XP = xp.t()
                        if tb == 0:
                            mset(DVE, XP[:, 0:4], 0.0)
                        else:
                            cp(DVE, XP[:, 0:4], xpp.t()[:, 512:516])
                        act(XP[:, 4:516], ps, AF.Identity, bias=col(f"b{qk}{h}"))

                        def conv_part(qi=qi, qk=qk, tb=tb, XP=XP, dstb=dstb):
                            pcv = PS[4 + ((qi * TB + tb) % 2)]
                            for tap in range(4):
                                mm(pcv, DIAG[:, ((h * 2 + qi) * 4 + tap), :], XP[:, 1 + tap:513 + tap],
                                   start=(tap == 0), stop=(tap == 3))
                            sg = sgb[tb % 2].t()
                            act(sg, pcv, AF.Sigmoid, bias=col(f"cb{qk}{h}"))
                            stt(T(dstb.ap[:, tb * 512:tb * 512 + 512], f"{dstb.name}:{tb}"), pcv, col(f"cb{qk}{h}"), sg,
                                ALU.add, ALU.mult)

                        while late_cv:
                            late_cv.pop(0)()
                        late_cv.append(conv_part)
                for tt_ in range(NT):
                    ps = PS[6 + (tt_ % 2)]
                    for k in range(KC):
                        mm(ps[:, 0:256], xT(k, tt_ * 128, tt_ * 128 + 128), wv[k], start=(k == 0), stop=False)
                    mm(ps[:, 0:256], ident_b, T(browb.ap[:, h * 256:h * 256 + 256], browb.name + ":"),
                       start=False, stop=True)
                    cp(ACT, T(vml.ap[:, tt_, :], f"{vml.name}:{tt_}"), ps[:, 0:256])
                    while late_cv:
                        late_cv.pop(0)()
                for b2 in range(2):
                    for tb in range(TB):
                        ps = PS[6 + ((b2 * TB + tb) % 2)]
                        for k in range(KC):
                            mm(ps, wo[k][:, b2 * 128:b2 * 128 + 128], xT(k, tb * 512, tb * 512 + 512),
                               start=(k == 0), stop=(k == KC - 1))
                        act(T(ogb.ap[:, b2, tb * 512:tb * 512 + 512], f"{ogb.name}:{b2}_{tb}"), ps, AF.Sigmoid,
                            bias=col(f"bo{h}_{b2}"))
                if h == 0:
                    dump("mq0", qT_b.t())
                    dump("mk0", kT_b.t())
                CS = Cst.t()
                CBS = [Cbf[0].t(), Cbf[1].t(), Cbf[2].t()]

                def stageA(c):
                    i2 = c % 2
                    cs = slice(c * 128, c * 128 + 128)
                    tb = c // 4
                    pa = PS[0 + i2]
                    po = PS[2 + i2]
                    qc = T(qT_b.ap[:, cs], f"{qT_b.name}:{tb}")
                    kc_ = T(kT_b.ap[:, cs], f"{kT_b.name}:{tb}")
                    mm(pa[:, 0:128], ident_b, mask_b, start=True, stop=False)
                    mm(pa[:, 0:128], sel_f(h), negA.t()[:, cs], start=False, stop=True)
                    mm(pa[:, 256:384], sel_f(h), wqr.t()[:, cs])
                    mm(po[:, 384:512], sel_f(h), flr.t()[:, cs])
                    mm(pa[:, 128:256], kc_, qc)
                    ptb = T(pbank[5][:].bitcast(BF16)[:, 0:128], "ps5")
                    tr(ptb, kc_, ident_b)
                    DT = DTb[i2].t()
                    act(DT, pa[:, 0:128], AF.Exp, bias=T(aT.ap[:, c, h:h + 1], aT.name + ":"))
                    act(ksb[i2].t(), ptb, AF.Copy, scale=T(wsT.ap[:, c, h:h + 1], wsT.name + ":"))
                    tt(DVE, Stb[i2].t(), pa[:, 128:256], DT, ALU.mult)
                    tt(DVE, qsb[i2].t(), pa[:, 256:384], qc, ALU.mult)

                def stageU(c):
                    if c >= NT - 1:
                        return
                    pc = PS[4]
                    ks = ksb[c % 2].t()
                    vc = T(vml.ap[:, c, :], f"{vml.name}:{c}")
                    mm(pc[:, 0:256], ks, vc)
                    mm(pc[:, 256:384], ks, ones_b)
                    if c == 0:
                        cp(DVE, CS, pc[:, 0:384])
                    else:
                        stt(CS, CS, T(decb.ap[:, h, c:c + 1], decb.name + ":"), pc[:, 0:384], ALU.mult, ALU.add)
                    cp(ACT, CBS[(c + 1) % 3], CS)

                def stageB(c):
                    i2 = c % 2
                    tb = c // 4
                    po = PS[2 + i2]
                    St, qs = Stb[i2].t(), qsb[i2].t()
                    CB_ = CBS[c % 3]
                    vc = T(vml.ap[:, c, :], f"{vml.name}:{c}")
                    for b3 in range(3):
                        lh = vc[:, b3 * 128:b3 * 128 + 128] if b3 < 2 else ones_b
                        mm(po[:, b3 * 128:b3 * 128 + 128], lh, St, start=True, stop=(c == 0))
                        if c > 0:
                            mm(po[:, b3 * 128:b3 * 128 + 128], CB_[:, b3 * 128:b3 * 128 + 128], qs, start=False, stop=True)
                    dn_ = dnb[i2].t()
                    act(dn_, po[:, 256:384], AF.Abs)
                    tt(DVE, dn_, dn_, po[:, 384:512], ALU.max)
                    act(dn_, dn_, AF.Ln)
                    act(dn_, dn_, AF.Exp, scale=-1.0)
                    HB = hbuf[tb % 2].t("c%d" % (c % 4))
                    for b3 in range(2):
                        tt(DVE, HB[:, b3, (c % 4) * 128:(c % 4) * 128 + 128], po[:, b3 * 128:b3 * 128 + 128], dn_, ALU.mult)

                def stageLN(tb):
                    hb_ = hbuf[tb % 2]
                    HALL = T(hb_.ap, [f"{hb_.name}:c{x}" for x in range(4)])
                    act(hsq.t(), HALL, AF.Square)
                    pm, pq = PS[6], PS[7]
                    for b3 in range(2):
                        mm(pm, ones256_b, HALL[:, b3, :], start=(b3 == 0), stop=(b3 == 1))
                    for b3 in range(2):
                        mm(pq, ones256_b, hsq.t()[:, b3, :], start=(b3 == 0), stop=(b3 == 1))
                    act(lnm.t(), pm, AF.Copy)
                    act(lnv.t(), pm, AF.Square)
                    tt(DVE, lnv.t(), pq, lnv.t(), ALU.subtract)
                    act(lnr.t(), lnv.t(), AF.Ln, bias=EPS)
                    act(lnr.t(), lnr.t(), AF.Exp, scale=-0.5)
                    for b3 in range(2):
                        lt_ = lnt[b3].t()
                        tt(DVE, lt_, HALL[:, b3, :], lnm.t(), ALU.subtract)
                        tt(DVE, lt_, lt_, lnr.t(), ALU.mult)
                        stt(T(hnA.ap[:, 2 * h + b3, tb * 512:tb * 512 + 512], f"{hnA.name}:{2 * h + b3}_{tb}"), lt_,
                            col(f"mlnw{2 * h + b3}"),
                            T(ogb.ap[:, b3, tb * 512:tb * 512 + 512], f"{ogb.name}:{b3}_{tb}"), ALU.mult, ALU.mult)

                stageA(0)
                ln_pending = []
                for c in range(NT):
                    if c + 1 < NT:
                        stageA(c + 1)
                    stageB(c)
                    stageU(c)
                    if ln_pending and c >= ln_pending[0][1]:
                        stageLN(ln_pending.pop(0)[0])
                    if c % 4 == 3:
                        ln_pending.append((c // 4, c + 2))
                for tb_, _ in ln_pending:
                    stageLN(tb_)
            dump("hnA", T(hnA.ap, hnA.name + ":"))
            for b_ in ([qT_b, kT_b, vml, ogb, Cst, hsq, lnm, lnv, lnr, negA, wqr, flr, aT, wsT, decb]
                       + Cbf + hbuf + xpad["q"] + xpad["k"] + sgb + DTb + Stb + qsb + ksb + flb + dnb + lnt):
                b_.free()

            cosb = AR.take("cosT", [128, S], F32)
            sinb = AR.take("sinT", [128, S], F32)
            posi = AR.take("posi", [128, S], I32)
            ang = AR.take("ang", [128, S], F32)
            tmpa = AR.take("tmpa", [128, S], F32)
            tmpb = AR.take("tmpb", [128, S], F32)
            dma(SP, posi.ap, pos_d[sq:sq + 1, :].partition_broadcast(128), writes=[posi.name + ":"], sem="pos")
            cp(DVE, ang.t(), posi.t())
            ts(DVE, ang.t(), ang.t(), col("invf"), ALU.mult)
            for tab, shift, scale_ap in ((sinb, 0.0, col("sgn")), (cosb, math.pi / 2, None)):
                src_ = ang.t()
                if shift != 0.0:
                    ts(DVE, tmpb.t(), ang.t(), shift, ALU.add)
                    src_ = tmpb.t()
                ts(DVE, tmpa.t(), src_, 1.0 / TWO_PI, ALU.mult, MAGIC, ALU.add)
                ts(DVE, tmpa.t(), tmpa.t(), MAGIC, ALU.subtract)
                stt(tmpa.t(), tmpa.t(), -TWO_PI, src_, ALU.mult, ALU.add)
                ts(DVE, tmpa.t(), tmpa.t(), 3.14159, ALU.min, -3.14159, ALU.max)
                act(tab.t(), tmpa.t(), AF.Sin, scale=scale_ap)
            for b_ in (posi, ang, tmpa, tmpb):
                b_.free()
            dump("cos", cosb.t())
            dump("sin", sinb.t())

            oB = AR.take("oB", [128, 8, S], BF16, top=True)
            qz = [AR.take(f"qz{i}", [128, S], BF16) for i in range(2)]
            mset(DVE, T(qz[0].ap[64:128, :], f"{qz[0].name}:z"), 0.0)
            mset(DVE, T(qz[1].ap[0:64, :], f"{qz[1].name}:z"), 0.0)
            kr = AR.take("kr", [128, S], BF16)
            vda = AR.take("vda", [128, NT, 128], BF16)
            raw = [AR.take(f"raw{i}", [128, 512], BF16) for i in range(2)]
            t1 = [AR.take(f"t1_{i}", [128, 512], F32) for i in range(2)]
            u1 = [AR.take(f"u1_{i}", [128, 512], F32) for i in range(2)]
            ET = [AR.take(f"ET{i}", [128, 512], BF16) for i in range(6)]
            o1b = AR.take("o1", [128, 512], F32)
            odbs = [AR.take(f"od{i}", [128, 512], F32) for i in range(2)]
            epi = []
            nblk = [0]
            rcb = [AR.take(f"rc{i}", [128, 512], F32) for i in range(2)]
            osqb = AR.take("osq", [128, 512], BF16)
            rsb = AR.take("rs", [128, 512], F32)
            cnt = {"raw": 0, "et": 0, "sc": 0, "acc": 0}
            for hd in range(8):
                wt = wload(f"da{hd}")
                wk = wview(wt, 384)
                late = []
                for which, dstb, bname in ((0, None, f"dbq{hd}"), (1, kr, f"dbk{hd}")):
                    for tb in range(TB):
                        ps = PS[6 + (cnt["raw"] % 2)]
                        for k in range(KC):
                            mm(ps, wk[k][:, which * 128:which * 128 + 128], xT(k, tb * 512, tb * 512 + 512),
                               start=(k == 0), stop=(k == KC - 1))
                        i2 = cnt["raw"] % 2
                        cnt["raw"] += 1
                        rw = raw[i2].t()
                        act(rw, ps, AF.Identity, bias=col(bname))

                        def rope_part(i2=i2, rw=rw, tb=tb, dstb=dstb):
                            ps2 = PS[4 + i2]
                            mm(ps2, perm_b, rw)
                            tt(DVE, t1[i2].t(), rw, cosb.t()[:, tb * 512:tb * 512 + 512], ALU.mult)
                            tt(DVE, u1[i2].t(), ps2, sinb.t()[:, tb * 512:tb * 512 + 512], ALU.mult)
                            tsl = slice(tb * 512, tb * 512 + 512)
                            if dstb is None:
                                tt(DVE, T(qz[0].ap[0:64, tsl], f"{qz[0].name}:{tb}"), t1[i2].t()[0:64], u1[i2].t()[0:64],
                                   ALU.add)
                                tt(DVE, T(qz[1].ap[64:128, tsl], f"{qz[1].name}:{tb}"), t1[i2].t()[64:128],
                                   u1[i2].t()[64:128], ALU.add)
                            else:
                                tt(DVE, T(dstb.ap[:, tsl], f"{dstb.name}:{tb}"), t1[i2].t(), u1[i2].t(), ALU.add)

                        while late:
                            late.pop(0)()
                        late.append(rope_part)
                for tt_ in range(NT):
                    ps = PS[6 + (tt_ % 2)]
                    for k in range(KC):
                        mm(ps[:, 0:128], xT(k, tt_ * 128, tt_ * 128 + 128), wk[k][:, 256:384], start=(k == 0), stop=False)
                    mm(ps[:, 0:128], ident_b, T(browb.ap[:, 1024 + hd * 128:1024 + hd * 128 + 128], browb.name + ":"),
                       start=False, stop=True)
                    cp(ACT, T(vda.ap[:, tt_, :], f"{vda.name}:{tt_}"), ps[:, 0:128])
                    while late:
                        late.pop(0)()
                if hd == 0:
                    dump("kr0", kr.t())
                for qb in range(TB):
                    q0 = qb * 512
                    for c in range(2):
                        pr = slice(c * 64, c * 64 + 64)
                        Ob = PS[0 + 2 * (cnt["acc"] % 2)]
                        Db = PS[1 + 2 * (cnt["acc"] % 2)]
                        cnt["acc"] += 1
                        tiles = []
                        for j in range(4 * qb + 4):
                            r = j - 4 * qb
                            off = 128 * r if r > 0 else 0
                            tiles.append((j, off, 512 - off, r >= 0))

                        def qk(tile):
                            j, off, w, diag = tile
                            sc = PS[4 + (cnt["sc"] % 3)]
                            et = ET[cnt["et"] % 6]
                            cnt["sc"] += 1
                            cnt["et"] += 1
                            qT_ = T(qz[c].ap[:, q0 + off:q0 + 512], [f"{qz[c].name}:{qb}", f"{qz[c].name}:z"])
                            kT_ = T(kr.ap[:, j * 128:j * 128 + 128], f"{kr.name}:{j // 4}")
                            mm(sc[:, 0:w], kT_, qT_, start=True, stop=not diag)
                            if diag:
                                mm(sc[:, 0:128], ident_b, mask_b, start=False, stop=True)
                            e_ = et.t()[:, 0:w]
                            act(e_, sc[:, 0:w], AF.Exp, scale=0.125)
                            return e_

                        def pv(tile, e_, first, last):
                            j, off, w, diag = tile
                            v_ = T(vda.ap[:, j, :], f"{vda.name}:{j}")
                            mm(Ob[:, off:512], v_, e_, start=first, stop=last)
                            mm(Db[:, off:512], ones_b, e_, start=first, stop=last)

                        pend = []
                        issued = 0
                        ntl = len(tiles)
                        while issued < min(2, ntl):
                            pend.append(qk(tiles[issued]))
                            issued += 1
                        for ti in range(ntl):
                            if issued < ntl:
                                pend.append(qk(tiles[issued]))
                                issued += 1
                            pv(tiles[ti], pend.pop(0), ti == 0, ti == ntl - 1)
                            if c == 0 and ti == 2:
                                while epi:
                                    epi.pop(0)()
                        rc = rcb[c].t()
                        act(rc, Db, AF.Ln)
                        act(rc, rc, AF.Exp, scale=-1.0)
                        if c == 0:
                            tt(DVE, o1b.t(), Ob, rc, ALU.mult)
                            while epi:
                                epi.pop(0)()
                        else:
                            odb = odbs[nblk[0] % 2]
                            nblk[0] += 1
                            tt(DVE, odb.t(), Ob, rc, ALU.mult)
                            stt(odb.t(), odb.t(), neglam, o1b.t(), ALU.mult, ALU.add)

                            tt(DVE, osqb.t(), odb.t(), odb.t(), ALU.mult)

                            def epilogue(odb=odb, hd=hd, qb=qb, q0=q0):
                                psn = PS[7]
                                mm(psn, ones128_b, osqb.t())
                                act(rsb.t(), psn, AF.Ln, bias=EPS)
                                act(rsb.t(), rsb.t(), AF.Exp, scale=-0.5)
                                stt(T(oB.ap[:, hd, q0:q0 + 512], f"{oB.name}:{hd}_{qb}"), odb.t(), DAGN[:, hd:hd + 1],
                                    rsb.t(), ALU.mult, ALU.mult)

                            epi.append(epilogue)
            while epi:
                epi.pop(0)()
            for b_ in qz + [kr, vda, o1b, osqb, rsb, cosb, sinb] + odbs + raw + t1 + u1 + ET + rcb:
                b_.free()

            def layer_norm(zj_fn, l, out_bf, out_f32, tb, produce=None, cast_eng=ACT):
                zb16 = AR.take("zb16", [128, 8, 512], BF16)
                zsq = AR.take("zsq", [128, 8, 512], BF16)
                mean = AR.take("mean", [128, 512], F32)
                var = AR.take("var", [128, 512], F32)
                rstd = AR.take("rstd", [128, 512], F32)
                pm, pq = PS[6], PS[7]
                pend_st = []
                for j in range(8):
                    if produce is not None:
                        produce(j)
                    zj = zj_fn(j)
                    zbj = T(zb16.ap[:, j, :], f"{zb16.name}:{j}")
                    zsj = T(zsq.ap[:, j, :], f"{zsq.name}:{j}")
                    cp(cast_eng, zbj, zj)
                    act(zsj, zj, AF.Square)
                    pend_st.append((j, zbj, zsj))
                    if len(pend_st) > 3:
                        j_, zb_, zs_ = pend_st.pop(0)
                        mm(pm, onesD_b, zb_, start=(j_ == 0), stop=(j_ == 7))
                        mm(pq, onesD_b, zs_, start=(j_ == 0), stop=(j_ == 7))
                for j_, zb_, zs_ in pend_st:
                    mm(pm, onesD_b, zb_, start=(j_ == 0), stop=(j_ == 7))
                    mm(pq, onesD_b, zs_, start=(j_ == 0), stop=(j_ == 7))
                act(mean.t(), pm, AF.Copy)
                act(var.t(), pm, AF.Square)
                tt(DVE, var.t(), pq, var.t(), ALU.subtract)
                act(rstd.t(), var.t(), AF.Ln, bias=EPS)
                act(rstd.t(), rstd.t(), AF.Exp, scale=-0.5)
                tmps = [AR.take(f"lntmp{i}", [128, 512], F32) for i in range(3)]
                for j in range(8):
                    tmp = tmps[j % 3]
                    tt(DVE, tmp.t(), zj_fn(j), mean.t(), ALU.subtract)
                    tt(DVE, tmp.t(), tmp.t(), rstd.t(), ALU.mult)
                    if out_bf is not None:
                        act(T(out_bf.ap[:, j, tb * 512:tb * 512 + 512], f"{out_bf.name}:{j}_{tb}"), tmp.t(), AF.Identity,
                            bias=col(f"ln{l}b{j}"), scale=col(f"ln{l}g{j}"))
                    if out_f32 is not None:
                        out_f32(j, tmp)
                for b_ in [zb16, zsq, mean, var, rstd] + tmps:
                    b_.free()

            mixin = AR.take("mixin", [128, 8, S], BF16)
            sab = [AR.take(f"sa{i}", [128, 512], F32) for i in range(2)]
            sbb = [AR.take(f"sb{i}", [128, 512], F32) for i in range(2)]
            tab = [AR.take(f"ta{i}", [128, 512], F32) for i in range(2)]
            tbb = [AR.take(f"tb{i}", [128, 512], F32) for i in range(2)]
            for j in range(8):
                wj = wview(wload(f"p4_{j}"), 512)
                for tb in range(TB):
                    i2 = tb % 2
                    pss = [PS[4 * i2 + q] for q in range(4)]
                    srcs = [lambda k: T(hnA.ap[:, k, tb * 512:tb * 512 + 512], f"{hnA.name}:{k}_{tb}"),
                            lambda k: T(oB.ap[:, k, tb * 512:tb * 512 + 512], f"{oB.name}:{k}_{tb}"),
                            lambda k: xT(k, tb * 512, tb * 512 + 512), lambda k: xT(k, tb * 512, tb * 512 + 512)]
                    for q in range(4):
                        for k in range(KC):
                            mm(pss[q], wj[k][:, q * 128:q * 128 + 128], srcs[q](k), start=(k == 0), stop=(k == KC - 1))
                    act(sab[i2].t(), pss[2], AF.Sigmoid, bias=col(f"bga{j}"))
                    act(sbb[i2].t(), pss[3], AF.Sigmoid, bias=col(f"bgb{j}"))
                    tt(DVE, tab[i2].t(), pss[0], sab[i2].t(), ALU.mult)
                    tt(DVE, tbb[i2].t(), pss[1], sbb[i2].t(), ALU.mult)
                    tt(DVE, T(mixin.ap[:, j, tb * 512:tb * 512 + 512], f"{mixin.name}:{j}_{tb}"), tab[i2].t(),
                       tbb[i2].t(), ALU.add)
            for b_ in [hnA, oB, xTb] + sab + sbb + tab + tbb:
                b_.free()
            dump("mixin", T(mixin.ap, mixin.name + ":"))

            x1b = AR.take("x1b", [128, 8, S], BF16, top=True)
            wm = [wview(wload(f"mix{c}"), 512) for c in range(2)]
            zbufs5 = [AR.take(f"z{i}", [128, 8, 512], F32) for i in range(2)]
            xress5 = [AR.take(f"xres{i}", [128, 8, 512], F32) for i in range(2)]
            for tb in range(TB):
                zbuf = zbufs5[tb % 2]
                xres = xress5[tb % 2]
                dma(SP, xres.ap, xT_d[sq].rearrange("(kc p) s -> p kc s", p=128)[:, :, tb * 512:tb * 512 + 512],
                    writes=[xres.name + ":"], sem=f"xres{tb % 2}")
                def prod5(j, tb=tb, zbuf=zbuf, xres=xres):
                    ps = PS[j % 4]
                    for k in range(KC):
                        mm(ps, wm[j // 4][k][:, (j % 4) * 128:(j % 4) * 128 + 128],
                           T(mixin.ap[:, k, tb * 512:tb * 512 + 512], f"{mixin.name}:{k}_{tb}"),
                           start=(k == 0), stop=(k == KC - 1))
                    stt(T(zbuf.ap[:, j, :], f"{zbuf.name}:{j}"), xres.t()[:, j, :], ALPHA, ps, ALU.mult, ALU.add)

                layer_norm(lambda j, zbuf=zbuf: T(zbuf.ap[:, j, :], f"{zbuf.name}:{j}"), 1, x1b, None, tb, produce=prod5)
            for b_ in zbufs5 + xress5:
                b_.free()
            mixin.free()
            dump("x1", T(x1b.ap, x1b.name + ":"))

            memb = AR.take("memb", [128, KC, MEM], BF16)
            msrc = memT_d[sq].rearrange("(kc p) s -> p kc s", p=128)
            P.emit(POOL, lambda e, memb=memb, msrc=msrc: e.dma_start(out=memb.ap, in_=msrc), writes=[memb.name + ":"], dma="mem")
            xkT = AR.take("xkT", [128, 8, MEM], BF16)
            xvb = AR.take("xvb", [128, 2, D], BF16)
            MB = memb.t()
            for c in range(2):
                wc = wview(wload(f"xk{c}"), 512)
                for jj in range(4):
                    ps = PS[jj % 4]
                    for k in range(KC):
                        mm(ps[:, 0:MEM], wc[k][:, jj * 128:jj * 128 + 128], MB[:, k, :], start=(k == 0), stop=(k == KC - 1))
                    cp(ACT, T(xkT.ap[:, c * 4 + jj, :], f"{xkT.name}:{c * 4 + jj}"), ps[:, 0:MEM])
            for c in range(2):
                wc = wview(wload(f"xv{c}"), 512)
                for m in range(2):
                    ps = PS[4 + m]
                    for k in range(KC):
                        mm(ps, MB[:, k, m * 128:m * 128 + 128], wc[k], start=(k == 0), stop=(k == KC - 1))
                    cp(ACT, T(xvb.ap[:, m, c * 512:c * 512 + 512], f"{xvb.name}:{m}_{c}"), ps)
            xqT = AR.take("xqT", [128, 8, S], BF16)
            for c in range(2):
                wc = wview(wload(f"xq{c}"), 512)
                for jj in range(4):
                    for tb in range(TB):
                        ps = PS[(jj * TB + tb) % 4]
                        for k in range(KC):
                            mm(ps, wc[k][:, jj * 128:jj * 128 + 128],
                               T(x1b.ap[:, k, tb * 512:tb * 512 + 512], f"{x1b.name}:{k}_{tb}"),
                               start=(k == 0), stop=(k == KC - 1))
                        cp(ACT if (jj + tb) % 2 else DVE,
                           T(xqT.ap[:, c * 4 + jj, tb * 512:tb * 512 + 512], f"{xqT.name}:{c * 4 + jj}_{tb}"), ps)
            xoT = AR.take("xoT", [128, 8, S], BF16, top=True)
            EX = [AR.take(f"EX{i}", [128, 2, 512], BF16) for i in range(2)]
            rcx = [AR.take(f"rcx{i}", [128, 512], F32) for i in range(2)]
            n_ = 0
            iters = [(hx, tb) for hx in range(4) for tb in range(TB)]

            def xa_scores(n):
                hx, tb = iters[n]
                i2 = n % 2
                for m in range(2):
                    ps = PS[4 + 2 * i2 + m]
                    for kk in range(2):
                        mm(ps, T(xkT.ap[:, hx * 2 + kk, m * 128:m * 128 + 128], f"{xkT.name}:{hx * 2 + kk}"),
                           T(xqT.ap[:, hx * 2 + kk, tb * 512:tb * 512 + 512], f"{xqT.name}:{hx * 2 + kk}_{tb}"),
                           start=(kk == 0), stop=(kk == 1))
                    act(T(EX[i2].ap[:, m, :], f"{EX[i2].name}:{m}"), ps, AF.Exp, scale=1.0 / 16.0)

            def xa_rest(n):
                hx, tb = iters[n]
                i2 = n % 2
                pd = PS[0 + 3 * i2]
                for m in range(2):
                    mm(pd, ones_b, T(EX[i2].ap[:, m, :], f"{EX[i2].name}:{m}"), start=(m == 0), stop=(m == 1))
                act(rcx[i2].t(), pd, AF.Ln)
                act(rcx[i2].t(), rcx[i2].t(), AF.Exp, scale=-1.0)
                for b2 in range(2):
                    po_ = PS[1 + b2]
                    for m in range(2):
                        mm(po_, T(xvb.ap[:, m, hx * 256 + b2 * 128:hx * 256 + b2 * 128 + 128], f"{xvb.name}:{m}_{hx // 2}"),
                           T(EX[i2].ap[:, m, :], f"{EX[i2].name}:{m}"), start=(m == 0), stop=(m == 1))
                    tt(DVE, T(xoT.ap[:, hx * 2 + b2, tb * 512:tb * 512 + 512], f"{xoT.name}:{hx * 2 + b2}_{tb}"),
                       po_, rcx[i2].t(), ALU.mult)

            xa_scores(0)
            for n in range(len(iters)):
                if n + 1 < len(iters):
                    xa_scores(n + 1)
                xa_rest(n)
            for b_ in [memb, xkT, xvb, xqT] + EX + rcx:
                b_.free()
            dump("xo", T(xoT.ap, xoT.name + ":"))

            x2b = AR.take("x2b", [128, 8, S], BF16)
            wx = [wview(wload(f"xo{c}"), 512) for c in range(2)]
            zbufs7 = [AR.take(f"z{i}", [128, 8, 512], F32) for i in range(2)]
            for tb in range(TB):
                zbuf = zbufs7[tb % 2]

                def prod7(j, tb=tb, zbuf=zbuf):
                    ps = PS[j % 4]
                    for k in range(KC):
                        mm(ps, wx[j // 4][k][:, (j % 4) * 128:(j % 4) * 128 + 128],
                           T(xoT.ap[:, k, tb * 512:tb * 512 + 512], f"{xoT.name}:{k}_{tb}"),
                           start=(k == 0), stop=(k == KC - 1))
                    stt(T(zbuf.ap[:, j, :], f"{zbuf.name}:{j}"),
                        T(x1b.ap[:, j, tb * 512:tb * 512 + 512], f"{x1b.name}:{j}_{tb}"), ALPHA, ps, ALU.mult, ALU.add)

                layer_norm(lambda j, zbuf=zbuf: T(zbuf.ap[:, j, :], f"{zbuf.name}:{j}"), 2, x2b, None, tb, produce=prod7)
            for b_ in zbufs7:
                b_.free()
            xoT.free()
            x1b.free()
            dump("x2", T(x2b.ap, x2b.name + ":"))

            zx = AR.take("zx", [128, (34 * S) // 4], F32, top=True)
            x2c_ap = zx.ap[:, (18 * S) // 4:(34 * S) // 4].bitcast(BF16).rearrange("p (a b) -> p a b", b=S)
            zall_ap = zx.ap[:, 0:8 * S].rearrange("p (a b) -> p a b", b=S)

            def x2c(k, tb):
                return T(x2c_ap[:, k, tb * 512:tb * 512 + 512], f"{zx.name}:x{k}_{tb}")

            for k in range(KC):
                for tb in range(TB):
                    cp(DVE if (k + tb) % 2 else ACT, x2c(k, tb),
                       T(x2b.ap[:, k, tb * 512:tb * 512 + 512], f"{x2b.name}:{k}_{tb}"))
            x2b.free()
            hid = AR.take("hid", [128, NF, S], BF16)
            sgf = [AR.take(f"sgf{i}", [128, 512], F32) for i in range(2)]
            tf = [AR.take(f"tf{i}", [128, 512], F32) for i in range(2)]
            n_ = 0
            for c in range(NF // 2):
                wc = wview(wload(f"gu{c}"), 512)
                for ff in range(2):
                    f = 2 * c + ff
                    for tb in range(TB):
                        i2 = n_ % 2
                        n_ += 1
                        pg, pu = PS[2 * i2], PS[2 * i2 + 1]
                        for q, ps in ((0, pg), (1, pu)):
                            for k in range(KC):
                                mm(ps, wc[k][:, (2 * ff + q) * 128:(2 * ff + q) * 128 + 128], x2c(k, tb),
                                   start=(k == 0), stop=(k == KC - 1))
                        act(sgf[i2].t(), pg, AF.Sigmoid)
                        tt(DVE, tf[i2].t(), pg, sgf[i2].t(), ALU.mult)
                        tt(DVE, T(hid.ap[:, f, tb * 512:tb * 512 + 512], f"{hid.name}:{f}_{tb}"), pu, tf[i2].t(), ALU.mult)
            for b_ in sgf + tf:
                b_.free()
            for j in range(8):
                wd = wview(wload(f"dn{j}"), 128, kc=NF)
                for jp in (2 * j - 9, 2 * j - 8):
                    if 0 <= jp < 8:
                        summ = P.summary(f"{zx.name}:x{jp}_")
                        for tb in range(TB):
                            P.add_readers(f"{zx.name}:z{j}_{tb}", summ)
                for tb in range(TB):
                    ps = PS[(j * TB + tb) % 4]
                    for f in range(NF):
                        mm(ps, wd[f], T(hid.ap[:, f, tb * 512:tb * 512 + 512], f"{hid.name}:{f}_{tb}"),
                           start=(f == 0), stop=(f == NF - 1))
                    stt(T(zall_ap[:, j, tb * 512:tb * 512 + 512], f"{zx.name}:z{j}_{tb}"), x2c(j, tb), ALPHA, ps,
                        ALU.mult, ALU.add)
            hid.free()
            osrc = outT_d[sq].rearrange("(kc p) s -> p kc s", p=128)
            yos = [AR.take(f"yo{i}", [128, 512], F32) for i in range(4)]
            for tb in range(TB):
                def outcb(j, tmp, tb=tb):
                    yo = yos[j % 4]
                    act(yo.t(), tmp.t(), AF.Identity, bias=col(f"ln3b{j}"), scale=col(f"ln3g{j}"))
                    dma(SP, osrc[:, j, tb * 512:tb * 512 + 512], yo.ap, reads=yo.t().res, sem=f"out{j}")

                layer_norm(lambda j, tb=tb: T(zall_ap[:, j, tb * 512:tb * 512 + 512], f"{zx.name}:z{j}_{tb}"),
                           3, None, outcb, tb, cast_eng=DVE)
            zx.free()
            for b_ in yos:
                b_.free()

        P.final_wait(SP, ["yo"])
        P.replay()
        print(f"[build] arena peak {AR.peak} / {ARB}; instr counts {P.cnt}")
    return nc


_CACHE = {}


def kernel(**inputs):
    x = np.asarray(inputs["x"], np.float32)
    mem = np.asarray(inputs["mem"], np.float32)
    pos = np.asarray(inputs["positions"], np.int32)
    B, S, _ = x.shape
    n_cores = 8
    nseq = B // n_cores
    pk = pack_host({k: np.asarray(v) for k, v in inputs.items()})
    xT = np.ascontiguousarray(x.transpose(0, 2, 1))
    memT = np.ascontiguousarray(mem.transpose(0, 2, 1))
    key = (S, nseq)
    if key not in _CACHE:
        _CACHE[key] = build(S, nseq)
    nc = _CACHE[key]
    in_maps = []
    for c in range(n_cores):
        sl = slice(c * nseq, (c + 1) * nseq)
        m = dict(xT=xT[sl], memT=memT[sl], pos=pos[sl])
        m.update(pk)
        in_maps.append(m)
    res = run_bass_kernel_spmd(nc, in_maps, core_ids=list(range(n_cores)))
    outT = np.concatenate([r["outT"] for r in res.results], axis=0)
    return np.ascontiguousarray(outT.transpose(0, 2, 1))
```

```python
import math
from contextlib import ExitStack
import numpy as np
import concourse.bass as bass
import concourse.mybir as mybir
from concourse.bass_utils import run_bass_kernel_spmd

F32 = mybir.dt.float32
BF16 = mybir.dt.bfloat16
I32 = mybir.dt.int32
AF = mybir.ActivationFunctionType
ALU = mybir.AluOpType
PE, ACT, DVE, POOL, SP = "tensor", "scalar", "vector", "gpsimd", "sync"
ENGS = [PE, ACT, DVE, POOL, SP]

D = 1024
KC = 8
MEM = 256
FH = 2816
NF = FH // 128
ALPHA = 2.0 ** 0.25
EPS = 1e-5
LAM_INIT = 0.8 - 0.6 * math.exp(0.0)
MAGIC = 12582912.0
LNC = math.log(128.0 ** -0.5)
TWO_PI = 2.0 * math.pi
CHW = 4096


class T:
    def __init__(self, ap, res):
        self.ap = ap
        self.res = list(res) if isinstance(res, (list, tuple)) else [res]

    def __getitem__(self, k):
        return T(self.ap[k], self.res)


class Prog:
    def __init__(self, nc, stack):
        self.nc = nc
        self.stack = stack
        self.ops = {e: [] for e in ENGS}
        self.cnt = {e: 0 for e in ENGS}
        self.esem = {e: stack.enter_context(nc.semaphore("s_" + e)) for e in ENGS}
        self.res = {}
        self.known = {e: {} for e in ENGS}
        self.dsem = {}
        self.dcnt = {}
        self.prefix_init = []

    def _r(self, name):
        r = self.res.get(name)
        if r is None:
            r = {"w": None, "r": {}}
            for pre, summ in self.prefix_init:
                if name.startswith(pre):
                    for src, ev in summ.items():
                        o = r["r"].get(src)
                        if o is None or o[2] < ev[2]:
                            r["r"][src] = ev
            self.res[name] = r
        return r

    def summary(self, prefix):
        out = {}
        for n, r in self.res.items():
            if n.startswith(prefix):
                evs = list(r["r"].values())
                if r["w"] is not None:
                    evs.append(r["w"])
                for ev in evs:
                    o = out.get(ev[0])
                    if o is None or o[2] < ev[2]:
                        out[ev[0]] = ev
        return out

    def add_readers(self, name, summ):
        r = self._r(name)
        for src, ev in summ.items():
            o = r["r"].get(src)
            if o is None or o[2] < ev[2]:
                r["r"][src] = ev

    def emit(self, eng, fn, reads=(), writes=(), dma=None):
        deps = []
        for n in reads:
            r = self._r(n)
            w = r["w"]
            if w is not None:
                deps.append((w, "raw"))
            if n.startswith("ps") and dma is None:
                for src_, ev_ in r["r"].items():
                    if src_ != eng:
                        deps.append((ev_, "psum"))
        for n in writes:
            r = self._r(n)
            if r["w"] is not None:
                deps.append((r["w"], "waw"))
            for ev in r["r"].values():
                deps.append((ev, "war"))
        if dma is None:
            self.cnt[eng] += 1
            ev = (eng, self.esem[eng], self.cnt[eng])
        else:
            if dma not in self.dsem:
                self.dsem[dma] = self.stack.enter_context(self.nc.semaphore("d_" + dma))
                self.dcnt[dma] = 0
            if self.dcnt[dma] > 0:
                deps.append((("dma:" + dma, self.dsem[dma], self.dcnt[dma]), "waw"))
            self.dcnt[dma] += 16
            ev = ("dma:" + dma, self.dsem[dma], self.dcnt[dma])
        waits = {}
        for (src, sem, val), kind in deps:
            if src == eng and dma is None:
                if eng == PE:
                    continue
            if self.known[eng].get(src, 0) >= val:
                continue
            if src not in waits or waits[src][1] < val:
                waits[src] = (sem, val)
        for src, (sem, val) in waits.items():
            self.known[eng][src] = val
        self.ops[eng].append((list(waits.values()), fn, ev))
        for n in reads:
            r = self._r(n)
            o = r["r"].get(ev[0])
            if o is None or o[2] < ev[2]:
                r["r"][ev[0]] = ev
        for n in writes:
            r = self._r(n)
            r["w"] = ev
            r["r"] = {}
        return ev

    def final_wait(self, eng, prefixes):
        waits = {}
        for pre in prefixes:
            for src, ev in self.summary(pre).items():
                if src not in waits or waits[src][1] < ev[2]:
                    waits[src] = (ev[1], ev[2])
        self.ops[eng].append((list(waits.values()), None, None))

    def replay(self):
        nc = self.nc
        with nc.Block() as block:
            for e in ENGS:
                ops = self.ops[e]

                def body(engine, ops=ops):
                    for waits, fn, ev in ops:
                        for sem, val in waits:
                            engine.wait_ge(sem, val)
                        if fn is None:
                            continue
                        ins = fn(engine)
                        ins.then_inc(ev[1], 16 if ev[0].startswith("dma:") else 1)

                getattr(block, e)(body)


class Arena:
    def __init__(self, P, ap, nbytes):
        self.P = P
        self.ap = ap
        self.nbytes = nbytes
        self.live = {}
        self.ghosts = []
        self.gen = 0
        self.peak = 0

    def take(self, base, shape, dtype, parts=128, top=False):
        esz = 2 if dtype == BF16 else 4
        n = 1
        for s in shape[1:]:
            n *= s
        nb = (n * esz + 63) // 64 * 64
        segs = sorted(self.live.values())
        if not top:
            lo = 0
            for a, b in segs:
                if lo + nb <= a:
                    break
                lo = max(lo, b)
        else:
            hi = self.nbytes
            lo = None
            for a, b in reversed(segs):
                if b + nb <= hi:
                    break
                hi = min(hi, a)
            lo = hi - nb
        assert lo >= 0 and lo + nb <= self.nbytes, \
            f"arena overflow taking {base} {nb} (live={sum(b - a for a, b in segs)}) {sorted((v, k) for k, v in self.live.items())}"
        self.gen += 1
        name = f"{base}@{self.gen}"
        self.live[name] = (lo, lo + nb)
        self.peak = max(self.peak, lo + nb)
        summ = {}
        for a, b, s in self.ghosts:
            if a < lo + nb and lo < b:
                for src, ev in s.items():
                    o = summ.get(src)
                    if o is None or o[2] < ev[2]:
                        summ[src] = ev
        if summ:
            self.P.prefix_init.append((name + ":", summ))
        v = self.ap[0:shape[0], lo // 4:(lo + nb) // 4]
        if dtype != F32:
            v = v.bitcast(dtype)
        v = v[:, 0:n]
        if len(shape) == 3:
            v = v.rearrange("p (a b) -> p a b", b=shape[2])
        elif len(shape) == 4:
            v = v.rearrange("p (a b c) -> p a b c", b=shape[2], c=shape[3])
        return Buf(self, name, v)

    def release(self, buf):
        lo, hi = self.live.pop(buf.name)
        self.ghosts.append((lo, hi, self.P.summary(buf.name + ":")))


class Buf:
    def __init__(self, arena, name, ap):
        self.arena = arena
        self.name = name
        self.ap = ap

    def t(self, sub="", key=None):
        ap = self.ap if key is None else self.ap[key]
        return T(ap, f"{self.name}:{sub}")

    def free(self):
        self.arena.release(self)


def chunk_plan():
    plan = []
    for hd in range(8):
        plan.append((f"da{hd}", KC * 384))
    for h in range(4):
        plan.append((f"mlqk{h}", KC * 256))
        plan.append((f"mlv{h}", KC * 256))
        plan.append((f"mlo{h}", KC * 256))
    for j in range(8):
        plan.append((f"p4_{j}", KC * 512))
    for c in range(2):
        plan.append((f"mix{c}", KC * 512))
    for nm in ("xk", "xv", "xq", "xo"):
        for c in range(2):
            plan.append((f"{nm}{c}", KC * 512))
    for c in range(NF // 2):
        plan.append((f"gu{c}", KC * 512))
    for j in range(8):
        plan.append((f"dn{j}", NF * 128))
    return plan


CHUNKS = chunk_plan()
CHIDX = {k: i for i, (k, n) in enumerate(CHUNKS)}
CHLEN = {k: n for k, n in CHUNKS}


def col_plan():
    names = []
    for h in range(4):
        names += [f"bq{h}", f"bk{h}", f"bo{h}_0", f"bo{h}_1"]
        for qk in "qk":
            for tap in range(4):
                names.append(f"cw{qk}{h}_{tap}")
            names.append(f"cb{qk}{h}")
    for hd in range(8):
        names += [f"dbq{hd}", f"dbk{hd}"]
    for j in range(8):
        names += [f"bga{j}", f"bgb{j}", f"mlnw{j}", f"danw{j}"]
        for l in (1, 2, 3):
            names += [f"ln{l}g{j}", f"ln{l}b{j}"]
    names += ["bi", "bf", "invf", "sgn"]
    return {n: i for i, n in enumerate(names)}


COLS = col_plan()
NCOL = len(COLS)

CB_ID, CB_ONE, CB_OD, CB_O256, CB_O128, CB_MASK, CB_PERM = [i * 128 for i in range(7)]
NCB = 7 * 128
CF_ID, CF_SEL = 0, 128
NCF = 128 + 512


def _pk(wc):
    K, n = wc.shape
    kc = K // 128
    a = np.ascontiguousarray(wc.reshape(kc, 128, n).transpose(1, 0, 2)).reshape(128, kc * n)
    out = np.zeros((128, CHW), np.float32)
    out[:, :kc * n] = a
    return out


def pack_host(inp):
    w_in = inp["w_in"][0]
    b_in = inp["b_in"][0]
    O_MLV, O_MLO, O_IF = 1024, 2048, 3072
    O_DQ, O_DK, O_DV, O_G = 3080, 4104, 5128, 6152
    chunks = {}
    for hd in range(8):
        cols = np.r_[O_DQ + hd * 128:O_DQ + hd * 128 + 128, O_DK + hd * 128:O_DK + hd * 128 + 128,
                     O_DV + hd * 128:O_DV + hd * 128 + 128]
        chunks[f"da{hd}"] = _pk(w_in[:, cols])
    for h in range(4):
        cols = np.r_[h * 128:h * 128 + 128, 512 + h * 128:512 + h * 128 + 128]
        chunks[f"mlqk{h}"] = _pk(w_in[:, cols])
        chunks[f"mlv{h}"] = _pk(w_in[:, O_MLV + h * 256:O_MLV + h * 256 + 256])
        chunks[f"mlo{h}"] = _pk(w_in[:, O_MLO + h * 256:O_MLO + h * 256 + 256])
    pa, pb = inp["w_proj_a"][0], inp["w_proj_b"][0]
    for j in range(8):
        sl = slice(j * 128, j * 128 + 128)
        chunks[f"p4_{j}"] = _pk(np.concatenate(
            [pa[:, sl], pb[:, sl], w_in[:, O_G + j * 128:O_G + j * 128 + 128],
             w_in[:, O_G + 1024 + j * 128:O_G + 1024 + j * 128 + 128]], axis=1))
    for nm, key in (("mix", "w_mix_out"), ("xk", "w_xk"), ("xv", "w_xv"), ("xq", "w_xq"), ("xo", "w_xo")):
        w = inp[key][0]
        for c in range(2):
            chunks[f"{nm}{c}"] = _pk(w[:, c * 512:c * 512 + 512])
    wg, wu, wd = inp["w_ffn_gate"][0], inp["w_ffn_up"][0], inp["w_ffn_down"][0]
    for c in range(NF // 2):
        f0, f1 = 2 * c, 2 * c + 1
        chunks[f"gu{c}"] = _pk(np.concatenate(
            [wg[:, f0 * 128:f0 * 128 + 128], wu[:, f0 * 128:f0 * 128 + 128],
             wg[:, f1 * 128:f1 * 128 + 128], wu[:, f1 * 128:f1 * 128 + 128]], axis=1))
    for j in range(8):
        chunks[f"dn{j}"] = _pk(wd[:, j * 128:j * 128 + 128])
    wpk = np.stack([chunks[k] for k, _ in CHUNKS], axis=0)

    wif = np.ascontiguousarray(w_in[:, O_IF:O_IF + 8].reshape(KC, 128, 8).transpose(1, 0, 2)).reshape(128, KC * 8)

    cols = np.zeros((128, NCOL), np.float32)
    conv_w, conv_b = inp["conv_w"][0], inp["conv_b"][0]
    for h in range(4):
        cols[:, COLS[f"bq{h}"]] = b_in[h * 128:h * 128 + 128]
        cols[:, COLS[f"bk{h}"]] = b_in[512 + h * 128:512 + h * 128 + 128]
        cols[:, COLS[f"bo{h}_0"]] = b_in[O_MLO + h * 256:O_MLO + h * 256 + 128]
        cols[:, COLS[f"bo{h}_1"]] = b_in[O_MLO + h * 256 + 128:O_MLO + h * 256 + 256]
        for qk, off in (("q", 0), ("k", 512)):
            ch = slice(off + h * 128, off + h * 128 + 128)
            for tap in range(4):
                cols[:, COLS[f"cw{qk}{h}_{tap}"]] = conv_w[tap, ch]
            cols[:, COLS[f"cb{qk}{h}"]] = conv_b[ch]
    for hd in range(8):
        cols[:, COLS[f"dbq{hd}"]] = b_in[O_DQ + hd * 128:O_DQ + hd * 128 + 128]
        cols[:, COLS[f"dbk{hd}"]] = b_in[O_DK + hd * 128:O_DK + hd * 128 + 128]
    for j in range(8):
        sl = slice(j * 128, j * 128 + 128)
        cols[:, COLS[f"bga{j}"]] = b_in[O_G + j * 128:O_G + j * 128 + 128]
        cols[:, COLS[f"bgb{j}"]] = b_in[O_G + 1024 + j * 128:O_G + 1024 + j * 128 + 128]
        cols[:, COLS[f"mlnw{j}"]] = inp["ml_norm_w"][0][sl]
        cols[:, COLS[f"danw{j}"]] = inp["da_norm_w"][0][sl]
        for l in (1, 2, 3):
            cols[:, COLS[f"ln{l}g{j}"]] = inp[f"ln{l}_g"][0][sl]
            cols[:, COLS[f"ln{l}b{j}"]] = inp[f"ln{l}_b"][0][sl]
    cols[0:4, COLS["bi"]] = b_in[O_IF:O_IF + 4]
    cols[0:4, COLS["bf"]] = b_in[O_IF + 4:O_IF + 8]
    half = 32
    inv = (10000.0 ** (-np.arange(half, dtype=np.float32) / half)).astype(np.float32)
    p = np.arange(128)
    cols[:, COLS["invf"]] = inv[p % 32]
    cols[:, COLS["sgn"]] = np.where((p % 64) < 32, -1.0, 1.0)

    brow = np.concatenate([b_in[O_MLV:O_MLV + 1024], b_in[O_DV:O_DV + 1024]])[None, :].astype(np.float32)
    lam = np.concatenate([inp["lam_q1"][0], inp["lam_k1"][0], inp["lam_q2"][0], inp["lam_k2"][0]])[None, :]
    lam = np.ascontiguousarray(lam, dtype=np.float32)

    cbf = np.zeros((128, NCB), np.float32)
    cbf[:, CB_ID:CB_ID + 128] = np.eye(128)
    cbf[:, CB_ONE:CB_ONE + 128] = 1.0
    cbf[:, CB_OD:CB_OD + 128] = 1.0 / 1024
    cbf[:, CB_O256:CB_O256 + 128] = 1.0 / 256
    cbf[:, CB_O128:CB_O128 + 128] = 1.0 / 128
    s_, t_ = np.meshgrid(np.arange(128), np.arange(128), indexing="ij")
    cbf[:, CB_MASK:CB_MASK + 128] = np.where(s_ <= t_, 0.0, -30000.0)
    partner = np.where((p % 64) < 32, p + 32, p - 32)
    perm = np.zeros((128, 128), np.float32)
    perm[partner, p] = 1.0
    cbf[:, CB_PERM:CB_PERM + 128] = perm
    cf = np.zeros((128, NCF), np.float32)
    cf[:, CF_ID:CF_ID + 128] = np.eye(128)
    for h in range(4):
        cf[h, CF_SEL + h * 128:CF_SEL + h * 128 + 128] = 1.0
    return dict(wpk=wpk, wif=wif, cols=cols, brow=brow, lam=lam, cbf=cbf, cf=cf)


def build(S=2048, NSEQ=2, dbg=None):
    NT, TB = S // 128, S // 512
    nc = bass.Bass("TRN2", target_bir_lowering=False)
    xT_d = nc.dram_tensor("xT", [NSEQ, D, S], F32, kind="ExternalInput").ap()
    memT_d = nc.dram_tensor("memT", [NSEQ, D, MEM], F32, kind="ExternalInput").ap()
    pos_d = nc.dram_tensor("pos", [NSEQ, S], I32, kind="ExternalInput").ap()
    wpk_d = nc.dram_tensor("wpk", [len(CHUNKS), 128, CHW], F32, kind="ExternalInput").ap()
    wif_d = nc.dram_tensor("wif", [128, KC * 8], F32, kind="ExternalInput").ap()
    cols_d = nc.dram_tensor("cols", [128, NCOL], F32, kind="ExternalInput").ap()
    brow_d = nc.dram_tensor("brow", [1, 2048], F32, kind="ExternalInput").ap()
    lam_d = nc.dram_tensor("lam", [1, 256], F32, kind="ExternalInput").ap()
    cbf_d = nc.dram_tensor("cbf", [128, NCB], F32, kind="ExternalInput").ap()
    cf_d = nc.dram_tensor("cf", [128, NCF], F32, kind="ExternalInput").ap()
    outT_d = nc.dram_tensor("outT", [NSEQ, D, S], F32, kind="ExternalOutput").ap()
    dbg_d = {}
    if dbg:
        for k, shp in dbg.items():
            dbg_d[k] = nc.dram_tensor("dbg_" + k, list(shp), F32, kind="ExternalOutput").ap()

    st = ExitStack()
    with st:
        P = Prog(nc, st)
        ARB = 206 * 1024
        arena_t = st.enter_context(nc.sbuf_tensor("arena", [128, ARB // 4], F32))
        AR = Arena(P, arena_t[:], ARB)
        pbank = [st.enter_context(nc.psum_tensor(f"ps{i}", [128, 512], F32)) for i in range(8)]
        PS = [T(pbank[i][:], f"ps{i}") for i in range(8)]

        def rd(*ts):
            out = []
            for t in ts:
                if isinstance(t, T):
                    out += t.res
            return out

        def mm(out, lhsT, rhs, start=True, stop=True):
            P.emit(PE, lambda e: e.matmul(out.ap, lhsT=lhsT.ap, rhs=rhs.ap, start=start, stop=stop),
                   reads=rd(lhsT, rhs), writes=out.res)

        def tr(out, in_, ident):
            P.emit(PE, lambda e: e.transpose(out.ap, in_.ap, ident.ap), reads=rd(in_, ident), writes=out.res)

        def apof(x):
            return x.ap if isinstance(x, T) else x

        def act(out, in_, func, bias=None, scale=None, eng=ACT):
            kw = {}
            if bias is not None:
                kw["bias"] = apof(bias)
            if scale is not None:
                kw["scale"] = apof(scale)
            P.emit(ACT, lambda e: e.activation(out=out.ap, in_=in_.ap, func=func, **kw),
                   reads=rd(in_, bias, scale), writes=out.res)

        def tt(eng, out, in0, in1, op):
            P.emit(eng, lambda e: e.tensor_tensor(out=out.ap, in0=in0.ap, in1=in1.ap, op=op),
                   reads=rd(in0, in1), writes=out.res)

        def ts(eng, out, in0, s1, op0, s2=None, op1=None):
            if op1 is None:
                P.emit(eng, lambda e: e.tensor_scalar(out=out.ap, in0=in0.ap, scalar1=apof(s1), scalar2=None, op0=op0),
                       reads=rd(in0, s1), writes=out.res)
            else:
                P.emit(eng, lambda e: e.tensor_scalar(out=out.ap, in0=in0.ap, scalar1=apof(s1), scalar2=apof(s2),
                                                      op0=op0, op1=op1),
                       reads=rd(in0, s1, s2), writes=out.res)

        def stt(out, in0, sc, in1, op0, op1):
            P.emit(DVE, lambda e: e.scalar_tensor_tensor(out=out.ap, in0=in0.ap, scalar=apof(sc), in1=in1.ap,
                                                         op0=op0, op1=op1),
                   reads=rd(in0, sc, in1), writes=out.res)

        def cp(eng, out, in_):
            if eng == ACT:
                act(out, in_, AF.Copy)
            else:
                P.emit(eng, lambda e: e.tensor_copy(out=out.ap, in_=in_.ap), reads=rd(in_), writes=out.res)

        def mset(eng, out, val):
            P.emit(eng, lambda e: e.memset(out.ap, val), writes=out.res)

        def scan(out, d0, d1, init, op0, op1):
            P.emit(DVE, lambda e: e.tensor_tensor_scan(out=out.ap, data0=d0.ap, data1=d1.ap, initial=init,
                                                       op0=op0, op1=op1),
                   reads=rd(d0, d1), writes=out.res)

        def rcp(out, in_):
            P.emit(DVE, lambda e: e.reciprocal(out=out.ap, in_=in_.ap), reads=rd(in_), writes=out.res)

        def dma(eng, out_ap, in_ap, reads=(), writes=(), sem=None):
            P.emit(eng, lambda e: e.dma_start(out=out_ap, in_=in_ap), reads=list(reads), writes=list(writes), dma=sem)

        dbg_n = [0]

        def dump(key, t, dst_key=None):
            if key in dbg_d:
                dst = dbg_d[key] if dst_key is None else dbg_d[key][dst_key]
                dbg_n[0] += 1
                res = []
                for r_ in t.res:
                    res += [n for n in P.res if n.startswith(r_)]
                pieces = [(t.ap, dst)] if len(t.ap.shape) == 2 else [(t.ap[:, j_, :], dst[:, j_, :]) for j_ in range(t.ap.shape[1])]
                for sap, dap in pieces:
                    stg = AR.take("dbgstg", list(sap.shape), F32)
                    P.emit(DVE, lambda e, sap=sap, stg=stg: e.tensor_copy(out=stg.ap, in_=sap), reads=res, writes=[stg.name + ":"])
                    dma(SP, dap, stg.ap, reads=[stg.name + ":"], sem=f"dbg{dbg_n[0] % 4}")
                    stg.free()

        cbf = AR.take("cbf", [128, NCB], BF16)
        cfc = AR.take("cf", [128, NCF], F32)
        colb = AR.take("cols", [128, NCOL], F32)
        browb = AR.take("brow", [128, 2048], BF16)
        lamb = AR.take("lam", [1, 256], F32, parts=1)
        misc = AR.take("misc", [128, 16], F32)
        dma(POOL, cbf.ap, cbf_d, writes=[cbf.name + ":"], sem="c_cbf")
        dma(SP, cfc.ap, cf_d, writes=[cfc.name + ":"], sem="c_cf")
        dma(SP, colb.ap, cols_d, writes=[colb.name + ":"], sem="c_cols")
        dma(POOL, browb.ap, brow_d.partition_broadcast(128), writes=[browb.name + ":"], sem="c_brow")
        dma(SP, lamb.ap, lam_d, writes=[lamb.name + ":"], sem="c_lam")
        CBT = cbf.t()
        ident_b = CBT[:, CB_ID:CB_ID + 128]
        ones_b = CBT[:, CB_ONE:CB_ONE + 128]
        onesD_b = CBT[:, CB_OD:CB_OD + 128]
        ones256_b = CBT[:, CB_O256:CB_O256 + 128]
        ones128_b = CBT[:, CB_O128:CB_O128 + 128]
        mask_b = CBT[:, CB_MASK:CB_MASK + 128]
        perm_b = CBT[:, CB_PERM:CB_PERM + 128]
        CFT = cfc.t()
        ident_f = CFT[:, CF_ID:CF_ID + 128]

        def sel_f(h):
            return CFT[0:4, CF_SEL + h * 128:CF_SEL + h * 128 + 128]

        ones_row_f = CFT[0:1, CF_SEL:CF_SEL + 128]
        ones_row_b = T(cbf.ap[0:1, CB_ONE:CB_ONE + 128], cbf.name + ":")
        COLT = colb.t()

        def col(name, parts=128):
            i = COLS[name]
            return COLT[0:parts, i:i + 1]

        MISC = misc.t()
        neglam = MISC[:, 0:1]
        nbf = MISC[0:4, 1:2]
        dagn = AR.take("dagn", [128, 8], F32)
        for j in range(8):
            ts(DVE, dagn.t()[:, j:j + 1], col(f"danw{j}"), 1.0 - LAM_INIT, ALU.mult)
        DAGN = dagn.t()
        lt = AR.take("lamtmp", [1, 192], F32, parts=1)
        LT = lt.t()
        LAMT = lamb.t()
        tt(DVE, LT[:, 0:64], LAMT[:, 0:64], LAMT[:, 64:128], ALU.mult)
        tt(DVE, LT[:, 64:128], LAMT[:, 128:192], LAMT[:, 192:256], ALU.mult)
        P.emit(DVE, lambda e: e.tensor_reduce(out=lt.ap[:, 128:129], in_=lt.ap[:, 0:64], axis=mybir.AxisListType.X,
                                              op=ALU.add), reads=LT.res, writes=LT.res)
        P.emit(DVE, lambda e: e.tensor_reduce(out=lt.ap[:, 129:130], in_=lt.ap[:, 64:128], axis=mybir.AxisListType.X,
                                              op=ALU.add), reads=LT.res, writes=LT.res)
        act(LT[:, 130:132], LT[:, 128:130], AF.Exp)
        tt(DVE, LT[:, 132:133], LT[:, 131:132], LT[:, 130:131], ALU.subtract)
        ts(DVE, LT[:, 133:134], LT[:, 132:133], -LAM_INIT, ALU.add)
        mm(PS[0][:, 0:2], ones_row_f, LT[:, 132:134])
        cp(DVE, neglam, PS[0][:, 1:2])
        ts(DVE, nbf, col("bf", 4), -1.0, ALU.mult)
        lt.free()

        diagb = AR.take("diag", [128, 32, 128], BF16)
        for h_ in range(4):
            for qi_, qk_ in enumerate("qk"):
                for tap_ in range(4):
                    ts(DVE, T(diagb.ap[:, (h_ * 2 + qi_) * 4 + tap_, :], diagb.name + ":"), ident_f,
                       col(f"cw{qk_}{h_}_{tap_}"), ALU.mult)
        DIAG = diagb.t()

        NSLOT = 3
        wslots = [AR.take(f"wslot{i}", [128, CHW], BF16) for i in range(NSLOT)]
        wstate = {"n": 0}

        def wload(key):
            i = wstate["n"] % NSLOT
            wstate["n"] += 1
            n = CHLEN[key]
            slot = wslots[i]
            res = f"{slot.name}:w"
            half = n // 2
            src = wpk_d[CHIDX[key]]
            P.emit(POOL, lambda e: e.dma_start(out=slot.ap[:, 0:n], in_=src[:, 0:n]), writes=[res], dma=f"ws{i}")
            return T(slot.ap[:, 0:n], res)

        def wview(wt, ncols, kc=KC):
            return [wt[:, k * ncols:(k + 1) * ncols] for k in range(kc)]

        for sq in range(NSEQ):
            xTb = AR.take("xTb", [128, KC, S], BF16, top=True)
            xsrc = xT_d[sq].rearrange("(kc p) s -> p kc s", p=128)
            for k in range(KC):
                P.emit(POOL, lambda e, k=k, xTb=xTb, xsrc=xsrc: e.dma_start(out=xTb.ap[:, k, :], in_=xsrc[:, k, :]),
                       writes=[f"{xTb.name}:{k}"], dma=f"x{k}")

            def xT(k, lo, hi):
                return T(xTb.ap[:, k, lo:hi], f"{xTb.name}:{k}")

            rows = {n: AR.take("row_" + n, [4, S], F32, parts=4) for n in ("i", "l", "a", "A", "w1", "w2")}
            negA = AR.take("negA", [4, S], F32, parts=4)
            wqr = AR.take("wqr", [4, S], F32, parts=4)
            flr = AR.take("flr", [4, S], F32, parts=4)
            aT = AR.take("aT", [128, NT, 4], F32)
            wsT = AR.take("wsT", [128, NT, 4], F32)
            decb = AR.take("decb", [128, 4, NT], F32)
            wifb = AR.take("wifb", [128, KC * 8], BF16)
            dma(POOL, wifb.ap, wif_d, writes=[wifb.name + ":"], sem="wif")
            R = {n: b.t() for n, b in rows.items()}
            for tb in range(TB):
                sl = slice(tb * 512, tb * 512 + 512)
                for g, ps in ((0, PS[6]), (1, PS[7])):
                    for k in range(KC):
                        mm(ps[0:4, :], T(wifb.ap[:, k * 8 + g * 4:k * 8 + g * 4 + 4], wifb.name + ":"),
                           xT(k, tb * 512, tb * 512 + 512), start=(k == 0), stop=(k == KC - 1))
                act(R["i"][:, sl], PS[6][0:4, :], AF.Identity, bias=col("bi", 4))
                act(R["l"][:, sl], PS[7][0:4, :], AF.Exp, bias=nbf, scale=-1.0)
            act(R["l"], R["l"], AF.Ln, bias=1.0)
            mset(DVE, R["w1"], 1.0)
            scan(R["w2"], R["w1"], R["l"], 0.0, ALU.mult, ALU.add)
            tt(DVE, R["a"], R["i"], R["w2"], ALU.add)
            scan(R["A"], R["a"], R["a"], 0.0, ALU.max, ALU.max)
            ts(DVE, negA.t(), R["A"], -1.0, ALU.mult)
            tt(DVE, flr.t(), R["w2"], R["A"], ALU.subtract)
            act(flr.t(), flr.t(), AF.Exp)
            A3 = rows["A"].ap.rearrange("p (c t) -> p c t", t=128)
            w13 = rows["w1"].ap.rearrange("p (c t) -> p c t", t=128)
            il3 = rows["i"].ap.rearrange("p (c t) -> p c t", t=128)
            P.emit(DVE, lambda e, w13=w13, A3=A3: e.tensor_copy(out=w13, in_=A3[:, :, 127:128].to_broadcast([4, NT, 128])),
                   reads=R["A"].res, writes=R["w1"].res)
            mset(DVE, T(il3[:, 0, :], R["i"].res), 0.0)
            if NT > 1:
                P.emit(DVE, lambda e, il3=il3, w13=w13: e.tensor_copy(out=il3[:, 1:NT, :], in_=w13[:, 0:NT - 1, :]),
                       reads=R["w1"].res, writes=R["i"].res)
            tt(DVE, wqr.t(), R["i"], R["A"], ALU.subtract)
            act(wqr.t(), wqr.t(), AF.Exp, bias=LNC)
            tt(DVE, R["l"], R["a"], R["w1"], ALU.subtract)
            act(R["l"], R["l"], AF.Exp)
            P.emit(DVE, lambda e, rows=rows, il3=il3, w13=w13: e.tensor_tensor(out=rows["w2"].ap[:, 0:NT], in0=il3[:, :, 0], in1=w13[:, :, 0],
                                                  op=ALU.subtract), reads=R["i"].res + R["w1"].res, writes=R["w2"].res)
            act(R["w2"][:, 0:NT], R["w2"][:, 0:NT], AF.Exp)
            for c in range(NT):
                tr(PS[6][:, c * 4:c * 4 + 4], R["a"][:, c * 128:c * 128 + 128], ident_f[0:4, 0:4])
                tr(PS[7][:, c * 4:c * 4 + 4], R["l"][:, c * 128:c * 128 + 128], ident_f[0:4, 0:4])
            ts(DVE, T(aT.ap.rearrange("p c h -> p (c h)"), aT.name + ":"), PS[6][:, 0:NT * 4], LNC, ALU.add)
            cp(DVE, T(wsT.ap.rearrange("p c h -> p (c h)"), wsT.name + ":"), PS[7][:, 0:NT * 4])
            for h in range(4):
                mm(PS[6][:, h * NT:(h + 1) * NT], sel_f(h), R["w2"][:, 0:NT])
            cp(DVE, T(decb.ap.rearrange("p h c -> p (h c)"), decb.name + ":"), PS[6][:, 0:4 * NT])
            dump("negA", negA.t())
            dump("flr", flr.t())
            dump("wqr", wqr.t())
            for b_ in rows.values():
                b_.free()
            wifb.free()

            hnA = AR.take("hnA", [128, 8, S], BF16)
            qT_b = AR.take("qT", [128, S], BF16)
            kT_b = AR.take("kT", [128, S], BF16)
            vml = AR.take("vml", [128, NT, 256], BF16)
            ogb = AR.take("og", [128, 2, S], BF16)
            xpad = {qk: [AR.take(f"xp{qk}{i}", [128, 516], BF16) for i in range(2)] for qk in "qk"}
            sgb = [AR.take(f"sg{i}", [128, 512], F32) for i in range(2)]
            Cst = AR.take("Cst", [128, 384], F32)
            Cbf = [AR.take(f"Cbf{i}", [128, 384], BF16) for i in range(3)]
            hbuf = [AR.take(f"hbuf{i}", [128, 2, 512], BF16) for i in range(2)]
            hsq = AR.take("hsq", [128, 2, 512], BF16)
            DTb = [AR.take(f"DT{i}", [128, 128], F32) for i in range(2)]
            Stb = [AR.take(f"St{i}", [128, 128], BF16) for i in range(2)]
            qsb = [AR.take(f"qs{i}", [128, 128], BF16) for i in range(2)]
            ksb = [AR.take(f"ks{i}", [128, 128], BF16) for i in range(2)]
            flb = [AR.take(f"fl{i}", [128, 128], F32) for i in range(2)]
            dnb = [AR.take(f"dn{i}", [128, 128], F32) for i in range(2)]
            lnm = AR.take("lnm", [128, 512], F32)
            lnv = AR.take("lnv", [128, 512], F32)
            lnr = AR.take("lnr", [128, 512], F32)
            lnt = [AR.take(f"lnt{i}", [128, 512], F32) for i in range(2)]
            for h in range(4):
                wqk = wview(wload(f"mlqk{h}"), 256)
                wv = wview(wload(f"mlv{h}"), 256)
                wo = wview(wload(f"mlo{h}"), 256)
                late_cv = []
                for qi, (qk, dstb) in enumerate((("q", qT_b), ("k", kT_b))):
                    for tb in range(TB):
                        ps = PS[6 + ((qi * TB + tb) % 2)]
                        for k in range(KC):
                            mm(ps, wqk[k][:, qi * 128:qi * 128 + 128], xT(k, tb * 512, tb * 512 + 512),
                               start=(k == 0), stop=(k == KC - 1))
                        xp = xpad[qk][tb % 2]
                        xpp = xpad[qk][(tb + 1) % 2]
                        XP = xp.t()
                        if tb == 0:
                            mset(DVE, XP[:, 0:4], 0.0)
                        else:
                            cp(DVE, XP[:, 0:4], xpp.t()[:, 512:516])
                        act(XP[:, 4:516], ps, AF.Identity, bias=col(f"b{qk}{h}"))

                        def conv_part(qi=qi, qk=qk, tb=tb, XP=XP, dstb=dstb):
                            pcv = PS[4 + ((qi * TB + tb) % 2)]
                            for tap in range(4):
                                mm(pcv, DIAG[:, ((h * 2 + qi) * 4 + tap), :], XP[:, 1 + tap:513 + tap],
                                   start=(tap == 0), stop=(tap == 3))
                            sg = sgb[tb % 2].t()
                            act(sg, pcv, AF.Sigmoid, bias=col(f"cb{qk}{h}"))
                            stt(T(dstb.ap[:, tb * 512:tb * 512 + 512], f"{dstb.name}:{tb}"), pcv, col(f"cb{qk}{h}"), sg,
                                ALU.add, ALU.mult)

                        while late_cv:
                            late_cv.pop(0)()
                        late_cv.append(conv_part)
                for tt_ in range(NT):
                    ps = PS[6 + (tt_ % 2)]
                    for k in range(KC):
                        mm(ps[:, 0:256], xT(k, tt_ * 128, tt_ * 128 + 128), wv[k], start=(k == 0), stop=False)
                    mm(ps[:, 0:256], ident_b, T(browb.ap[:, h * 256:h * 256 + 256], browb.name + ":"),
                       start=False, stop=True)
                    cp(ACT, T(vml.ap[:, tt_, :], f"{vml.name}:{tt_}"), ps[:, 0:256])
                    while late_cv:
                        late_cv.pop(0)()
                for b2 in range(2):
                    for tb in range(TB):
                        ps = PS[6 + ((b2 * TB + tb) % 2)]
                        for k in range(KC):
                            mm(ps, wo[k][:, b2 * 128:b2 * 128 + 128], xT(k, tb * 512, tb * 512 + 512),
                               start=(k == 0), stop=(k == KC - 1))
                        act(T(ogb.ap[:, b2, tb * 512:tb * 512 + 512], f"{ogb.name}:{b2}_{tb}"), ps, AF.Sigmoid,
                            bias=col(f"bo{h}_{b2}"))
                if h == 0:
                    dump("mq0", qT_b.t())
                    dump("mk0", kT_b.t())
                CS = Cst.t()
                CBS = [Cbf[0].t(), Cbf[1].t(), Cbf[2].t()]

                def stageA(c):
                    i2 = c % 2
                    cs = slice(c * 128, c * 128 + 128)
                    tb = c // 4
                    pa = PS[0 + i2]
                    po = PS[2 + i2]
                    qc = T(qT_b.ap[:, cs], f"{qT_b.name}:{tb}")
                    kc_ = T(kT_b.ap[:, cs], f"{kT_b.name}:{tb}")
                    mm(pa[:, 0:128], ident_b, mask_b, start=True, stop=False)
                    mm(pa[:, 0:128], sel_f(h), negA.t()[:, cs], start=False, stop=True)
                    mm(pa[:, 256:384], sel_f(h), wqr.t()[:, cs])
                    mm(po[:, 384:512], sel_f(h), flr.t()[:, cs])
                    mm(pa[:, 128:256], kc_, qc)
                    ptb = T(pbank[5][:].bitcast(BF16)[:, 0:128], "ps5")
                    tr(ptb, kc_, ident_b)
                    DT = DTb[i2].t()
                    act(DT, pa[:, 0:128], AF.Exp, bias=T(aT.ap[:, c, h:h + 1], aT.name + ":"))
                    act(ksb[i2].t(), ptb, AF.Copy, scale=T(wsT.ap[:, c, h:h + 1], wsT.name + ":"))
                    tt(DVE, Stb[i2].t(), pa[:, 128:256], DT, ALU.mult)
                    tt(DVE, qsb[i2].t(), pa[:, 256:384], qc, ALU.mult)

                def stageU(c):
                    if c >= NT - 1:
                        return
                    pc = PS[4]
                    ks = ksb[c % 2].t()
                    vc = T(vml.ap[:, c, :], f"{vml.name}:{c}")
                    mm(pc[:, 0:256], ks, vc)
                    mm(pc[:, 256:384], ks, ones_b)
                    if c == 0:
                        cp(DVE, CS, pc[:, 0:384])
                    else:
                        stt(CS, CS, T(decb.ap[:, h, c:c + 1], decb.name + ":"), pc[:, 0:384], ALU.mult, ALU.add)
                    cp(ACT, CBS[(c + 1) % 3], CS)

                def stageB(c):
                    i2 = c % 2
                    tb = c // 4
                    po = PS[2 + i2]
                    St, qs = Stb[i2].t(), qsb[i2].t()
                    CB_ = CBS[c % 3]
                    vc = T(vml.ap[:, c, :], f"{vml.name}:{c}")
                    for b3 in range(3):
                        lh = vc[:, b3 * 128:b3 * 128 + 128] if b3 < 2 else ones_b
                        mm(po[:, b3 * 128:b3 * 128 + 128], lh, St, start=True, stop=(c == 0))
                        if c > 0:
                            mm(po[:, b3 * 128:b3 * 128 + 128], CB_[:, b3 * 128:b3 * 128 + 128], qs, start=False, stop=True)
                    dn_ = dnb[i2].t()
                    act(dn_, po[:, 256:384], AF.Abs)
                    tt(DVE, dn_, dn_, po[:, 384:512], ALU.max)
                    act(dn_, dn_, AF.Ln)
                    act(dn_, dn_, AF.Exp, scale=-1.0)
                    HB = hbuf[tb % 2].t("c%d" % (c % 4))
                    for b3 in range(2):
                        tt(DVE, HB[:, b3, (c % 4) * 128:(c % 4) * 128 + 128], po[:, b3 * 128:b3 * 128 + 128], dn_, ALU.mult)

                def hall(tb):
                    hb_ = hbuf[tb % 2]
                    return T(hb_.ap, [f"{hb_.name}:c{x}" for x in range(4)])

                def stageLN0(tb):
                    act(hsq.t(), hall(tb), AF.Square)

                def stageLNa(tb):
                    HALL = hall(tb)
                    pm, pq = PS[6], PS[7]
                    for b3 in range(2):
                        mm(pm, ones256_b, HALL[:, b3, :], start=(b3 == 0), stop=(b3 == 1))
                    for b3 in range(2):
                        mm(pq, ones256_b, hsq.t()[:, b3, :], start=(b3 == 0), stop=(b3 == 1))
                    act(lnm.t(), pm, AF.Copy)
                    act(lnv.t(), pm, AF.Square)
                    tt(DVE, lnv.t(), pq, lnv.t(), ALU.subtract)
                    act(lnr.t(), lnv.t(), AF.Ln, bias=EPS)
                    act(lnr.t(), lnr.t(), AF.Exp, scale=-0.5)

                def stageLNb(tb, b3):
                    HALL = hall(tb)
                    lt_ = lnt[b3].t()
                    tt(DVE, lt_, HALL[:, b3, :], lnm.t(), ALU.subtract)
                    tt(DVE, lt_, lt_, lnr.t(), ALU.mult)
                    stt(T(hnA.ap[:, 2 * h + b3, tb * 512:tb * 512 + 512], f"{hnA.name}:{2 * h + b3}_{tb}"), lt_,
                        col(f"mlnw{2 * h + b3}"),
                        T(ogb.ap[:, b3, tb * 512:tb * 512 + 512], f"{ogb.name}:{b3}_{tb}"), ALU.mult, ALU.mult)

                stageA(0)
                ln_pending = []
                for c in range(NT):
                    if c + 1 < NT:
                        stageA(c + 1)
                    stageB(c)
                    if c % 4 == 3:
                        tb_ = c // 4
                        stageLN0(tb_)
                        ln_pending.append((c + 2, lambda tb_=tb_: stageLNa(tb_)))
                        ln_pending.append((c + 3, lambda tb_=tb_: stageLNb(tb_, 0)))
                        ln_pending.append((c + 4, lambda tb_=tb_: stageLNb(tb_, 1)))
                    stageU(c)
                    while ln_pending and c >= ln_pending[0][0]:
                        ln_pending.pop(0)[1]()
                for _, fn_ in ln_pending:
                    fn_()
            dump("hnA", T(hnA.ap, hnA.name + ":"))
            for b_ in ([qT_b, kT_b, vml, ogb, Cst, hsq, lnm, lnv, lnr, negA, wqr, flr, aT, wsT, decb]
                       + Cbf + hbuf + xpad["q"] + xpad["k"] + sgb + DTb + Stb + qsb + ksb + flb + dnb + lnt):
                b_.free()

            cosb = AR.take("cosT", [128, S], F32)
            sinb = AR.take("sinT", [128, S], F32)
            posi = AR.take("posi", [128, S], I32)
            ang = AR.take("ang", [128, S], F32)
            tmpa = AR.take("tmpa", [128, S], F32)
            tmpb = AR.take("tmpb", [128, S], F32)
            dma(SP, posi.ap, pos_d[sq:sq + 1, :].partition_broadcast(128), writes=[posi.name + ":"], sem="pos")
            cp(DVE, ang.t(), posi.t())
            ts(DVE, ang.t(), ang.t(), col("invf"), ALU.mult)
            for tab, shift, scale_ap in ((sinb, 0.0, col("sgn")), (cosb, math.pi / 2, None)):
                src_ = ang.t()
                if shift != 0.0:
                    ts(DVE, tmpb.t(), ang.t(), shift, ALU.add)
                    src_ = tmpb.t()
                ts(DVE, tmpa.t(), src_, 1.0 / TWO_PI, ALU.mult, MAGIC, ALU.add)
                ts(DVE, tmpa.t(), tmpa.t(), MAGIC, ALU.subtract)
                stt(tmpa.t(), tmpa.t(), -TWO_PI, src_, ALU.mult, ALU.add)
                ts(DVE, tmpa.t(), tmpa.t(), 3.14159, ALU.min, -3.14159, ALU.max)
                act(tab.t(), tmpa.t(), AF.Sin, scale=scale_ap)
            for b_ in (posi, ang, tmpa, tmpb):
                b_.free()
            dump("cos", cosb.t())
            dump("sin", sinb.t())

            oB = AR.take("oB", [128, 8, S], BF16, top=True)
            qz = [AR.take(f"qz{i}", [128, S], BF16) for i in range(2)]
            mset(DVE, T(qz[0].ap[64:128, :], f"{qz[0].name}:z"), 0.0)
            mset(DVE, T(qz[1].ap[0:64, :], f"{qz[1].name}:z"), 0.0)
            kr = AR.take("kr", [128, S], BF16)
            vda = AR.take("vda", [128, NT, 128], BF16)
            raw = [AR.take(f"raw{i}", [128, 512], BF16) for i in range(2)]
            t1 = [AR.take(f"t1_{i}", [128, 512], F32) for i in range(2)]
            u1 = [AR.take(f"u1_{i}", [128, 512], F32) for i in range(2)]
            ET = [AR.take(f"ET{i}", [128, 512], BF16) for i in range(6)]
            o1b = AR.take("o1", [128, 512], F32)
            odbs = [AR.take(f"od{i}", [128, 512], F32) for i in range(2)]
            epi = []
            nblk = [0]
            rcb = [AR.take(f"rc{i}", [128, 512], F32) for i in range(2)]
            osqb = AR.take("osq", [128, 512], BF16)
            rsb = AR.take("rs", [128, 512], F32)
            cnt = {"raw": 0, "et": 0, "sc": 0, "acc": 0}
            for hd in range(8):
                wt = wload(f"da{hd}")
                wk = wview(wt, 384)
                late = []
                for which, dstb, bname in ((0, None, f"dbq{hd}"), (1, kr, f"dbk{hd}")):
                    for tb in range(TB):
                        ps = PS[6 + (cnt["raw"] % 2)]
                        for k in range(KC):
                            mm(ps, wk[k][:, which * 128:which * 128 + 128], xT(k, tb * 512, tb * 512 + 512),
                               start=(k == 0), stop=(k == KC - 1))
                        i2 = cnt["raw"] % 2
                        cnt["raw"] += 1
                        rw = raw[i2].t()
                        act(rw, ps, AF.Identity, bias=col(bname))

                        def rope_part(i2=i2, rw=rw, tb=tb, dstb=dstb):
                            ps2 = PS[4 + i2]
                            mm(ps2, perm_b, rw)
                            tt(DVE, t1[i2].t(), rw, cosb.t()[:, tb * 512:tb * 512 + 512], ALU.mult)
                            tt(DVE, u1[i2].t(), ps2, sinb.t()[:, tb * 512:tb * 512 + 512], ALU.mult)
                            tsl = slice(tb * 512, tb * 512 + 512)
                            if dstb is None:
                                tt(DVE, T(qz[0].ap[0:64, tsl], f"{qz[0].name}:{tb}"), t1[i2].t()[0:64], u1[i2].t()[0:64],
                                   ALU.add)
                                tt(DVE, T(qz[1].ap[64:128, tsl], f"{qz[1].name}:{tb}"), t1[i2].t()[64:128],
                                   u1[i2].t()[64:128], ALU.add)
                            else:
                                tt(DVE, T(dstb.ap[:, tsl], f"{dstb.name}:{tb}"), t1[i2].t(), u1[i2].t(), ALU.add)

                        while late:
                            late.pop(0)()
                        late.append(rope_part)
                for tt_ in range(NT):
                    ps = PS[6 + (tt_ % 2)]
                    for k in range(KC):
                        mm(ps[:, 0:128], xT(k, tt_ * 128, tt_ * 128 + 128), wk[k][:, 256:384], start=(k == 0), stop=False)
                    mm(ps[:, 0:128], ident_b, T(browb.ap[:, 1024 + hd * 128:1024 + hd * 128 + 128], browb.name + ":"),
                       start=False, stop=True)
                    cp(ACT, T(vda.ap[:, tt_, :], f"{vda.name}:{tt_}"), ps[:, 0:128])
                    while late:
                        late.pop(0)()
                if hd == 0:
                    dump("kr0", kr.t())
                for qb in range(TB):
                    q0 = qb * 512
                    for c in range(2):
                        pr = slice(c * 64, c * 64 + 64)
                        Ob = PS[0 + 2 * (cnt["acc"] % 2)]
                        Db = PS[1 + 2 * (cnt["acc"] % 2)]
                        cnt["acc"] += 1
                        tiles = []
                        for j in range(4 * qb + 4):
                            r = j - 4 * qb
                            off = 128 * r if r > 0 else 0
                            tiles.append((j, off, 512 - off, r >= 0))

                        def qk(tile):
                            j, off, w, diag = tile
                            sc = PS[4 + (cnt["sc"] % 3)]
                            et = ET[cnt["et"] % 6]
                            cnt["sc"] += 1
                            cnt["et"] += 1
                            qT_ = T(qz[c].ap[:, q0 + off:q0 + 512], [f"{qz[c].name}:{qb}", f"{qz[c].name}:z"])
                            kT_ = T(kr.ap[:, j * 128:j * 128 + 128], f"{kr.name}:{j // 4}")
                            mm(sc[:, 0:w], kT_, qT_, start=True, stop=not diag)
                            if diag:
                                mm(sc[:, 0:128], ident_b, mask_b, start=False, stop=True)
                            e_ = et.t()[:, 0:w]
                            act(e_, sc[:, 0:w], AF.Exp, scale=0.125)
                            return e_

                        def pv(tile, e_, first, last):
                            j, off, w, diag = tile
                            v_ = T(vda.ap[:, j, :], f"{vda.name}:{j}")
                            mm(Ob[:, off:512], v_, e_, start=first, stop=last)
                            mm(Db[:, off:512], ones_b, e_, start=first, stop=last)

                        pend = []
                        issued = 0
                        ntl = len(tiles)
                        while issued < min(2, ntl):
                            pend.append(qk(tiles[issued]))
                            issued += 1
                        for ti in range(ntl):
                            if issued < ntl:
                                pend.append(qk(tiles[issued]))
                                issued += 1
                            pv(tiles[ti], pend.pop(0), ti == 0, ti == ntl - 1)
                            if c == 0 and ti == 2:
                                while epi:
                                    epi.pop(0)()
                        rc = rcb[c].t()
                        act(rc, Db, AF.Ln)
                        act(rc, rc, AF.Exp, scale=-1.0)
                        if c == 0:
                            tt(DVE, o1b.t(), Ob, rc, ALU.mult)
                            while epi:
                                epi.pop(0)()
                        else:
                            odb = odbs[nblk[0] % 2]
                            nblk[0] += 1
                            tt(DVE, odb.t(), Ob, rc, ALU.mult)
                            stt(odb.t(), odb.t(), neglam, o1b.t(), ALU.mult, ALU.add)

                            tt(DVE, osqb.t(), odb.t(), odb.t(), ALU.mult)

                            def epilogue(odb=odb, hd=hd, qb=qb, q0=q0):
                                psn = PS[7]
                                mm(psn, ones128_b, osqb.t())
                                act(rsb.t(), psn, AF.Ln, bias=EPS)
                                act(rsb.t(), rsb.t(), AF.Exp, scale=-0.5)
                                stt(T(oB.ap[:, hd, q0:q0 + 512], f"{oB.name}:{hd}_{qb}"), odb.t(), DAGN[:, hd:hd + 1],
                                    rsb.t(), ALU.mult, ALU.mult)

                            epi.append(epilogue)
            while epi:
                epi.pop(0)()
            for b_ in qz + [kr, vda, o1b, osqb, rsb, cosb, sinb] + odbs + raw + t1 + u1 + ET + rcb:
                b_.free()

            def layer_norm(zj_fn, l, out_bf, out_f32, tb, produce=None, cast_eng=ACT):
                zb16 = AR.take("zb16", [128, 8, 512], BF16)
                zsq = AR.take("zsq", [128, 8, 512], BF16)
                mean = AR.take("mean", [128, 512], F32)
                var = AR.take("var", [128, 512], F32)
                rstd = AR.take("rstd", [128, 512], F32)
                pm, pq = PS[6], PS[7]
                pend_st = []
                for j in range(8):
                    if produce is not None:
                        produce(j)
                    zj = zj_fn(j)
                    zbj = T(zb16.ap[:, j, :], f"{zb16.name}:{j}")
                    zsj = T(zsq.ap[:, j, :], f"{zsq.name}:{j}")
                    cp(cast_eng, zbj, zj)
                    act(zsj, zj, AF.Square)
                    pend_st.append((j, zbj, zsj))
                    if len(pend_st) > 3:
                        j_, zb_, zs_ = pend_st.pop(0)
                        mm(pm, onesD_b, zb_, start=(j_ == 0), stop=(j_ == 7))
                        mm(pq, onesD_b, zs_, start=(j_ == 0), stop=(j_ == 7))
                for j_, zb_, zs_ in pend_st:
                    mm(pm, onesD_b, zb_, start=(j_ == 0), stop=(j_ == 7))
                    mm(pq, onesD_b, zs_, start=(j_ == 0), stop=(j_ == 7))
                act(mean.t(), pm, AF.Copy)
                act(var.t(), pm, AF.Square)
                tt(DVE, var.t(), pq, var.t(), ALU.subtract)
                act(rstd.t(), var.t(), AF.Ln, bias=EPS)
                act(rstd.t(), rstd.t(), AF.Exp, scale=-0.5)
                tmps = [AR.take(f"lntmp{i}", [128, 512], F32) for i in range(3)]
                for j in range(8):
                    tmp = tmps[j % 3]
                    tt(DVE, tmp.t(), zj_fn(j), mean.t(), ALU.subtract)
                    tt(DVE, tmp.t(), tmp.t(), rstd.t(), ALU.mult)
                    if out_bf is not None:
                        act(T(out_bf.ap[:, j, tb * 512:tb * 512 + 512], f"{out_bf.name}:{j}_{tb}"), tmp.t(), AF.Identity,
                            bias=col(f"ln{l}b{j}"), scale=col(f"ln{l}g{j}"))
                    if out_f32 is not None:
                        out_f32(j, tmp)
                for b_ in [zb16, zsq, mean, var, rstd] + tmps:
                    b_.free()

            mixin = AR.take("mixin", [128, 8, S], BF16)
            sab = [AR.take(f"sa{i}", [128, 512], F32) for i in range(2)]
            sbb = [AR.take(f"sb{i}", [128, 512], F32) for i in range(2)]
            tab = [AR.take(f"ta{i}", [128, 512], F32) for i in range(2)]
            tbb = [AR.take(f"tb{i}", [128, 512], F32) for i in range(2)]
            for j in range(8):
                wj = wview(wload(f"p4_{j}"), 512)
                for tb in range(TB):
                    i2 = tb % 2
                    pss = [PS[4 * i2 + q] for q in range(4)]
                    srcs = [lambda k: T(hnA.ap[:, k, tb * 512:tb * 512 + 512], f"{hnA.name}:{k}_{tb}"),
                            lambda k: T(oB.ap[:, k, tb * 512:tb * 512 + 512], f"{oB.name}:{k}_{tb}"),
                            lambda k: xT(k, tb * 512, tb * 512 + 512), lambda k: xT(k, tb * 512, tb * 512 + 512)]
                    for q in range(4):
                        for k in range(KC):
                            mm(pss[q], wj[k][:, q * 128:q * 128 + 128], srcs[q](k), start=(k == 0), stop=(k == KC - 1))
                    act(sab[i2].t(), pss[2], AF.Sigmoid, bias=col(f"bga{j}"))
                    act(sbb[i2].t(), pss[3], AF.Sigmoid, bias=col(f"bgb{j}"))
                    tt(DVE, tab[i2].t(), pss[0], sab[i2].t(), ALU.mult)
                    tt(DVE, tbb[i2].t(), pss[1], sbb[i2].t(), ALU.mult)
                    tt(DVE, T(mixin.ap[:, j, tb * 512:tb * 512 + 512], f"{mixin.name}:{j}_{tb}"), tab[i2].t(),
                       tbb[i2].t(), ALU.add)
            for b_ in [hnA, oB, xTb] + sab + sbb + tab + tbb:
                b_.free()
            dump("mixin", T(mixin.ap, mixin.name + ":"))

            x1b = AR.take("x1b", [128, 8, S], BF16, top=True)
            wm = [wview(wload(f"mix{c}"), 512) for c in range(2)]
            zbufs5 = [AR.take(f"z{i}", [128, 8, 512], F32) for i in range(2)]
            xress5 = [AR.take(f"xres{i}", [128, 8, 512], F32) for i in range(2)]
            for tb in range(TB):
                zbuf = zbufs5[tb % 2]
                xres = xress5[tb % 2]
                dma(SP, xres.ap, xT_d[sq].rearrange("(kc p) s -> p kc s", p=128)[:, :, tb * 512:tb * 512 + 512],
                    writes=[xres.name + ":"], sem=f"xres{tb % 2}")
                def prod5(j, tb=tb, zbuf=zbuf, xres=xres):
                    ps = PS[j % 4]
                    for k in range(KC):
                        mm(ps, wm[j // 4][k][:, (j % 4) * 128:(j % 4) * 128 + 128],
                           T(mixin.ap[:, k, tb * 512:tb * 512 + 512], f"{mixin.name}:{k}_{tb}"),
                           start=(k == 0), stop=(k == KC - 1))
                    stt(T(zbuf.ap[:, j, :], f"{zbuf.name}:{j}"), xres.t()[:, j, :], ALPHA, ps, ALU.mult, ALU.add)

                layer_norm(lambda j, zbuf=zbuf: T(zbuf.ap[:, j, :], f"{zbuf.name}:{j}"), 1, x1b, None, tb, produce=prod5)
            for b_ in zbufs5 + xress5:
                b_.free()
            mixin.free()
            dump("x1", T(x1b.ap, x1b.name + ":"))

            memb = AR.take("memb", [128, KC, MEM], BF16)
            msrc = memT_d[sq].rearrange("(kc p) s -> p kc s", p=128)
            P.emit(POOL, lambda e, memb=memb, msrc=msrc: e.dma_start(out=memb.ap, in_=msrc), writes=[memb.name + ":"], dma="mem")
            xkT = AR.take("xkT", [128, 8, MEM], BF16)
            xvb = AR.take("xvb", [128, 2, D], BF16)
            MB = memb.t()
            for c in range(2):
                wc = wview(wload(f"xk{c}"), 512)
                for jj in range(4):
                    ps = PS[jj % 4]
                    for k in range(KC):
                        mm(ps[:, 0:MEM], wc[k][:, jj * 128:jj * 128 + 128], MB[:, k, :], start=(k == 0), stop=(k == KC - 1))
                    cp(ACT, T(xkT.ap[:, c * 4 + jj, :], f"{xkT.name}:{c * 4 + jj}"), ps[:, 0:MEM])
            for c in range(2):
                wc = wview(wload(f"xv{c}"), 512)
                for m in range(2):
                    ps = PS[4 + m]
                    for k in range(KC):
                        mm(ps, MB[:, k, m * 128:m * 128 + 128], wc[k], start=(k == 0), stop=(k == KC - 1))
                    cp(ACT, T(xvb.ap[:, m, c * 512:c * 512 + 512], f"{xvb.name}:{m}_{c}"), ps)
            xqT = AR.take("xqT", [128, 8, S], BF16)
            for c in range(2):
                wc = wview(wload(f"xq{c}"), 512)
                for jj in range(4):
                    for tb in range(TB):
                        ps = PS[(jj * TB + tb) % 4]
                        for k in range(KC):
                            mm(ps, wc[k][:, jj * 128:jj * 128 + 128],
                               T(x1b.ap[:, k, tb * 512:tb * 512 + 512], f"{x1b.name}:{k}_{tb}"),
                               start=(k == 0), stop=(k == KC - 1))
                        cp(ACT if (jj + tb) % 2 else DVE,
                           T(xqT.ap[:, c * 4 + jj, tb * 512:tb * 512 + 512], f"{xqT.name}:{c * 4 + jj}_{tb}"), ps)
            xoT = AR.take("xoT", [128, 8, S], BF16, top=True)
            EX = [AR.take(f"EX{i}", [128, 2, 512], BF16) for i in range(2)]
            rcx = [AR.take(f"rcx{i}", [128, 512], F32) for i in range(2)]
            n_ = 0
            iters = [(hx, tb) for hx in range(4) for tb in range(TB)]

            def xa_scores(n):
                hx, tb = iters[n]
                i2 = n % 2
                for m in range(2):
                    ps = PS[4 + 2 * i2 + m]
                    for kk in range(2):
                        mm(ps, T(xkT.ap[:, hx * 2 + kk, m * 128:m * 128 + 128], f"{xkT.name}:{hx * 2 + kk}"),
                           T(xqT.ap[:, hx * 2 + kk, tb * 512:tb * 512 + 512], f"{xqT.name}:{hx * 2 + kk}_{tb}"),
                           start=(kk == 0), stop=(kk == 1))
                    act(T(EX[i2].ap[:, m, :], f"{EX[i2].name}:{m}"), ps, AF.Exp, scale=1.0 / 16.0)

            def xa_rest(n):
                hx, tb = iters[n]
                i2 = n % 2
                pd = PS[0 + 3 * i2]
                for m in range(2):
                    mm(pd, ones_b, T(EX[i2].ap[:, m, :], f"{EX[i2].name}:{m}"), start=(m == 0), stop=(m == 1))
                act(rcx[i2].t(), pd, AF.Ln)
                act(rcx[i2].t(), rcx[i2].t(), AF.Exp, scale=-1.0)
                for b2 in range(2):
                    po_ = PS[1 + b2]
                    for m in range(2):
                        mm(po_, T(xvb.ap[:, m, hx * 256 + b2 * 128:hx * 256 + b2 * 128 + 128], f"{xvb.name}:{m}_{hx // 2}"),
                           T(EX[i2].ap[:, m, :], f"{EX[i2].name}:{m}"), start=(m == 0), stop=(m == 1))
                    tt(DVE, T(xoT.ap[:, hx * 2 + b2, tb * 512:tb * 512 + 512], f"{xoT.name}:{hx * 2 + b2}_{tb}"),
                       po_, rcx[i2].t(), ALU.mult)

            xa_scores(0)
            for n in range(len(iters)):
                if n + 1 < len(iters):
                    xa_scores(n + 1)
                xa_rest(n)
            for b_ in [memb, xkT, xvb, xqT] + EX + rcx:
                b_.free()
            dump("xo", T(xoT.ap, xoT.name + ":"))

            x2b = AR.take("x2b", [128, 8, S], BF16)
            wx = [wview(wload(f"xo{c}"), 512) for c in range(2)]
            zbufs7 = [AR.take(f"z{i}", [128, 8, 512], F32) for i in range(2)]
            for tb in range(TB):
                zbuf = zbufs7[tb % 2]

                def prod7(j, tb=tb, zbuf=zbuf):
                    ps = PS[j % 4]
                    for k in range(KC):
                        mm(ps, wx[j // 4][k][:, (j % 4) * 128:(j % 4) * 128 + 128],
                           T(xoT.ap[:, k, tb * 512:tb * 512 + 512], f"{xoT.name}:{k}_{tb}"),
                           start=(k == 0), stop=(k == KC - 1))
                    stt(T(zbuf.ap[:, j, :], f"{zbuf.name}:{j}"),
                        T(x1b.ap[:, j, tb * 512:tb * 512 + 512], f"{x1b.name}:{j}_{tb}"), ALPHA, ps, ALU.mult, ALU.add)

                layer_norm(lambda j, zbuf=zbuf: T(zbuf.ap[:, j, :], f"{zbuf.name}:{j}"), 2, x2b, None, tb, produce=prod7)
            for b_ in zbufs7:
                b_.free()
            xoT.free()
            x1b.free()
            dump("x2", T(x2b.ap, x2b.name + ":"))

            zx = AR.take("zx", [128, (34 * S) // 4], F32, top=True)
            x2c_ap = zx.ap[:, (18 * S) // 4:(34 * S) // 4].bitcast(BF16).rearrange("p (a b) -> p a b", b=S)
            zall_ap = zx.ap[:, 0:8 * S].rearrange("p (a b) -> p a b", b=S)

            def x2c(k, tb):
                return T(x2c_ap[:, k, tb * 512:tb * 512 + 512], f"{zx.name}:x{k}_{tb}")

            for k in range(KC):
                for tb in range(TB):
                    cp(DVE if (k + tb) % 2 else ACT, x2c(k, tb),
                       T(x2b.ap[:, k, tb * 512:tb * 512 + 512], f"{x2b.name}:{k}_{tb}"))
            x2b.free()
            hid = AR.take("hid", [128, NF, S], BF16)
            sgf = [AR.take(f"sgf{i}", [128, 512], F32) for i in range(2)]
            tf = [AR.take(f"tf{i}", [128, 512], F32) for i in range(2)]
            n_ = 0
            for c in range(NF // 2):
                wc = wview(wload(f"gu{c}"), 512)
                for ff in range(2):
                    f = 2 * c + ff
                    for tb in range(TB):
                        i2 = n_ % 2
                        n_ += 1
                        pg, pu = PS[2 * i2], PS[2 * i2 + 1]
                        for q, ps in ((0, pg), (1, pu)):
                            for k in range(KC):
                                mm(ps, wc[k][:, (2 * ff + q) * 128:(2 * ff + q) * 128 + 128], x2c(k, tb),
                                   start=(k == 0), stop=(k == KC - 1))
                        act(sgf[i2].t(), pg, AF.Sigmoid)
                        tt(DVE, tf[i2].t(), pg, sgf[i2].t(), ALU.mult)
                        tt(DVE, T(hid.ap[:, f, tb * 512:tb * 512 + 512], f"{hid.name}:{f}_{tb}"), pu, tf[i2].t(), ALU.mult)
            for b_ in sgf + tf:
                b_.free()
            for j in range(8):
                wd = wview(wload(f"dn{j}"), 128, kc=NF)
                for jp in (2 * j - 9, 2 * j - 8):
                    if 0 <= jp < 8:
                        summ = P.summary(f"{zx.name}:x{jp}_")
                        for tb in range(TB):
                            P.add_readers(f"{zx.name}:z{j}_{tb}", summ)
                for tb in range(TB):
                    ps = PS[(j * TB + tb) % 4]
                    for f in range(NF):
                        mm(ps, wd[f], T(hid.ap[:, f, tb * 512:tb * 512 + 512], f"{hid.name}:{f}_{tb}"),
                           start=(f == 0), stop=(f == NF - 1))
                    stt(T(zall_ap[:, j, tb * 512:tb * 512 + 512], f"{zx.name}:z{j}_{tb}"), x2c(j, tb), ALPHA, ps,
                        ALU.mult, ALU.add)
            hid.free()
            osrc = outT_d[sq].rearrange("(kc p) s -> p kc s", p=128)
            yos = [AR.take(f"yo{i}", [128, 512], F32) for i in range(4)]
            for tb in range(TB):
                def outcb(j, tmp, tb=tb):
                    yo = yos[j % 4]
                    act(yo.t(), tmp.t(), AF.Identity, bias=col(f"ln3b{j}"), scale=col(f"ln3g{j}"))
                    dma(SP, osrc[:, j, tb * 512:tb * 512 + 512], yo.ap, reads=yo.t().res, sem=f"out{j}")

                layer_norm(lambda j, tb=tb: T(zall_ap[:, j, tb * 512:tb * 512 + 512], f"{zx.name}:z{j}_{tb}"),
                           3, None, outcb, tb, cast_eng=DVE)
            zx.free()
            for b_ in yos:
                b_.free()

        P.final_wait(SP, ["yo"])
        P.replay()
        print(f"[build] arena peak {AR.peak} / {ARB}; instr counts {P.cnt}")
    return nc


_CACHE = {}


def kernel(**inputs):
    x = np.asarray(inputs["x"], np.float32)
    mem = np.asarray(inputs["mem"], np.float32)
    pos = np.asarray(inputs["positions"], np.int32)
    B, S, _ = x.shape
    n_cores = 8
    nseq = B // n_cores
    pk = pack_host({k: np.asarray(v) for k, v in inputs.items()})
    xT = np.ascontiguousarray(x.transpose(0, 2, 1))
    memT = np.ascontiguousarray(mem.transpose(0, 2, 1))
    key = (S, nseq)
    if key not in _CACHE:
        _CACHE[key] = build(S, nseq)
    nc = _CACHE[key]
    in_maps = []
    for c in range(n_cores):
        sl = slice(c * nseq, (c + 1) * nseq)
        m = dict(xT=xT[sl], memT=memT[sl], pos=pos[sl])
        m.update(pk)
        in_maps.append(m)
    res = run_bass_kernel_spmd(nc, in_maps, core_ids=list(range(n_cores)))
    outT = np.concatenate([r["outT"] for r in res.results], axis=0)
    return np.ascontiguousarray(outT.transpose(0, 2, 1))
```
